# Optimizing a Trainium2 kernel written in Bass

```python
import math
import jax, jax.numpy as jnp
from jax import lax
import numpy as np

D_MODEL = 1024
BATCH = 8
SEQ = 2048
DEPTH = 2

ATT_HEADS = 8
ATT_QK_DIM = 64
ATT_V_DIM = 2 * ATT_QK_DIM
ATT_QK_WIDTH = ATT_HEADS * 2 * ATT_QK_DIM
ATT_WIDTH = ATT_HEADS * ATT_V_DIM
ROPE_THETA = 10000.0
Q_BLOCK = 128
CONV_WIDTH = 1024
CONV_KERNEL = 31
DN_HEADS = 8
DN_HEAD_DIM = 128
DN_WIDTH = DN_HEADS * DN_HEAD_DIM
DN_SHORT_CONV = 4
DN_CHUNK = 64
N_BRANCHES = 3
RMS_EPS = 1e-6
LN_EPS = 1e-5
IN_WIDTHS = (ATT_QK_WIDTH, ATT_QK_WIDTH, ATT_WIDTH, ATT_WIDTH,
             2 * CONV_WIDTH, CONV_WIDTH,
             DN_WIDTH, DN_WIDTH, DN_WIDTH, DN_WIDTH, DN_HEADS, DN_HEADS,
             N_BRANCHES * D_MODEL)
IN_DIM = sum(IN_WIDTHS)

kernel_name = 'hybrid_diffattn_conformer_gdn_gated_merge'


def rms_norm(x, w, eps=RMS_EPS):
    xf = x.astype(jnp.float32)
    y = xf * lax.rsqrt(jnp.mean(xf * xf, axis=-1, keepdims=True) + eps)
    return (y * w.astype(jnp.float32)).astype(x.dtype)


def layer_norm(x, w, b, eps=LN_EPS):
    xf = x.astype(jnp.float32)
    mu = jnp.mean(xf, axis=-1, keepdims=True)
    xc = xf - mu
    y = xc * lax.rsqrt(jnp.mean(xc * xc, axis=-1, keepdims=True) + eps)
    return (y * w.astype(jnp.float32) + b.astype(jnp.float32)).astype(x.dtype)


def l2_normalize(x, eps=1e-6):
    xf = x.astype(jnp.float32)
    return xf * lax.rsqrt(jnp.sum(xf * xf, axis=-1, keepdims=True) + eps)


def causal_depthwise_conv(x, w):
    K, C = w.shape
    return lax.conv_general_dilated(x, w[:, None, :].astype(x.dtype), window_strides=(1,),
                                    padding=[(K - 1, 0)],
                                    dimension_numbers=('NWC', 'WIO', 'NWC'),
                                    feature_group_count=C)


def rope_tables(positions):
    inv_freq = ROPE_THETA ** (-jnp.arange(0, ATT_QK_DIM, 2, dtype=jnp.float32) / ATT_QK_DIM)
    ang = positions.astype(jnp.float32)[..., None] * inv_freq
    return jnp.cos(ang)[:, :, None, None, :], jnp.sin(ang)[:, :, None, None, :]


def apply_rope(t, cos, sin):
    tf = t.astype(jnp.float32)
    t1, t2 = jnp.split(tf, 2, axis=-1)
    return jnp.concatenate([t1 * cos - t2 * sin, t2 * cos + t1 * sin], axis=-1).astype(t.dtype)


def diff_attention_branch(q_in, k_in, v_in, z, cos, sin, lam_qk, subln_w, w_o, lambda_init):
    B, L, _ = q_in.shape
    q = apply_rope(q_in.reshape(B, L, ATT_HEADS, 2, ATT_QK_DIM), cos, sin).transpose(0, 2, 3, 1, 4)
    k = apply_rope(k_in.reshape(B, L, ATT_HEADS, 2, ATT_QK_DIM), cos, sin).transpose(0, 2, 3, 1, 4)
    v = v_in.reshape(B, L, ATT_HEADS, ATT_V_DIM).transpose(0, 2, 1, 3)
    lq = lam_qk.astype(jnp.float32)
    lam = jnp.exp(jnp.sum(lq[0] * lq[1])) - jnp.exp(jnp.sum(lq[2] * lq[3])) + lambda_init
    scale = ATT_QK_DIM ** -0.5
    outs = []
    for blk in range(L // Q_BLOCK):
        s0 = blk * Q_BLOCK
        e = s0 + Q_BLOCK
        s = jnp.einsum('bhmqd,bhmkd->bhmqk', q[:, :, :, s0:e], k[:, :, :, :e]).astype(jnp.float32) * scale
        mask = (s0 + jnp.arange(Q_BLOCK))[:, None] >= jnp.arange(e)[None, :]
        p = jax.nn.softmax(jnp.where(mask, s, -jnp.inf), axis=-1)
        a = p[:, :, 0] - lam * p[:, :, 1]
        outs.append(jnp.einsum('bhqk,bhkd->bhqd', a.astype(v.dtype), v[:, :, :e]))
    o = jnp.concatenate(outs, axis=2)
    o = rms_norm(o, subln_w) * (1.0 - lambda_init)
    o = o.transpose(0, 2, 1, 3).reshape(B, L, ATT_WIDTH)
    return (o * jax.nn.silu(z)) @ w_o


def conformer_conv_branch(glu_in, z, dw_w, dw_b, ln_w, ln_b, w_o):
    a, g = jnp.split(glu_in, 2, axis=-1)
    u = a * jax.nn.sigmoid(g)
    u = causal_depthwise_conv(u, dw_w) + dw_b
    u = layer_norm(u, ln_w, ln_b)
    u = jax.nn.silu(u) * jax.nn.silu(z)
    return u @ w_o


def chunked_gated_delta_rule(q, k, v, g, beta):
    B, H, L, dk = q.shape
    dv = v.shape[-1]
    C = DN_CHUNK
    N = L // C
    f32 = jnp.float32
    q = q.astype(f32) * dk ** -0.5
    k = k.astype(f32)
    v = v.astype(f32)
    chunk = lambda t: t.reshape(B, H, N, C, *t.shape[3:])
    q, k, v, g, beta = chunk(q), chunk(k), chunk(v), chunk(g.astype(f32)), chunk(beta.astype(f32))
    g = jnp.cumsum(g, axis=-1)
    idx = jnp.arange(C)
    causal = idx[:, None] >= idx[None, :]
    strict = idx[:, None] > idx[None, :]
    decay = jnp.exp(jnp.where(causal, g[..., :, None] - g[..., None, :], -jnp.inf))
    kb = k * beta[..., None]
    lower = jnp.where(strict, jnp.einsum('bhnid,bhnjd->bhnij', kb, k) * decay, 0.0)
    tri = lower + jnp.eye(C, dtype=f32)
    rhs = jnp.concatenate([v * beta[..., None], kb * jnp.exp(g)[..., None]], axis=-1)
    sol = lax.linalg.triangular_solve(tri, rhs, left_side=True, lower=True, unit_diagonal=True)
    u, w = sol[..., :dv], sol[..., dv:]
    intra = jnp.einsum('bhnid,bhnjd->bhnij', q, k) * decay

    def step(state, inp):
        q_n, k_n, u_n, w_n, g_n, a_n = inp
        v_new = u_n - jnp.einsum('bhck,bhkv->bhcv', w_n, state)
        o_n = (jnp.einsum('bhck,bhkv->bhcv', q_n * jnp.exp(g_n)[..., None], state)
               + jnp.einsum('bhij,bhjv->bhiv', a_n, v_new))
        g_last = g_n[..., -1:]
        k_dec = k_n * jnp.exp(g_last - g_n)[..., None]
        state = state * jnp.exp(g_last)[..., None] + jnp.einsum('bhck,bhcv->bhkv', k_dec, v_new)
        return state, o_n

    xs = tuple(jnp.moveaxis(t, 2, 0) for t in (q, k, u, w, g, intra))
    state0 = jnp.zeros((B, H, dk, dv), f32)
    _, o = lax.scan(step, state0, xs)
    return jnp.moveaxis(o, 0, 2).reshape(B, H, L, dv)


def gated_deltanet_branch(q_in, k_in, v_in, z, b_in, a_in, conv_w, a_log, dt_bias, norm_w, w_o):
    B, L, _ = q_in.shape
    qkv = jax.nn.silu(causal_depthwise_conv(jnp.concatenate([q_in, k_in, v_in], axis=-1), conv_w))
    q, k, v = jnp.split(qkv, 3, axis=-1)
    heads = lambda t: t.reshape(B, L, DN_HEADS, DN_HEAD_DIM).transpose(0, 2, 1, 3)
    q = l2_normalize(heads(q))
    k = l2_normalize(heads(k))
    v = heads(v)
    beta = jax.nn.sigmoid(b_in.astype(jnp.float32)).transpose(0, 2, 1)
    g = (-jnp.exp(a_log.astype(jnp.float32))
         * jax.nn.softplus(a_in.astype(jnp.float32) + dt_bias.astype(jnp.float32))).transpose(0, 2, 1)
    o = chunked_gated_delta_rule(q, k, v, g, beta).transpose(0, 2, 1, 3)
    zh = z.reshape(B, L, DN_HEADS, DN_HEAD_DIM).astype(jnp.float32)
    o = rms_norm(o, norm_w) * jax.nn.silu(zh)
    return o.reshape(B, L, DN_WIDTH).astype(z.dtype) @ w_o


def hybrid_layer(x, cos, sin, layer_idx, norm_w, w_in, lam_qk, attn_subln_w, w_attn_out,
                 conv_dw_w, conv_dw_b, conv_ln_w, conv_ln_b, w_conv_out,
                 dn_conv_w, dn_a_log, dn_dt_bias, dn_norm_w, w_dn_out, w_out):
    h = rms_norm(x, norm_w)
    proj = h @ w_in
    split_points = np.cumsum(np.array(IN_WIDTHS))[:-1].tolist()
    (aq, ak, av, az, c_glu, cz, dq, dk, dvv, dz, db, da, gate_logits) = jnp.split(proj, split_points, axis=-1)
    lambda_init = 0.8 - 0.6 * math.exp(-0.3 * layer_idx)
    y_a = diff_attention_branch(aq, ak, av, az, cos, sin, lam_qk, attn_subln_w, w_attn_out, lambda_init)
    y_c = conformer_conv_branch(c_glu, cz, conv_dw_w, conv_dw_b, conv_ln_w, conv_ln_b, w_conv_out)
    y_d = gated_deltanet_branch(dq, dk, dvv, dz, db, da, dn_conv_w, dn_a_log, dn_dt_bias, dn_norm_w, w_dn_out)
    g_a, g_c, g_d = jnp.split(jax.nn.sigmoid(gate_logits), N_BRANCHES, axis=-1)
    merged = g_a * y_a + g_c * y_c + g_d * y_d
    return x + merged @ w_out


def setup_inputs(seed: int = 0) -> dict:
    key = jax.random.key(seed)
    ks = jax.random.split(key, 20)
    f32 = jnp.float32
    nrm = lambda k, shape, s: jax.random.normal(k, shape, f32) * s
    x = nrm(ks[0], (BATCH, SEQ, D_MODEL), 1.0)
    positions = jnp.tile(jnp.arange(SEQ, dtype=jnp.int32)[None, :], (BATCH, 1))
    norm_w = 1.0 + nrm(ks[1], (DEPTH, D_MODEL), 0.02)
    w_in = nrm(ks[2], (DEPTH, D_MODEL, IN_DIM), D_MODEL ** -0.5)
    lam_qk = nrm(ks[3], (DEPTH, 4, ATT_QK_DIM), 0.1)
    attn_subln_w = 1.0 + nrm(ks[4], (DEPTH, ATT_V_DIM), 0.02)
    w_attn_out = nrm(ks[5], (DEPTH, ATT_WIDTH, D_MODEL), ATT_WIDTH ** -0.5)
    conv_dw_w = nrm(ks[6], (DEPTH, CONV_KERNEL, CONV_WIDTH), CONV_KERNEL ** -0.5)
    conv_dw_b = nrm(ks[7], (DEPTH, CONV_WIDTH), 0.02)
    conv_ln_w = 1.0 + nrm(ks[8], (DEPTH, CONV_WIDTH), 0.02)
    conv_ln_b = nrm(ks[9], (DEPTH, CONV_WIDTH), 0.02)
    w_conv_out = nrm(ks[10], (DEPTH, CONV_WIDTH, D_MODEL), CONV_WIDTH ** -0.5)
    dn_conv_w = nrm(ks[11], (DEPTH, DN_SHORT_CONV, 3 * DN_WIDTH), DN_SHORT_CONV ** -0.5)
    dn_a_log = jnp.log(jax.random.uniform(ks[12], (DEPTH, DN_HEADS), f32, 1.0, 16.0))
    dt = jnp.exp(jax.random.uniform(ks[13], (DEPTH, DN_HEADS), f32, math.log(1e-3), math.log(1e-1)))
    dn_dt_bias = dt + jnp.log(-jnp.expm1(-dt))
    dn_norm_w = 1.0 + nrm(ks[14], (DEPTH, DN_HEAD_DIM), 0.02)
    w_dn_out = nrm(ks[15], (DEPTH, DN_WIDTH, D_MODEL), DN_WIDTH ** -0.5)
    w_out = nrm(ks[16], (DEPTH, D_MODEL, D_MODEL), D_MODEL ** -0.5)
    final_norm_w = 1.0 + nrm(ks[17], (D_MODEL,), 0.02)
    return {'x': x, 'positions': positions, 'norm_w': norm_w, 'w_in': w_in, 'lam_qk': lam_qk,
            'attn_subln_w': attn_subln_w, 'w_attn_out': w_attn_out, 'conv_dw_w': conv_dw_w,
            'conv_dw_b': conv_dw_b, 'conv_ln_w': conv_ln_w, 'conv_ln_b': conv_ln_b,
            'w_conv_out': w_conv_out, 'dn_conv_w': dn_conv_w, 'dn_a_log': dn_a_log,
            'dn_dt_bias': dn_dt_bias, 'dn_norm_w': dn_norm_w, 'w_dn_out': w_dn_out,
            'w_out': w_out, 'final_norm_w': final_norm_w}


def reference(x, positions, norm_w, w_in, lam_qk, attn_subln_w, w_attn_out, conv_dw_w, conv_dw_b,
              conv_ln_w, conv_ln_b, w_conv_out, dn_conv_w, dn_a_log, dn_dt_bias, dn_norm_w,
              w_dn_out, w_out, final_norm_w):
    cos, sin = rope_tables(positions)
    for l in range(DEPTH):
        x = hybrid_layer(x, cos, sin, l, norm_w[l], w_in[l], lam_qk[l], attn_subln_w[l], w_attn_out[l],
                         conv_dw_w[l], conv_dw_b[l], conv_ln_w[l], conv_ln_b[l], w_conv_out[l],
                         dn_conv_w[l], dn_a_log[l], dn_dt_bias[l], dn_norm_w[l], w_dn_out[l], w_out[l])
    return rms_norm(x, final_norm_w)
```

```python
import contextlib
import math
import os
import numpy as np
import concourse.bass as bass
import concourse.mybir as mybir
from concourse.bass_utils import run_bass_kernel_spmd

F32 = mybir.dt.float32
BF16 = mybir.dt.bfloat16
I32 = mybir.dt.int32
AF = mybir.ActivationFunctionType
ALU = mybir.AluOpType
AX = mybir.AxisListType

L = 2048
D = 1024
NT = 16
DEPTH = 2
INW = 14352
OFF = dict(aq=0, ak=1024, av=2048, az=3072, ca=4096, cg=5120, cz=6144, dq=7168, dk=8192, dv=9216,
           dz=10240, db=11264, da=11272, ga=11280, gc=12304, gd=13328)
NCST = 1160
ARENA_F32 = 53200


class Res:
    __slots__ = ("w", "r", "excl")

    def __init__(self, excl=False):
        self.w = None
        self.r = {}
        self.excl = excl


class Prog:
    ENGS = ("pe", "act", "dve", "pool", "sp")
    ENGOBJ = {"pe": "tensor", "act": "scalar", "dve": "vector", "pool": "gpsimd", "sp": "sync"}

    def __init__(self, nc):
        self.nc = nc
        self.q = {e: [] for e in self.ENGS}
        self.cnt = {e: 0 for e in self.ENGS}
        self.seen = {e: {} for e in self.ENGS}
        self.slots = []

    def slot(self, name):
        self.cnt[name] = 0
        self.slots.append(name)
        return name

    def _deps(self, eng, reads, writes):
        waits = {}

        def need(dep):
            if dep is None:
                return
            c, s = dep
            if c == eng and eng == "pe":
                return
            if c in self.slots:
                s = self.cnt[c]
            if self.seen[eng].get(c, 0) < s:
                waits[c] = max(waits.get(c, 0), s)
        for r in reads:
            need(r.w)
        for w in writes:
            need(w.w)
            for c, s in w.r.items():
                need((c, s))
        for c, s in waits.items():
            self.seen[eng][c] = s
        return waits

    def _mark(self, counter, seq, reads, writes):
        for r in reads:
            r.r[counter] = max(r.r.get(counter, 0), seq)
        for w in writes:
            w.w = (counter, seq)
            w.r = {}

    def op(self, eng, fn, reads=(), writes=(), sig=True):
        ex = [r for r in reads if r.excl]
        if ex:
            reads = [r for r in reads if not r.excl]
            writes = list(writes) + ex
        waits = self._deps(eng, reads, writes)
        seq = self.cnt[eng] + 1
        if sig:
            self.cnt[eng] = seq
        self.q[eng].append((waits, fn, eng if sig else None, 1))
        self._mark(eng, seq, reads, writes)

    def dma(self, queue, slot, fn, reads=(), writes=()):
        waits = self._deps(queue, reads, writes)
        self.cnt[slot] += 1
        self.q[queue].append((waits, fn, slot, 16))
        self._mark(slot, self.cnt[slot], reads, writes)

    def wait_for(self, eng, resources):
        waits = self._deps(eng, resources, ())
        self.q[eng].append((waits, None, None, 0))

    def barrier(self):
        for e in self.ENGS:
            waits = {}
            for c, v in self.cnt.items():
                if c == e:
                    continue
                if self.seen[e].get(c, 0) < v:
                    waits[c] = v
                    self.seen[e][c] = v
            self.q[e].append((waits, None, None, 0))

    def emit(self, st):
        nc = self.nc
        sems = {c: st.enter_context(nc.semaphore("s_" + c)) for c in self.cnt}
        mult = {c: (16 if c in self.slots else 1) for c in self.cnt}
        with nc.Block() as block:
            def mk(ename):
                def body(eng):
                    for (waits, fn, inc, amt) in self.q[ename]:
                        for c, s in waits.items():
                            eng.wait_ge(sems[c], s * mult[c])
                        if fn is not None:
                            ins = fn(eng)
                            if inc is not None:
                                ins.then_inc(sems[inc], amt)
                return body
            for ename in self.ENGS:
                getattr(block, self.ENGOBJ[ename])(mk(ename))


class Arena:
    def __init__(self, nc, st, n, t=None, lo=0):
        self.t = t if t is not None else st.enter_context(nc.sbuf_tensor("arena", [128, n], F32))
        self.p = lo
        self.n = lo + n
        self.hi = 0

    def f32(self, n):
        ap = self.t[:, self.p:self.p + n]
        self.p += n
        self.hi = max(self.hi, self.p)
        assert self.p <= self.n, ("arena overflow", self.p, self.n)
        return ap

    def bf(self, n):
        nf = (n + 1) // 2
        return self.f32(nf).bitcast(BF16)

    def i32(self, n):
        return self.f32(n).bitcast(I32)


def v3(ap, a):
    return ap.rearrange("p (a b) -> p a b", a=a)


def build(dbg=None, depth=DEPTH, branches="DAC"):
    nc = bass.Bass("TRN2", target_bir_lowering=False)
    dram = {}

    def din(name, shape, dt=F32):
        dram[name] = nc.dram_tensor(name, list(shape), dt, kind="ExternalInput").ap()
        return dram[name]
    x_in = din("x", [L, D])
    pos_in = din("positions", [L], I32)
    norm_w = din("norm_w", [DEPTH, D])
    w_in = din("w_in", [DEPTH, D, INW])
    lam_qk = din("lam_qk", [DEPTH, 256])
    attn_subln_w = din("attn_subln_w", [DEPTH, 128])
    w_attn_out = din("w_attn_out", [DEPTH, D, D])
    conv_dw_w = din("conv_dw_w", [DEPTH, 31, D])
    conv_vecs = din("conv_vecs", [DEPTH, 24, 128])
    w_conv_out = din("w_conv_out", [DEPTH, D, D])
    dn_conv_w = din("dn_conv_w", [DEPTH, 4, 3072])
    dn_a_log = din("dn_a_log", [DEPTH, 8])
    dn_dt_bias = din("dn_dt_bias", [DEPTH, 8])
    dn_norm_w = din("dn_norm_w", [DEPTH, 128])
    w_dn_out = din("w_dn_out", [DEPTH, D, D])
    w_out = din("w_out", [DEPTH, D, D])
    final_norm_w = din("final_norm_w", [D])
    cst_in = din("cst", [128, NCST])
    y_out = nc.dram_tensor("y", [L, D], F32, kind="ExternalOutput").ap()
    xs = nc.dram_tensor("xs", [L, D], F32, kind="Internal").ap()
    dbg_out = None
    if dbg:
        dbg_out = nc.dram_tensor("dbg", [128, 8, L], F32, kind="ExternalOutput").ap()

    P = Prog(nc)
    st = contextlib.ExitStack()
    with st:
        A = Arena(nc, st, ARENA_F32)
        pst = [st.enter_context(nc.psum_tensor(f"ps{i}", [128, 1024], F32)) for i in range(4)]
        banks = []
        for i in range(4):
            for hh in range(2):
                banks.append((pst[i][:, hh * 512:(hh + 1) * 512], Res(excl=True)))
        rot = {"list": list(range(8)), "i": 0}

        def set_rot(lst):
            rot["list"] = lst
            rot["i"] = 0

        def bank():
            b = banks[rot["list"][rot["i"] % len(rot["list"])]]
            rot["i"] += 1
            return b

        s_ld = P.slot("d_ld")
        s_st = P.slot("d_st")
        s_w = [P.slot("d_w0"), P.slot("d_w1")]
        s_misc = P.slot("d_misc")
        wslot = {"i": 0}

        def op(eng, name, *args, r=(), w=(), **kw):
            P.op(eng, lambda e: getattr(e, name)(*args, **kw), r, w)

        def mm(out, lhsT, rhs, start, stop, r, w):
            P.op("pe", lambda e: e.matmul(out, lhsT=lhsT, rhs=rhs, start=start, stop=stop), r, w, sig=True)

        def tr(out, in_, ident, r, w):
            P.op("pe", lambda e: e.transpose(out, in_, ident), r, w)

        def ld(out, in_, w, r=(), slot=None):
            P.dma("sp", slot or s_ld, lambda e: e.dma_start(out=out, in_=in_), r, w)

        def ldw(out, in_, w, r=()):
            s = s_w[0]
            P.dma("pool", s, lambda e: e.dma_start(out=out, in_=in_), r, w)

        def sigmoid_act(dst, src_, r_src, r_dst):
            op("act", "activation", dst, src_, AF.Exp, scale=-1.0, r=list(r_src), w=[r_dst])
            op("act", "activation", dst, dst, AF.Ln, bias=cst[:, 256:257], scale=1.0, r=[r_dst, r_cst], w=[r_dst])
            op("act", "activation", dst, dst, AF.Exp, scale=-1.0, r=[r_dst], w=[r_dst])

        def rsqrt_act(dst, src_, scale, eps_ap, r_src, r_dst):
            op("act", "activation", dst, src_, AF.Ln, bias=eps_ap, scale=scale, r=list(r_src) + [r_cst], w=[r_dst])
            op("act", "activation", dst, dst, AF.Exp, scale=-0.5, r=[r_dst], w=[r_dst])

        def bc(ap, axis, shape):
            return ap.unsqueeze(axis).broadcast_to(list(shape))

        def w_in_cols(l, c0, n):
            return w_in[l].rearrange("(kc p) c -> p kc c", p=128)[:, :, c0:c0 + n]

        def wsq_cols(wt, l, c0, n):
            return wt[l].rearrange("(kc p) c -> p kc c", p=128)[:, :, c0:c0 + n]

        cst = A.f32(NCST)
        r_cst = Res()
        ld(cst, cst_in[:, :], [r_cst], slot=s_misc)
        cb = A.bf(1152)
        r_cb = Res()
        op("dve", "tensor_copy", cb, cst[:, 0:1152], r=[r_cst], w=[r_cb])
        ident_b, pswap_b, ones_b = cb[:, 0:128], cb[:, 128:256], cb[:, 256:384]
        MU_b, MUI_b, CAUS_b = cb[:, 384:512], cb[:, 512:640], cb[:, 768:896]
        ident_f, ones_f = cst[:, 0:128], cst[:, 256:384]
        MU_f, MUI_f, SL_f = cst[:, 384:512], cst[:, 512:640], cst[:, 640:768]
        CI0_f, CI1_f = cst[:, 896:1024], cst[:, 1024:1152]
        invf, sgn, eps6, eps5 = cst[:, 1152:1153], cst[:, 1153:1154], cst[:, 1154:1155], cst[:, 1155:1156]
        one_col = cst[:, 256:257]
        hT = A.bf(8 * L)
        hT3 = v3(hT, 8)
        r_hT = Res()
        mg_off = A.p
        mg = A.bf(8 * L)
        mg3 = v3(mg, 8)
        r_mg = Res()
        stat = A.f32(64)
        r_stat = Res()
        r_stats = [Res(), Res()]
        pmark = A.p

        def rope_tables():
            cosT = A.f32(L)
            sinT = A.f32(L)
            r_cos, r_sin = Res(), Res()
            tmark = A.p
            pi_ = A.i32(L)
            pf = A.f32(L)
            kf = A.f32(L)
            yv = A.f32(L)
            m1 = A.f32(L)
            ki = m1.bitcast(I32)
            r = Res()
            ld(pi_, pos_in.partition_broadcast(128), [r], slot=s_misc)
            op("dve", "tensor_copy", pf, pi_, r=[r], w=[r])
            op("dve", "tensor_scalar", pf, pf, invf, None, op0=ALU.mult, r=[r, r_cst], w=[r])
            op("dve", "tensor_scalar", kf, pf, 1.0 / (2 * math.pi), None, op0=ALU.mult, r=[r], w=[r])
            op("dve", "tensor_copy", ki, kf, r=[r], w=[r])
            op("dve", "tensor_copy", kf, ki, r=[r], w=[r])
            c1 = 6.28125
            c2 = 2 * math.pi - c1
            op("dve", "scalar_tensor_tensor", yv, kf, -c1, pf, op0=ALU.mult, op1=ALU.add, r=[r], w=[r])
            op("dve", "scalar_tensor_tensor", yv, kf, -c2, yv, op0=ALU.mult, op1=ALU.add, r=[r], w=[r])
            op("dve", "tensor_scalar", m1, yv, -math.pi, 2 * math.pi, op0=ALU.is_lt, op1=ALU.mult, r=[r], w=[r])
            op("dve", "tensor_tensor", yv, yv, m1, op=ALU.add, r=[r], w=[r])
            op("dve", "tensor_scalar", m1, yv, math.pi, 2 * math.pi, op0=ALU.is_gt, op1=ALU.mult, r=[r], w=[r])
            op("dve", "tensor_tensor", yv, yv, m1, op=ALU.subtract, r=[r], w=[r])
            op("act", "activation", sinT, yv, AF.Sin, r=[r], w=[r_sin])
            op("dve", "tensor_scalar", sinT, sinT, sgn, None, op0=ALU.mult, r=[r_sin, r_cst], w=[r_sin])
            op("dve", "tensor_scalar", yv, yv, math.pi / 2, None, op0=ALU.add, r=[r], w=[r])
            op("dve", "tensor_scalar", m1, yv, math.pi, 2 * math.pi, op0=ALU.is_gt, op1=ALU.mult, r=[r], w=[r])
            op("dve", "tensor_tensor", yv, yv, m1, op=ALU.subtract, r=[r], w=[r])
            op("act", "activation", cosT, yv, AF.Sin, r=[r], w=[r_cos])
            P.barrier()
            A.p = tmark
            return cosT, sinT, r_cos, r_sin

        def norm_to_hT(xt, r_xt, wbc, r_wbc, t, hn, r_hn, junk, r_junk, sc):
            ssq = stat[:, sc:sc + 1]
            rs = stat[:, sc + 1:sc + 2]
            r_st = r_stats[(sc // 2) % 2]
            op("dve", "memset", ssq, 0.0, w=[r_st])
            op("act", "activation", junk, xt, AF.Square, accum_out=ssq, r=[r_xt, r_st], w=[r_junk, r_st])
            rsqrt_act(rs, ssq, 1.0 / D, eps6, [r_st], r_st)
            op("dve", "scalar_tensor_tensor", hn, xt, rs, wbc, op0=ALU.mult, op1=ALU.mult,
               r=[r_xt, r_st, r_wbc], w=[r_hn])

        def hn_to_hT(hn, r_hn, t):
            pb, r_pb = bank()
            pbb = pb.bitcast(BF16)
            for kc in range(8):
                tr(pbb[:, kc * 128:(kc + 1) * 128], hn[:, kc * 128:(kc + 1) * 128], ident_b, [r_hn, r_cb], [r_pb])
            op("act", "copy", hT3[:, :, t * 128:(t + 1) * 128], v3(pbb, 8), r=[r_pb], w=[r_hT])

        def phase0(l):
            wbc = A.f32(D)
            r_wbc = Res()
            ld(wbc, norm_w[l].partition_broadcast(128), [r_wbc], slot=s_misc)
            xts = [(A.f32(D), Res()) for _ in range(2)]
            hns = [(A.bf(D), Res()) for _ in range(2)]
            junks = [(A.bf(D), Res()) for _ in range(2)]
            for t in range(NT):
                xt, r_xt = xts[t % 2]
                hn, r_hn = hns[t % 2]
                ld(xt, x_in[t * 128:(t + 1) * 128, :], [r_xt])
                norm_to_hT(xt, r_xt, wbc, r_wbc, t, hn, r_hn, junks[t % 2][0], junks[t % 2][1], 2 * (t % 2))
                hn_to_hT(hn, r_hn, t)
            P.barrier()
            A.p = pmark

        def finish(act3, r_act, w_bo, l, goff, first):
            set_rot(list(range(8)))
            wbs = [(A.bf(8 * 128), Res()) for _ in range(2)]
            wgs = [(A.bf(8 * 128), Res()) for _ in range(2)]
            gss = [(A.f32(512), Res()) for _ in range(2)]
            tms = [(A.f32(512), Res()) for _ in range(2)]
            it = 0
            for oc in range(8):
                wb, r_wb = wbs[oc % 2]
                wg, r_wg = wgs[oc % 2]
                wb3, wg3 = v3(wb, 8), v3(wg, 8)
                ldw(wb3, wsq_cols(w_bo, l, oc * 128, 128), [r_wb])
                ldw(wg3, w_in_cols(l, goff + oc * 128, 128), [r_wg])
                for tc in range(4):
                    cs = slice(tc * 512, (tc + 1) * 512)
                    py, r_py = bank()
                    for c in range(8):
                        mm(py, wb3[:, c, :], act3[:, c, cs], c == 0, c == 7, [r_wb, r_act], [r_py])
                    pg, r_pg = bank()
                    for c in range(8):
                        mm(pg, wg3[:, c, :], hT3[:, c, cs], c == 0, c == 7, [r_wg, r_hT], [r_pg])
                    gs, r_gs = gss[it % 2]
                    tm, r_tm = tms[it % 2]
                    it += 1
                    sigmoid_act(gs, pg, [r_pg], r_gs)
                    if first:
                        op("dve", "tensor_tensor", mg3[:, oc, cs], py, gs, op=ALU.mult, r=[r_py, r_gs], w=[r_mg])
                    else:
                        op("dve", "tensor_tensor", tm, py, gs, op=ALU.mult, r=[r_py, r_gs], w=[r_tm])
                        op("pool", "tensor_tensor", mg3[:, oc, cs], mg3[:, oc, cs], tm, op=ALU.add,
                           r=[r_tm, r_mg], w=[r_mg])
            P.barrier()
            A.p = pmark

        def branch_A(l, first):
            li = 0.8 - 0.6 * math.exp(-0.3 * l)
            ao = A.bf(8 * L)
            ao3 = v3(ao, 8)
            r_ao = Res()
            amark = A.p
            cosT, sinT, r_cos, r_sin = rope_tables()
            lq = A.f32(256)
            r_lq = Res()
            ld(lq, lam_qk[l].partition_broadcast(128), [r_lq], slot=s_misc)
            sm = A.f32(8)
            r_sm = Res()
            pr = A.f32(64)
            op("dve", "tensor_tensor", pr, lq[:, 0:64], lq[:, 64:128], op=ALU.mult, r=[r_lq], w=[r_sm])
            op("dve", "reduce_sum", sm[:, 0:1], pr, axis=AX.X, r=[r_sm], w=[r_sm])
            op("dve", "tensor_tensor", pr, lq[:, 128:192], lq[:, 192:256], op=ALU.mult, r=[r_lq, r_sm], w=[r_sm])
            op("dve", "reduce_sum", sm[:, 1:2], pr, axis=AX.X, r=[r_sm], w=[r_sm])
            op("act", "activation", sm[:, 2:4], sm[:, 0:2], AF.Exp, r=[r_sm], w=[r_sm])
            op("dve", "tensor_tensor", sm[:, 4:5], sm[:, 3:4], sm[:, 2:3], op=ALU.subtract, r=[r_sm], w=[r_sm])
            op("dve", "tensor_scalar", sm[:, 4:5], sm[:, 4:5], -li, None, op0=ALU.add, r=[r_sm], w=[r_sm])
            nlam = sm[:, 4:5]
            wbc_ = A.f32(128)
            r_wbc_ = Res()
            ld(wbc_, attn_subln_w[l].partition_broadcast(128), [r_wbc_], slot=s_misc)
            op("dve", "tensor_tensor", wbc_, wbc_, ident_f, op=ALU.mult, r=[r_wbc_, r_cst], w=[r_wbc_])
            op("dve", "reduce_sum", sm[:, 5:6], wbc_, axis=AX.X, r=[r_wbc_, r_sm], w=[r_sm])
            op("dve", "tensor_scalar", sm[:, 6:7], sm[:, 5:6], 1.0 - li, None, op0=ALU.mult, r=[r_sm], w=[r_sm])
            wcol = sm[:, 6:7]
            STGA = int(os.environ.get("STGA", "9"))
            if STGA < 2:
                P.barrier(); A.p = pmark; return
            W4s = [(A.bf(8 * 512), Res()) for _ in range(2)]
            qTs = [(A.bf(L), Res()) for _ in range(2)]
            kTs = [(A.bf(L), Res()) for _ in range(2)]
            vhs = [(A.bf(L), Res()) for _ in range(2)]
            szs = [(A.bf(L), Res()) for _ in range(2)]
            raws = [(A.bf(512), Res()) for _ in range(2)]
            t1s = [(A.f32(512), Res()) for _ in range(2)]
            t2s = [(A.f32(512), Res()) for _ in range(2)]
            ebs = [(A.bf(512), Res()) for _ in range(4)]
            ob = [(A.f32(512), Res()) for _ in range(4)]
            sqb = (A.bf(512), Res())
            O_b = [banks[0], banks[1]]
            S_b = [banks[2], banks[3]]
            sc_b = [[banks[4], banks[5]], [banks[6], banks[7]]]
            set_rot([4, 5, 6, 7])
            cnt = {"raw": 0, "e": 0}
            for h in range(8):
                W4, r_W4 = W4s[h % 2]
                W44 = W4.rearrange("p (k s c) -> p k s c", k=8, s=4)
                for s, nm in enumerate(("aq", "ak", "av", "az")):
                    ldw(W44[:, :, s, :], w_in_cols(l, OFF[nm] + h * 128, 128), [r_W4])
                qT, r_qT = qTs[h % 2]
                kT, r_kT = kTs[h % 2]
                vh, r_vh = vhs[h % 2]
                vh3 = v3(vh, 16)
                sz, r_sz = szs[h % 2]
                for s, (dst, r_dst) in enumerate(((qT, r_qT), (kT, r_kT))):
                    for tc in range(4):
                        cs = slice(tc * 512, (tc + 1) * 512)
                        pq, r_pq = bank()
                        for kc in range(8):
                            mm(pq, W44[:, kc, s, :], hT3[:, kc, cs], kc == 0, kc == 7, [r_W4, r_hT], [r_pq])
                        raw, r_raw = raws[cnt["raw"] % 2]
                        t1, r_t1 = t1s[cnt["raw"] % 2]
                        t2, r_t2 = t2s[cnt["raw"] % 2]
                        cnt["raw"] += 1
                        op("act", "copy", raw, pq, r=[r_pq], w=[r_raw])
                        op("dve", "tensor_tensor", t1, pq, cosT[:, cs], op=ALU.mult, r=[r_pq, r_cos], w=[r_t1])
                        p2, r_p2 = bank()
                        mm(p2, pswap_b, raw, True, True, [r_cb, r_raw], [r_p2])
                        op("dve", "tensor_tensor", t2, p2, sinT[:, cs], op=ALU.mult, r=[r_p2, r_sin], w=[r_t2])
                        op("pool", "tensor_tensor", dst[:, cs], t1, t2, op=ALU.add, r=[r_t1, r_t2], w=[r_dst])
                for g4 in range(4 if STGA >= 3 else 0):
                    pv, r_pv = bank()
                    pv3 = v3(pv, 4)
                    for tt in range(4):
                        t = g4 * 4 + tt
                        for kc in range(8):
                            mm(pv3[:, tt, :], hT3[:, kc, t * 128:(t + 1) * 128], W44[:, kc, 2, :], kc == 0, kc == 7,
                               [r_W4, r_hT], [r_pv])
                    op("act", "copy", vh3[:, g4 * 4:(g4 + 1) * 4, :], pv3, r=[r_pv], w=[r_vh])
                for tc in range(4 if STGA >= 3 else 0):
                    cs = slice(tc * 512, (tc + 1) * 512)
                    pz, r_pz = bank()
                    for kc in range(8):
                        mm(pz, W44[:, kc, 3, :], hT3[:, kc, cs], kc == 0, kc == 7, [r_W4, r_hT], [r_pz])
                    t1, r_t1 = t1s[tc % 2]
                    sigmoid_act(t1, pz, [r_pz], r_t1)
                    op("dve", "tensor_tensor", sz[:, cs], pz, t1, op=ALU.mult, r=[r_pz, r_t1], w=[r_sz])
                for qc in range(4 if STGA >= 4 else 0):
                    nk = 4 * qc + 4
                    qs0 = qc * 512
                    def c0_of(kt):
                        return max(kt - 4 * qc, 0) * 128

                    def emit_scores(kt):
                        c0 = c0_of(kt)
                        for m in range(2):
                            ps_, r_ps = sc_b[kt % 2][m]
                            rows = slice(m * 64, (m + 1) * 64)
                            mm(ps_[:, c0:512], kT[rows, kt * 128:(kt + 1) * 128], qT[rows, qs0 + c0:qs0 + 512],
                               True, True, [r_kT, r_qT], [r_ps])

                    def emit_exp(kt):
                        c0 = c0_of(kt)
                        for m in range(2):
                            ps_, r_ps = sc_b[kt % 2][m]
                            eb, r_eb = ebs[(kt % 2) * 2 + m]
                            op("act", "activation", eb[:, c0:512], ps_[:, c0:512], AF.Exp, scale=0.125,
                               r=[r_ps], w=[r_eb])
                            if kt - 4 * qc >= 0:
                                op("pool", "tensor_tensor", eb[:, c0:c0 + 128], eb[:, c0:c0 + 128], CAUS_b,
                                   op=ALU.mult, r=[r_eb, r_cb], w=[r_eb])

                    def emit_pv(kt):
                        c0 = c0_of(kt)
                        for m in range(2):
                            eb, r_eb = ebs[(kt % 2) * 2 + m]
                            mm(O_b[m][0][:, c0:512], vh3[:, kt, :], eb[:, c0:512], kt == 0, kt == nk - 1,
                               [r_vh, r_eb], [O_b[m][1]])
                            mm(S_b[m][0][:, c0:512], ones_b, eb[:, c0:512], kt == 0, kt == nk - 1,
                               [r_cb, r_eb], [S_b[m][1]])
                    emit_scores(0)
                    for kt in range(nk):
                        if kt + 1 < nk:
                            emit_scores(kt + 1)
                        emit_exp(kt)
                        emit_pv(kt)
                    if STGA < 5:
                        continue
                    cs = slice(qs0, qs0 + 512)
                    (o0, r_o0), (o1, r_o1), (rc, r_rc), (rc1, r_rc1) = ob
                    op("dve", "tensor_copy", o0, O_b[0][0], r=[O_b[0][1]], w=[r_o0])
                    op("act", "activation", rc, S_b[0][0], AF.Ln, r=[S_b[0][1]], w=[r_rc])
                    op("dve", "tensor_copy", o1, O_b[1][0], r=[O_b[1][1]], w=[r_o1])
                    op("act", "activation", rc1, S_b[1][0], AF.Ln, r=[S_b[1][1]], w=[r_rc1])
                    op("act", "activation", rc, rc, AF.Exp, scale=-1.0, r=[r_rc], w=[r_rc])
                    op("act", "activation", rc1, rc1, AF.Exp, scale=-1.0, r=[r_rc1], w=[r_rc1])
                    op("dve", "tensor_tensor", o0, o0, rc, op=ALU.mult, r=[r_o0, r_rc], w=[r_o0])
                    op("dve", "tensor_tensor", o1, o1, rc1, op=ALU.mult, r=[r_o1, r_rc1], w=[r_o1])
                    op("dve", "scalar_tensor_tensor", o0, o1, nlam, o0, op0=ALU.mult, op1=ALU.add,
                       r=[r_o0, r_o1, r_sm], w=[r_o0])
                    op("act", "activation", sqb[0], o0, AF.Square, r=[r_o0], w=[sqb[1]])
                    pss, r_pss = bank()
                    mm(pss, ones_b, sqb[0], True, True, [r_cb, sqb[1]], [r_pss])
                    rsqrt_act(rc, pss, 1.0 / 128, eps6, [r_pss], r_rc)
                    op("dve", "tensor_tensor", o0, o0, rc, op=ALU.mult, r=[r_o0, r_rc], w=[r_o0])
                    op("dve", "scalar_tensor_tensor", ao3[:, h, cs], o0, wcol, sz[:, cs], op0=ALU.mult, op1=ALU.mult,
                       r=[r_o0, r_sm, r_sz], w=[r_ao])
            P.barrier()
            A.p = amark
            if STGA < 6:
                A.p = pmark; return
            finish(ao3, r_ao, w_attn_out, l, OFF["ga"], first)
            A.p = pmark

        def branch_C(l, first):
            co = A.bf(8 * L)
            co3 = v3(co, 8)
            r_co = Res()
            mean = A.f32(L)
            rstd = A.f32(L)
            r_mr = Res()
            cmark = A.p
            set_rot(list(range(8)))
            cwr = A.f32(D)
            r_cwr = Res()
            ld(cwr[0:31, :], conv_dw_w[l], [r_cwr], slot=s_misc)
            cvr = A.f32(128)
            r_cvr = Res()
            ld(cvr[0:24, :], conv_vecs[l], [r_cvr], slot=s_misc)
            cwT = A.f32(8 * 32)
            cwT3 = v3(cwT, 8)
            r_cwT = Res()
            cv = A.f32(24)
            r_cv = Res()
            pb, r_pb = bank()
            for c in range(8):
                tr(pb[:, c * 32:c * 32 + 31], cwr[0:31, c * 128:(c + 1) * 128], ident_f[0:31, 0:31], [r_cwr, r_cst], [r_pb])
            op("dve", "tensor_copy", v3(cwT, 8)[:, :, 0:31], v3(pb[:, 0:256], 8)[:, :, 0:31], r=[r_pb], w=[r_cwT])
            pb2, r_pb2 = bank()
            tr(pb2[:, 0:24], cvr[0:24, :], ident_f[0:24, 0:24], [r_cvr, r_cst], [r_pb2])
            op("dve", "tensor_copy", cv, pb2[:, 0:24], r=[r_pb2], w=[r_cv])
            STG = int(os.environ.get("STG", "9"))
            if STG < 2:
                P.barrier(); A.p = pmark; return
            Wags = [(A.bf(8 * 256), Res()) for _ in range(2)]
            dgs = [(A.bf(31 * 128), Res()) for _ in range(2)]
            ups = [(A.bf(30 + L + 2), Res()) for _ in range(2)]
            sgs = [(A.f32(512), Res()) for _ in range(2)]
            for (u, r_u) in ups:
                op("pool", "memset", u[:, 0:30], 0.0, w=[r_u])
            itc = {"i": 0}
            cstate = {}

            def c1_setup(c):
                Wag, r_Wag = Wags[c % 2]
                Wag4 = Wag.rearrange("p (k s c) -> p k s c", k=8, s=2)
                ldw(Wag4[:, :, 0, :], w_in_cols(l, OFF["ca"] + c * 128, 128), [r_Wag])
                ldw(Wag4[:, :, 1, :], w_in_cols(l, OFF["cg"] + c * 128, 128), [r_Wag])
                dg, r_dg = dgs[c % 2]
                dg3 = v3(dg, 31)
                op("dve", "tensor_tensor", dg3, bc(ident_b, 1, [128, 31, 128]), bc(cwT3[:, c, 0:31], 2, [128, 31, 128]),
                   op=ALU.mult, r=[r_cb, r_cwT], w=[r_dg])
                cstate[c] = (Wag4, r_Wag, dg3, r_dg, ups[c % 2])

            def c1_proj(c, tc):
                Wag4, r_Wag, dg3, r_dg, (up, r_up) = cstate[c]
                cs = slice(tc * 512, (tc + 1) * 512)
                pa, r_pa = bank()
                for kc in range(8):
                    mm(pa, Wag4[:, kc, 0, :], hT3[:, kc, cs], kc == 0, kc == 7, [r_Wag, r_hT], [r_pa])
                pg, r_pg = bank()
                for kc in range(8):
                    mm(pg, Wag4[:, kc, 1, :], hT3[:, kc, cs], kc == 0, kc == 7, [r_Wag, r_hT], [r_pg])
                sg, r_sg = sgs[itc["i"] % 2]
                itc["i"] += 1
                sigmoid_act(sg, pg, [r_pg], r_sg)
                op("dve", "tensor_tensor", up[:, 30 + tc * 512:30 + (tc + 1) * 512], pa, sg, op=ALU.mult,
                   r=[r_pa, r_sg], w=[r_up])

            def c1_conv(c, tc):
                Wag4, r_Wag, dg3, r_dg, (up, r_up) = cstate[c]
                cs = slice(tc * 512, (tc + 1) * 512)
                pc, r_pc = bank()
                for j in range(31):
                    mm(pc, dg3[:, j, :], up[:, tc * 512 + j:tc * 512 + j + 512], j == 0, j == 30, [r_dg, r_up], [r_pc])
                op("dve", "tensor_scalar", co3[:, c, cs], pc, cv[:, c:c + 1], None, op0=ALU.add,
                   r=[r_pc, r_cv], w=[r_co])

            for c in range(9):
                if c < 8:
                    c1_setup(c)
                for tc in range(4):
                    if c < 8:
                        c1_proj(c, tc)
                    if c >= 1:
                        c1_conv(c - 1, tc)
            if STG < 3:
                P.barrier(); A.p = pmark; return
            sqs = [(A.bf(512), Res()) for _ in range(2)]
            m2 = (A.f32(512), Res())
            it = 0
            for tc in range(4):
                cs = slice(tc * 512, (tc + 1) * 512)
                psm, r_psm = bank()
                psq, r_psq = bank()
                for c in range(8):
                    mm(psm, ones_b, co3[:, c, cs], c == 0, c == 7, [r_cb, r_co], [r_psm])
                for c in range(8):
                    sq, r_sq = sqs[it % 2]
                    it += 1
                    op("pool", "tensor_tensor", sq, co3[:, c, cs], co3[:, c, cs], op=ALU.mult, r=[r_co], w=[r_sq])
                    mm(psq, ones_b, sq, c == 0, c == 7, [r_cb, r_sq], [r_psq])
                op("dve", "tensor_scalar", mean[:, cs], psm, 1.0 / D, None, op0=ALU.mult, r=[r_psm], w=[r_mr])
                op("dve", "tensor_tensor", m2[0], mean[:, cs], mean[:, cs], op=ALU.mult, r=[r_mr], w=[m2[1]])
                op("dve", "scalar_tensor_tensor", m2[0], psq, 1.0 / D, m2[0], op0=ALU.mult, op1=ALU.subtract,
                   r=[r_psq, m2[1]], w=[m2[1]])
                rsqrt_act(rstd[:, cs], m2[0], 1.0, eps5, [m2[1]], r_mr)
            if STG < 4:
                P.barrier(); A.p = pmark; return
            Wzs = [(A.bf(8 * 128), Res()) for _ in range(2)]
            szs = [(A.f32(512), Res()) for _ in range(2)]
            n1s = [(A.f32(512), Res()) for _ in range(2)]
            s1s = [(A.f32(512), Res()) for _ in range(2)]
            it = 0
            for c in range(8):
                Wz, r_Wz = Wzs[c % 2]
                Wz3 = v3(Wz, 8)
                ldw(Wz3, w_in_cols(l, OFF["cz"] + c * 128, 128), [r_Wz])
                for tc in range(4):
                    cs = slice(tc * 512, (tc + 1) * 512)
                    pz, r_pz = bank()
                    for kc in range(8):
                        mm(pz, Wz3[:, kc, :], hT3[:, kc, cs], kc == 0, kc == 7, [r_Wz, r_hT], [r_pz])
                    sz, r_sz = szs[it % 2]
                    n1, r_n1 = n1s[it % 2]
                    it += 1
                    s1, r_s1 = s1s[(it - 1) % 2]
                    sigmoid_act(sz, pz, [r_pz], r_sz)
                    op("dve", "tensor_tensor", sz, pz, sz, op=ALU.mult, r=[r_pz, r_sz], w=[r_sz])
                    op("dve", "tensor_tensor", n1, co3[:, c, cs], mean[:, cs], op=ALU.subtract, r=[r_co, r_mr], w=[r_n1])
                    op("dve", "tensor_tensor", n1, n1, rstd[:, cs], op=ALU.mult, r=[r_n1, r_mr], w=[r_n1])
                    op("dve", "tensor_scalar", n1, n1, cv[:, 8 + c:9 + c], cv[:, 16 + c:17 + c], op0=ALU.mult, op1=ALU.add,
                       r=[r_n1, r_cv], w=[r_n1])
                    sigmoid_act(s1, n1, [r_n1], r_s1)
                    op("pool", "tensor_tensor", n1, n1, s1, op=ALU.mult, r=[r_n1, r_s1], w=[r_n1])
                    op("pool", "tensor_tensor", co3[:, c, cs], n1, sz, op=ALU.mult, r=[r_n1, r_sz, r_co], w=[r_co])
            P.barrier()
            A.p = cmark
            if STG < 5:
                A.p = pmark; return
            finish(co3, r_co, w_conv_out, l, OFF["gc"], first)
            A.p = pmark

        def run_interleaved(*gens):
            gens = [g_ for g_ in gens if g_ is not None]
            while gens:
                for g_ in list(gens):
                    try:
                        next(g_)
                    except StopIteration:
                        gens.remove(g_)

        def branch_D(l, first):
            do = A.bf(8 * L)
            do3 = v3(do, 8)
            r_do = Res()
            dmark = A.p
            cwd = A.f32(24 * 4)
            cwd3 = v3(cwd, 24)
            r_cwd = Res()
            sv = A.f32(32)
            r_sv = Res()
            Wba = A.bf(8 * 16)
            Wba3 = v3(Wba, 8)
            r_Wba = Res()
            gmark = A.p
            cwr = A.f32(3072)
            r_cwr = Res()
            ld(cwr[0:4, :], dn_conv_w[l], [r_cwr], slot=s_misc)
            set_rot(list(range(8)))
            pb, r_pb = bank()
            for c in range(24):
                tr(pb[:, c * 4:c * 4 + 4], cwr[0:4, c * 128:(c + 1) * 128], ident_f[0:4, 0:4], [r_cwr, r_cst], [r_pb])
            op("dve", "tensor_copy", cwd, pb[:, 0:96], r=[r_pb], w=[r_cwd])
            ld(sv[:, 0:8], dn_a_log[l].partition_broadcast(128), [r_sv], slot=s_misc)
            ld(sv[:, 8:16], dn_dt_bias[l].partition_broadcast(128), [r_sv], r=[r_sv], slot=s_misc)
            wbc_ = A.f32(128)
            r_wbc_ = Res()
            ld(wbc_, dn_norm_w[l].partition_broadcast(128), [r_wbc_], slot=s_misc)
            op("dve", "tensor_tensor", wbc_, wbc_, ident_f, op=ALU.mult, r=[r_wbc_, r_cst], w=[r_wbc_])
            op("dve", "reduce_sum", sv[:, 16:17], wbc_, axis=AX.X, r=[r_wbc_, r_sv], w=[r_sv])
            op("act", "activation", sv[:, 0:8], sv[:, 0:8], AF.Exp, r=[r_sv], w=[r_sv])
            op("dve", "tensor_scalar", sv[:, 0:8], sv[:, 0:8], -1.0, None, op0=ALU.mult, r=[r_sv], w=[r_sv])
            nA, dtb, dnw = sv[:, 0:8], sv[:, 8:16], sv[:, 16:17]
            ldw(Wba3, w_in_cols(l, OFF["db"], 16), [r_Wba])
            P.barrier()
            for g in range(2):
                A.p = gmark
                set_rot([0, 1, 2, 3, 5])
                psO, r_psO = banks[4]
                psR = pst[3][:, :]
                r_psR = banks[6][1]
                Wd = A.bf(8 * 3 * 512)
                Wd4 = Wd.rearrange("p (k s c) -> p k s c", k=8, s=3)
                r_Wd = Res()
                for s, nm in enumerate(("dq", "dk", "dv")):
                    for k2 in range(2):
                        ldw(Wd4[:, k2 * 4:(k2 + 1) * 4, s, :],
                            w_in_cols(l, OFF[nm] + g * 512, 512)[:, k2 * 4:(k2 + 1) * 4, :], [r_Wd])
                Wz = A.bf(8 * 512)
                Wz3 = v3(Wz, 8)
                r_Wz = Res()
                for k2 in range(2):
                    ldw(Wz3[:, k2 * 4:(k2 + 1) * 4, :], w_in_cols(l, OFF["dz"] + g * 512, 512)[:, k2 * 4:(k2 + 1) * 4, :], [r_Wz])
                dgd = A.bf(12 * 4 * 128)
                dgd4 = dgd.rearrange("p (c j d) -> p c j d", c=12, j=4)
                r_dgd = Res()
                dgd3 = v3(dgd, 48)
                for s in range(3):
                    c0 = (s * 8 + g * 4) * 4
                    op("dve", "tensor_tensor", dgd3[:, s * 16:(s + 1) * 16, :], bc(ident_b, 1, [128, 16, 128]),
                       bc(cwd[:, c0:c0 + 16], 2, [128, 16, 128]), op=ALU.mult, r=[r_cb, r_cwd], w=[r_dgd])
                rawb = [[(A.bf(4 * 132), Res()) for s in range(3)] for par in range(2)]
                S = A.f32(512)
                S3 = v3(S, 4)
                r_S = Res()
                Sb = A.bf(512)
                Sb3 = v3(Sb, 4)
                r_Sb = Res()
                op("dve", "memset", S, 0.0, w=[r_S])
                op("dve", "memset", Sb, 0.0, w=[r_Sb])

                A2 = Arena(None, None, 4 * L, t=A.t, lo=mg_off)

                def T(n, dt=BF16):
                    nf = n if dt == F32 else (n + 1) // 2
                    AA = A if A.p + nf <= A.n else A2
                    return ((AA.bf(n) if dt == BF16 else AA.f32(n)), Res())
                qs_, ks_ = T(512, F32), T(512, F32)
                sqq, sqk = T(512), T(512)
                rnq, rnk = T(512, F32), T(512, F32)
                vsT_ = T(512)
                Lg_, Gb_ = T(512, F32), T(512, F32)
                FB = [dict(qT=T(512), kT=T(512), kt=T(512), E=T(512, F32), EG=T(512, F32)) for _ in range(2)]
                Mp_, IT_ = T(512, F32), T(512, F32)
                Mpp = [T(512) for _ in range(2)]
                App = [T(512) for _ in range(2)]
                Ms = [Mpp[k % 2] for k in range(6)]
                As = [App[k % 2] for k in range(5)]
                PB = [dict(rb=T(1024), wT=T(512), qg=T(512), it=T(512), kd=T(512), col=T(64, F32)) for _ in range(3)]
                vn_ = T(512)
                sz_s, on_s = T(512, F32), T(512, F32)
                sqs_, rns_ = T(512), T(512, F32)
                psR3 = v3(psR, 4)
                psO3 = v3(psO, 4)

                def pre_cols(n, B, F):
                    ts_ = slice(n * 128, (n + 1) * 128)
                    c_, r_c = B["col"]
                    E_, EG_ = F["E"], F["EG"]
                    pba, r_pba = bank()
                    for kc in range(8):
                        mm(pba[:, 0:16], hT3[:, kc, ts_], Wba3[:, kc, :], kc == 0, kc == 7, [r_hT, r_Wba], [r_pba])
                    beta4 = c_[:, 0:4]
                    g4 = c_[:, 4:8]
                    sigmoid_act(beta4, pba[:, g * 4:g * 4 + 4], [r_pba], r_c)
                    op("dve", "tensor_tensor", c_[:, 8:12], pba[:, 8 + g * 4:12 + g * 4], dtb[:, g * 4:g * 4 + 4], op=ALU.add,
                       r=[r_pba, r_sv, r_c], w=[r_c])
                    yield
                    op("act", "activation", c_[:, 8:12], c_[:, 8:12], AF.Exp, r=[r_c], w=[r_c])
                    op("act", "activation", c_[:, 8:12], c_[:, 8:12], AF.Ln, bias=one_col, scale=1.0, r=[r_c, r_cst], w=[r_c])
                    op("dve", "tensor_tensor", g4, c_[:, 8:12], nA[:, g * 4:g * 4 + 4], op=ALU.mult, r=[r_c, r_sv], w=[r_c])
                    yield
                    pcl, r_pcl = bank()
                    mm(pcl[:, 0:4], MUI_f, g4, True, True, [r_cst, r_c], [r_pcl])
                    mm(pcl[:, 4:8], SL_f, g4, True, True, [r_cst, r_c], [r_pcl])
                    mm(pcl[:, 8:12], CI0_f, g4, True, True, [r_cst, r_c], [r_pcl])
                    mm(pcl[:, 12:16], CI1_f, g4, True, True, [r_cst, r_c], [r_pcl])
                    op("act", "activation", c_[:, 16:32], pcl[:, 0:16], AF.Exp, r=[r_pcl, r_c], w=[r_c])
                    Lg3, Gb3 = v3(Lg_[0], 4), v3(Gb_[0], 4)
                    op("dve", "tensor_tensor", Lg3, bc(SL_f, 1, [128, 4, 128]), bc(g4, 2, [128, 4, 128]), op=ALU.mult,
                       r=[r_cst, r_c], w=[Lg_[1]])
                    op("act", "activation", Gb3, bc(g4, 2, [128, 4, 128]), AF.Copy, r=[r_c], w=[Gb_[1]])
                    yield
                    pD, r_pD = bank()
                    pD3 = v3(pD, 4)
                    for hl in range(4):
                        mm(pD3[:, hl, :], Lg3[:, hl, :], MUI_f, True, True, [Lg_[1], r_cst], [r_pD])
                    op("act", "activation", E_[0], pD, AF.Exp, r=[r_pD], w=[E_[1]])
                    pGC, r_pGC = bank()
                    pGC3 = v3(pGC, 4)
                    for hl in range(4):
                        mm(pGC3[:, hl, :], Gb3[:, hl, :], MUI_f, True, True, [Gb_[1], r_cst], [r_pGC])
                    op("act", "activation", EG_[0], pGC, AF.Exp, r=[r_pGC], w=[EG_[1]])

                def preA(n):
                    par = n % 2
                    ts_ = slice(n * 128, (n + 1) * 128)
                    B = PB[n % 3]
                    F = FB[n % 2]
                    rb_, kd_ = B["rb"], B["kd"]
                    qT_, kT_, kt_ = F["qT"], F["kT"], F["kt"]
                    cols = pre_cols(n, B, F)
                    prs = []
                    for s in range(3):
                        pr_, r_pr = bank()
                        pr3 = v3(pr_, 4)
                        for hl in range(4):
                            for kc in range(8):
                                mm(pr3[:, hl, :], Wd4[:, kc, s, hl * 128:(hl + 1) * 128], hT3[:, kc, ts_], kc == 0, kc == 7,
                                   [r_Wd, r_hT], [r_pr])
                        prs.append((pr3, r_pr))
                    rws = []
                    for s in range(3):
                        rw, r_rw = rawb[par][s]
                        rw3 = v3(rw, 4)
                        if n == 0:
                            op("pool", "memset", rw3[:, :, 0:3], 0.0, w=[r_rw])
                        else:
                            pw, r_pw = rawb[1 - par][s]
                            op("pool", "tensor_copy", rw3[:, :, 0:3], v3(pw, 4)[:, :, 128:131], r=[r_pw], w=[r_rw])
                        if s == 1:
                            op("dve", "tensor_copy", rw3[:, :, 3:131], prs[s][0], r=[prs[s][1]], w=[r_rw])
                        else:
                            op("act", "copy", rw3[:, :, 3:131], prs[s][0], r=[prs[s][1]], w=[r_rw])
                        rws.append((rw3, r_rw))
                    next(cols, None)
                    yield
                    pcs = []
                    for s in range(3):
                        pc_, r_pc = bank()
                        pc3 = v3(pc_, 4)
                        rw3, r_rw = rws[s]
                        for hl in range(4):
                            for j in range(4):
                                mm(pc3[:, hl, :], dgd4[:, s * 4 + hl, j, :], rw3[:, hl, j:j + 128], j == 0, j == 3,
                                   [r_dgd, r_rw], [r_pc])
                        pcs.append((pc_, r_pc))
                    sg_tmp = (qs_, ks_, rnq)
                    for s in range(3):
                        sigmoid_act(sg_tmp[s][0], pcs[s][0], [pcs[s][1]], sg_tmp[s][1])
                    for s in range(3):
                        dst = (qs_, ks_, vsT_)[s]
                        op("dve", "tensor_tensor", dst[0], pcs[s][0], sg_tmp[s][0], op=ALU.mult,
                           r=[pcs[s][1], sg_tmp[s][1]], w=[dst[1]])
                    next(cols, None)
                    yield
                    pns = []
                    for (src_, sq_) in ((qs_, sqq), (ks_, sqk)):
                        op("pool", "tensor_tensor", sq_[0], src_[0], src_[0], op=ALU.mult, r=[src_[1]], w=[sq_[1]])
                    for sq_ in (sqq, sqk):
                        pn, r_pn = bank()
                        mm(pn, ones_b, sq_[0], True, True, [r_cb, sq_[1]], [r_pn])
                        pns.append((pn, r_pn))
                    for (pn, r_pn), rn_ in zip(pns, (rnq, rnk)):
                        rsqrt_act(rn_[0], pn, 1.0, eps6, [r_pn], rn_[1])
                    for (src_, dstT, scl, rn_) in ((qs_, qT_, 128.0 ** -0.5, rnq), (ks_, kT_, 1.0, rnk)):
                        op("dve", "scalar_tensor_tensor", dstT[0], src_[0], scl, rn_[0], op0=ALU.mult, op1=ALU.mult,
                           r=[src_[1], rn_[1]], w=[dstT[1]])
                    next(cols, None)
                    yield
                    qT3, kT3, vsT3 = v3(qT_[0], 4), v3(kT_[0], 4), v3(vsT_[0], 4)
                    rb3 = v3(rb_[0], 4)
                    kt3, kd3 = v3(kt_[0], 4), v3(kd_[0], 4)
                    pt, r_pt = bank()
                    ptb = v3(pt.bitcast(BF16)[:, 0:512], 4)
                    for hl in range(4):
                        tr(ptb[:, hl, :], kT3[:, hl, :], ident_b, [kT_[1], r_cb], [r_pt])
                    pt2, r_pt2 = bank()
                    ptb2 = v3(pt2.bitcast(BF16)[:, 0:512], 4)
                    for hl in range(4):
                        tr(ptb2[:, hl, :], vsT3[:, hl, :], ident_b, [vsT_[1], r_cb], [r_pt2])
                    op("act", "copy", kt3, ptb, r=[r_pt], w=[kt_[1]])
                    op("dve", "tensor_copy", rb3[:, :, 0:128], ptb2, r=[r_pt2], w=[rb_[1]])
                    for _ in cols:
                        yield

                def preB(n):
                    B = PB[n % 3]
                    F = FB[n % 2]
                    rb_, wT_, qg_, it_, kd_ = B["rb"], B["wT"], B["qg"], B["it"], B["kd"]
                    c_, r_c = B["col"]
                    beta4, egc4, edec4 = c_[:, 0:4], c_[:, 16:20], c_[:, 20:24]
                    qT_, kT_, kt_, E_, EG_ = F["qT"], F["kT"], F["kt"], F["E"], F["EG"]
                    qT3, kT3 = v3(qT_[0], 4), v3(kT_[0], 4)
                    rb3 = v3(rb_[0], 4)
                    kt3, kd3 = v3(kt_[0], 4), v3(kd_[0], 4)
                    op("dve", "tensor_tensor", qg_[0], qT_[0], EG_[0], op=ALU.mult, r=[qT_[1], EG_[1]], w=[qg_[1]])
                    pG, r_pG = bank()
                    pG3 = v3(pG, 4)
                    for hl in range(4):
                        mm(pG3[:, hl, :], kT3[:, hl, :], kT3[:, hl, :], True, True, [kT_[1]], [r_pG])
                    pQ, r_pQ = bank()
                    pQ3 = v3(pQ, 4)
                    for hl in range(4):
                        mm(pQ3[:, hl, :], kT3[:, hl, :], qT3[:, hl, :], True, True, [kT_[1], qT_[1]], [r_pQ])
                    op("dve", "tensor_tensor", Mp_[0], pG, E_[0], op=ALU.mult, r=[r_pG, E_[1]], w=[Mp_[1]])
                    op("dve", "tensor_tensor", IT_[0], pQ, E_[0], op=ALU.mult, r=[r_pQ, E_[1]], w=[IT_[1]])
                    Mp3, IT3, it3 = v3(Mp_[0], 4), v3(IT_[0], 4), v3(it_[0], 4)
                    M3 = [v3(m_[0], 4) for m_ in Ms]
                    A3 = [v3(a_[0], 4) for a_ in As]
                    op("dve", "tensor_tensor", Mp3, Mp3, bc(beta4, 2, [128, 4, 128]), op=ALU.mult, r=[Mp_[1], r_c], w=[Mp_[1]])
                    op("dve", "tensor_tensor", M3[0], Mp3, bc(MU_f, 1, [128, 4, 128]), op=ALU.mult,
                       r=[Mp_[1], r_cst], w=[Ms[0][1]])
                    op("pool", "tensor_tensor", it3, IT3, bc(MUI_f, 1, [128, 4, 128]), op=ALU.mult,
                       r=[IT_[1], r_cst], w=[it_[1]])
                    op("dve", "tensor_tensor", rb3[:, :, 128:256], kt3, bc(egc4, 2, [128, 4, 128]), op=ALU.mult,
                       r=[kt_[1], r_c], w=[rb_[1]])
                    yield

                    def chain(k):
                        for hl in range(4):
                            mm(psR3[:, hl, :], M3[k][:, hl, :], rb3[:, hl, :], True, True, [Ms[k][1], rb_[1]], [r_psR])
                        op("dve", "tensor_tensor", rb_[0], rb_[0], psR, op=(ALU.subtract if k == 0 else ALU.add),
                           r=[r_psR, rb_[1]], w=[rb_[1]])
                    pt, r_pt = bank()
                    ptb = v3(pt.bitcast(BF16)[:, 0:512], 4)
                    for hl in range(4):
                        tr(ptb[:, hl, :], M3[0][:, hl, :], ident_b, [Ms[0][1], r_cb], [r_pt])
                    op("act", "copy", A3[0], ptb, r=[r_pt], w=[As[0][1]])
                    yield
                    for k in range(1, 6):
                        pM, r_pM = bank()
                        pM3 = v3(pM, 4)
                        for hl in range(4):
                            mm(pM3[:, hl, :], A3[k - 1][:, hl, :], M3[k - 1][:, hl, :], True, True,
                               [As[k - 1][1], Ms[k - 1][1]], [r_pM])
                        if k < 5:
                            pA, r_pA = bank()
                            pA3 = v3(pA, 4)
                            for hl in range(4):
                                mm(pA3[:, hl, :], M3[k - 1][:, hl, :], A3[k - 1][:, hl, :], True, True,
                                   [As[k - 1][1], Ms[k - 1][1]], [r_pA])
                        op("act", "copy", Ms[k][0], pM, r=[r_pM], w=[Ms[k][1]])
                        if k < 5:
                            op("dve", "tensor_copy", As[k][0], pA, r=[r_pA], w=[As[k][1]])
                        chain(k - 1)
                        yield
                    chain(5)
                    op("pool", "tensor_tensor", kd3, kt3, bc(edec4, 2, [128, 4, 128]), op=ALU.mult,
                       r=[kt_[1], r_c], w=[kd_[1]])
                    op("dve", "tensor_tensor", rb3, rb3, bc(beta4, 2, [128, 4, 256]), op=ALU.mult,
                       r=[r_c, rb_[1]], w=[rb_[1]])
                    yield
                    pt, r_pt = bank()
                    ptb = v3(pt.bitcast(BF16)[:, 0:512], 4)
                    for hl in range(4):
                        tr(ptb[:, hl, :], rb3[:, hl, 128:256], ident_b, [rb_[1], r_cb], [r_pt])
                    op("act", "copy", v3(wT_[0], 4), ptb, r=[r_pt], w=[wT_[1]])

                def scan(n):
                    ts_ = slice(n * 128, (n + 1) * 128)
                    B = PB[n % 3]
                    rb_, wT_, qg_, it_, kd_ = B["rb"], B["wT"], B["qg"], B["it"], B["kd"]
                    c_, r_c = B["col"]
                    egl = [c_[:, 24:28], c_[:, 28:32]]
                    rb3, wT3, qg3, it3, kd3 = v3(rb_[0], 4), v3(wT_[0], 4), v3(qg_[0], 4), v3(it_[0], 4), v3(kd_[0], 4)
                    vn3 = v3(vn_[0], 4)
                    pz, r_pz = bank()
                    pz3 = v3(pz, 4)
                    for hl in range(4):
                        for kc in range(8):
                            mm(pz3[:, hl, :], Wz3[:, kc, hl * 128:(hl + 1) * 128], hT3[:, kc, ts_], kc == 0, kc == 7,
                               [r_Wz, r_hT], [r_pz])
                    sigmoid_act(sz_s[0], pz, [r_pz], sz_s[1])
                    op("dve", "tensor_tensor", sz_s[0], pz, sz_s[0], op=ALU.mult, r=[r_pz, sz_s[1]], w=[sz_s[1]])
                    yield
                    for c in range(2):
                        rows = slice(c * 64, (c + 1) * 64)
                        pW, r_pW = bank()
                        pW3 = v3(pW, 4)
                        for hl in range(4):
                            mm(pW3[rows, hl, :], wT3[:, hl, rows], Sb3[:, hl, :], True, True, [wT_[1], r_Sb], [r_pW])
                        op("dve", "tensor_tensor", vn3[rows, :, :], rb3[rows, :, 0:128], pW3[rows, :, :], op=ALU.subtract,
                           r=[rb_[1], r_pW], w=[vn_[1]])
                        yield
                        pS, r_pS = bank()
                        pS3 = v3(pS, 4)
                        for hl in range(4):
                            mm(pS3[:, hl, :], kd3[rows, hl, :], vn3[rows, hl, :], True, True, [kd_[1], vn_[1]], [r_pS])
                        for hl in range(4):
                            mm(psO3[:, hl, rows], Sb3[:, hl, :], qg3[:, hl, rows], True, False, [r_Sb, qg_[1]], [r_psO])
                            mm(psO3[:, hl, rows], vn3[rows, hl, :], it3[rows, hl, rows], False, True, [vn_[1], it_[1]], [r_psO])
                        op("dve", "tensor_tensor", S3, S3, bc(egl[c], 2, [128, 4, 128]), op=ALU.mult, r=[r_S, r_c], w=[r_S])
                        op("dve", "tensor_tensor", S3, S3, pS3, op=ALU.add, r=[r_S, r_pS], w=[r_S])
                        op("act", "copy", Sb, S, r=[r_S], w=[r_Sb])
                        yield
                    op("act", "activation", sqs_[0], psO, AF.Square, r=[r_psO], w=[sqs_[1]])
                    pn, r_pn = bank()
                    mm(pn, ones_b, sqs_[0], True, True, [r_cb, sqs_[1]], [r_pn])
                    rsqrt_act(rns_[0], pn, 1.0 / 128, eps6, [r_pn], rns_[1])
                    yield
                    op("dve", "tensor_tensor", on_s[0], psO, rns_[0], op=ALU.mult, r=[r_psO, rns_[1]], w=[on_s[1]])
                    op("dve", "scalar_tensor_tensor", do3[:, g * 4:(g + 1) * 4, ts_], v3(on_s[0], 4), dnw, v3(sz_s[0], 4),
                       op0=ALU.mult, op1=ALU.mult, r=[on_s[1], r_sv, sz_s[1]], w=[r_do])

                run_interleaved(preA(0))
                run_interleaved(preB(0), preA(1))
                for n in range(NT):
                    run_interleaved(scan(n), preB(n + 1) if n + 1 < NT else None, preA(n + 2) if n + 2 < NT else None)
                P.barrier()
            A.p = dmark
            finish(do3, r_do, w_dn_out, l, OFF["gd"], first)
            A.p = pmark

        def phase_end(l, last):
            set_rot(list(range(8)))
            Wo = A.bf(8 * D)
            Wo3 = v3(Wo, 8)
            r_Wo = Res()
            for kc in range(8):
                ldw(Wo3[:, kc:kc + 1, :], wsq_cols(w_out, l, 0, D)[:, kc:kc + 1, :], [r_Wo])
            wbc = A.f32(D)
            r_wbc = Res()
            src = final_norm_w if last else norm_w[l + 1]
            ld(wbc, src.partition_broadcast(128), [r_wbc], slot=s_misc)
            xts = [(A.f32(D), Res()) for _ in range(2)]
            hns = [(A.bf(D), Res()) for _ in range(2)]
            outs = [(A.f32(D), Res()) for _ in range(2)]
            junks = [(A.bf(D), Res()) for _ in range(2)]
            xsrc = x_in if l == 0 else xs
            r_xs = Res()
            r_y = Res()
            for t in range(NT):
                ts_ = slice(t * 128, (t + 1) * 128)
                xt, r_xt = xts[t % 2]
                ld(xt, xsrc[ts_, :], [r_xt], r=[r_xs])
                for hf in range(2):
                    pp, r_pp = bank()
                    for kc in range(8):
                        mm(pp, mg3[:, kc, ts_], Wo3[:, kc, hf * 512:(hf + 1) * 512], kc == 0, kc == 7, [r_mg, r_Wo], [r_pp])
                    op("dve", "tensor_tensor", xt[:, hf * 512:(hf + 1) * 512], xt[:, hf * 512:(hf + 1) * 512], pp, op=ALU.add,
                       r=[r_pp, r_xt], w=[r_xt])
                if not last:
                    P.dma("sp", s_st, (lambda o_, i_: (lambda e: e.dma_start(out=o_, in_=i_)))(xs[ts_, :], xt), [r_xt], [r_xs])
                    hn, r_hn = hns[t % 2]
                    norm_to_hT(xt, r_xt, wbc, r_wbc, t, hn, r_hn, junks[t % 2][0], junks[t % 2][1], 2 * (t % 2))
                    hn_to_hT(hn, r_hn, t)
                else:
                    ot, r_ot = outs[t % 2]
                    norm_to_hT(xt, r_xt, wbc, r_wbc, t, ot, r_ot, junks[t % 2][0], junks[t % 2][1], 2 * (t % 2))
                    P.dma("sp", s_st, (lambda o_, i_: (lambda e: e.dma_start(out=o_, in_=i_)))(y_out[ts_, :], ot), [r_ot], [r_y])
            P.barrier()
            A.p = pmark
            return r_y

        def dump_mg():
            tmp = A.f32(L)
            r_tmp = Res()
            r_d = Res()
            for c in range(8):
                op("dve", "tensor_copy", tmp, mg3[:, c, :], r=[r_mg], w=[r_tmp])
                P.dma("sp", s_st, (lambda o_, i_: (lambda e: e.dma_start(out=o_, in_=i_)))(dbg_out[:, c, :], tmp), [r_tmp], [r_d])
            P.wait_for("sp", [r_d])
            P.barrier()

        phase0(0)
        r_y = None
        stop = False
        for l in range(depth):
            first = True
            for b in branches:
                {"D": branch_D, "A": branch_A, "C": branch_C}[b](l, first)
                first = False
            if dbg == "mg" and l == depth - 1:
                dump_mg()
                stop = True
                break
            r_y = phase_end(l, l == DEPTH - 1)
        if dbg == "hT":
            tmp = A.f32(L)
            r_tmp, r_d = Res(), Res()
            for c in range(8):
                op("dve", "tensor_copy", tmp, hT3[:, c, :], r=[r_hT], w=[r_tmp])
                P.dma("sp", s_st, (lambda o_, i_: (lambda e: e.dma_start(out=o_, in_=i_)))(dbg_out[:, c, :], tmp), [r_tmp], [r_d])
            P.wait_for("sp", [r_d])
        if r_y is not None:
            P.wait_for("sp", [r_y])
        P.barrier()
        P.emit(st)
    return nc


def make_consts():
    c = np.zeros((128, NCST), np.float32)
    i = np.arange(128)
    c[:, 0:128] = np.eye(128)
    sw = np.where((i % 64) < 32, i + 32, i - 32)
    c[sw, 128 + i] = 1.0
    c[:, 256:384] = 1.0
    same = (i[:, None] // 64) == (i[None, :] // 64)
    c[:, 384:512] = ((i[None, :] > i[:, None]) & same)
    c[:, 512:640] = ((i[None, :] >= i[:, None]) & same)
    c[:, 640:768] = ((i[:, None] > i[None, :]) & same)
    c[:, 768:896] = (i[None, :] >= i[:, None])
    c[:, 896:1024] = (i[:, None] < 64)
    c[:, 1024:1152] = (i[:, None] >= 64)
    invf = (10000.0 ** (-np.arange(0, 64, 2, dtype=np.float32) / 64)).astype(np.float32)
    c[:, 1152] = np.tile(invf, 4)
    c[:, 1153] = np.where((i % 64) < 32, -1.0, 1.0)
    c[:, 1154] = 1e-6
    c[:, 1155] = 1e-5
    return c


_CACHE = {}


def _prep_common(inputs):
    f = lambda k: np.ascontiguousarray(np.asarray(inputs[k], dtype=np.float32))
    com = {k: f(k) for k in ("norm_w", "w_in", "attn_subln_w", "w_attn_out", "conv_dw_w", "w_conv_out", "dn_conv_w",
                             "dn_a_log", "dn_dt_bias", "dn_norm_w", "w_dn_out", "w_out", "final_norm_w")}
    com["lam_qk"] = f("lam_qk").reshape(DEPTH, 256)
    cv = np.stack([f("conv_dw_b").reshape(DEPTH, 8, 128), f("conv_ln_w").reshape(DEPTH, 8, 128),
                   f("conv_ln_b").reshape(DEPTH, 8, 128)], axis=1).reshape(DEPTH, 24, 128)
    com["conv_vecs"] = np.ascontiguousarray(cv)
    com["cst"] = make_consts()
    return com


def kernel(**inputs):
    x = np.asarray(inputs["x"], dtype=np.float32)
    pos = np.asarray(inputs["positions"], dtype=np.int32)
    B = x.shape[0]
    com = _prep_common(inputs)
    if "nc" not in _CACHE:
        _CACHE["nc"] = build()
    nc = _CACHE["nc"]
    in_maps = []
    for b in range(B):
        m = dict(com)
        m["x"] = np.ascontiguousarray(x[b])
        m["positions"] = np.ascontiguousarray(pos[b])
        in_maps.append(m)
    res = run_bass_kernel_spmd(nc, in_maps, core_ids=list(range(B)))
    return np.stack([np.asarray(r["y"], dtype=np.float32) for r in res.results], axis=0)
```

```python
import contextlib
import math
import os
import numpy as np
import concourse.bass as bass
import concourse.mybir as mybir
from concourse.bass_utils import run_bass_kernel_spmd

F32 = mybir.dt.float32
BF16 = mybir.dt.bfloat16
I32 = mybir.dt.int32
AF = mybir.ActivationFunctionType
ALU = mybir.AluOpType
AX = mybir.AxisListType

L = 2048
D = 1024
NT = 16
DEPTH = 2
INW = 14352
OFF = dict(aq=0, ak=1024, av=2048, az=3072, ca=4096, cg=5120, cz=6144, dq=7168, dk=8192, dv=9216,
           dz=10240, db=11264, da=11272, ga=11280, gc=12304, gd=13328)
NCST = 1160
ARENA_F32 = 53200


class Res:
    __slots__ = ("w", "r", "excl")

    def __init__(self, excl=False):
        self.w = None
        self.r = {}
        self.excl = excl


class Prog:
    ENGS = ("pe", "act", "dve", "pool", "sp")
    ENGOBJ = {"pe": "tensor", "act": "scalar", "dve": "vector", "pool": "gpsimd", "sp": "sync"}

    def __init__(self, nc):
        self.nc = nc
        self.q = {e: [] for e in self.ENGS}
        self.cnt = {e: 0 for e in self.ENGS}
        self.seen = {e: {} for e in self.ENGS}
        self.slots = []

    def slot(self, name):
        self.cnt[name] = 0
        self.slots.append(name)
        return name

    def _deps(self, eng, reads, writes):
        waits = {}

        def need(dep):
            if dep is None:
                return
            c, s = dep
            if c == eng and eng == "pe":
                return
            if c in self.slots:
                s = self.cnt[c]
            if self.seen[eng].get(c, 0) < s:
                waits[c] = max(waits.get(c, 0), s)
        for r in reads:
            need(r.w)
        for w in writes:
            need(w.w)
            for c, s in w.r.items():
                need((c, s))
        for c, s in waits.items():
            self.seen[eng][c] = s
        return waits

    def _mark(self, counter, seq, reads, writes):
        for r in reads:
            r.r[counter] = max(r.r.get(counter, 0), seq)
        for w in writes:
            w.w = (counter, seq)
            w.r = {}

    def op(self, eng, fn, reads=(), writes=(), sig=True):
        ex = [r for r in reads if r.excl]
        if ex:
            reads = [r for r in reads if not r.excl]
            writes = list(writes) + ex
        waits = self._deps(eng, reads, writes)
        seq = self.cnt[eng] + 1
        if sig:
            self.cnt[eng] = seq
        self.q[eng].append((waits, fn, eng if sig else None, 1))
        self._mark(eng, seq, reads, writes)

    def dma(self, queue, slot, fn, reads=(), writes=()):
        waits = self._deps(queue, reads, writes)
        self.cnt[slot] += 1
        self.q[queue].append((waits, fn, slot, 16))
        self._mark(slot, self.cnt[slot], reads, writes)

    def wait_for(self, eng, resources):
        waits = self._deps(eng, resources, ())
        self.q[eng].append((waits, None, None, 0))

    def barrier(self):
        for e in self.ENGS:
            waits = {}
            for c, v in self.cnt.items():
                if c == e:
                    continue
                if self.seen[e].get(c, 0) < v:
                    waits[c] = v
                    self.seen[e][c] = v
            self.q[e].append((waits, None, None, 0))

    def emit(self, st):
        nc = self.nc
        sems = {c: st.enter_context(nc.semaphore("s_" + c)) for c in self.cnt}
        mult = {c: (16 if c in self.slots else 1) for c in self.cnt}
        with nc.Block() as block:
            def mk(ename):
                def body(eng):
                    for (waits, fn, inc, amt) in self.q[ename]:
                        for c, s in waits.items():
                            eng.wait_ge(sems[c], s * mult[c])
                        if fn is not None:
                            ins = fn(eng)
                            if inc is not None:
                                ins.then_inc(sems[inc], amt)
                return body
            for ename in self.ENGS:
                getattr(block, self.ENGOBJ[ename])(mk(ename))


class Arena:
    def __init__(self, nc, st, n, t=None, lo=0):
        self.t = t if t is not None else st.enter_context(nc.sbuf_tensor("arena", [128, n], F32))
        self.p = lo
        self.n = lo + n
        self.hi = 0

    def f32(self, n):
        ap = self.t[:, self.p:self.p + n]
        self.p += n
        self.hi = max(self.hi, self.p)
        assert self.p <= self.n, ("arena overflow", self.p, self.n)
        return ap

    def bf(self, n):
        nf = (n + 1) // 2
        return self.f32(nf).bitcast(BF16)

    def i32(self, n):
        return self.f32(n).bitcast(I32)


def v3(ap, a):
    return ap.rearrange("p (a b) -> p a b", a=a)


def build(dbg=None, depth=DEPTH, branches="DAC"):
    nc = bass.Bass("TRN2", target_bir_lowering=False)
    dram = {}

    def din(name, shape, dt=F32):
        dram[name] = nc.dram_tensor(name, list(shape), dt, kind="ExternalInput").ap()
        return dram[name]
    x_in = din("x", [L, D])
    pos_in = din("positions", [L], I32)
    norm_w = din("norm_w", [DEPTH, D])
    w_in = din("w_in", [DEPTH, D, INW])
    lam_qk = din("lam_qk", [DEPTH, 256])
    attn_subln_w = din("attn_subln_w", [DEPTH, 128])
    w_attn_out = din("w_attn_out", [DEPTH, D, D])
    conv_dw_w = din("conv_dw_w", [DEPTH, 31, D])
    conv_vecs = din("conv_vecs", [DEPTH, 24, 128])
    w_conv_out = din("w_conv_out", [DEPTH, D, D])
    dn_conv_w = din("dn_conv_w", [DEPTH, 4, 3072])
    dn_a_log = din("dn_a_log", [DEPTH, 8])
    dn_dt_bias = din("dn_dt_bias", [DEPTH, 8])
    dn_norm_w = din("dn_norm_w", [DEPTH, 128])
    w_dn_out = din("w_dn_out", [DEPTH, D, D])
    w_out = din("w_out", [DEPTH, D, D])
    final_norm_w = din("final_norm_w", [D])
    cst_in = din("cst", [128, NCST])
    y_out = nc.dram_tensor("y", [L, D], F32, kind="ExternalOutput").ap()
    xs = nc.dram_tensor("xs", [L, D], F32, kind="Internal").ap()
    dbg_out = None
    if dbg:
        dbg_out = nc.dram_tensor("dbg", [128, 8, L], F32, kind="ExternalOutput").ap()

    P = Prog(nc)
    st = contextlib.ExitStack()
    with st:
        A = Arena(nc, st, ARENA_F32)
        pst = [st.enter_context(nc.psum_tensor(f"ps{i}", [128, 1024], F32)) for i in range(4)]
        banks = []
        for i in range(4):
            for hh in range(2):
                banks.append((pst[i][:, hh * 512:(hh + 1) * 512], Res(excl=True)))
        rot = {"list": list(range(8)), "i": 0}

        def set_rot(lst):
            rot["list"] = lst
            rot["i"] = 0

        def bank():
            b = banks[rot["list"][rot["i"] % len(rot["list"])]]
            rot["i"] += 1
            return b

        s_ld = P.slot("d_ld")
        s_st = P.slot("d_st")
        s_w = [P.slot("d_w0"), P.slot("d_w1")]
        s_misc = P.slot("d_misc")
        wslot = {"i": 0}

        def op(eng, name, *args, r=(), w=(), **kw):
            P.op(eng, lambda e: getattr(e, name)(*args, **kw), r, w)

        def mm(out, lhsT, rhs, start, stop, r, w):
            P.op("pe", lambda e: e.matmul(out, lhsT=lhsT, rhs=rhs, start=start, stop=stop), r, w, sig=True)

        def tr(out, in_, ident, r, w):
            P.op("pe", lambda e: e.transpose(out, in_, ident), r, w)

        def ld(out, in_, w, r=(), slot=None):
            P.dma("sp", slot or s_ld, lambda e: e.dma_start(out=out, in_=in_), r, w)

        def ldw(out, in_, w, r=()):
            s = s_w[0]
            P.dma("pool", s, lambda e: e.dma_start(out=out, in_=in_), r, w)

        def sigmoid_act(dst, src_, r_src, r_dst):
            op("act", "activation", dst, src_, AF.Exp, scale=-1.0, r=list(r_src), w=[r_dst])
            op("act", "activation", dst, dst, AF.Ln, bias=cst[:, 256:257], scale=1.0, r=[r_dst, r_cst], w=[r_dst])
            op("act", "activation", dst, dst, AF.Exp, scale=-1.0, r=[r_dst], w=[r_dst])

        def rsqrt_act(dst, src_, scale, eps_ap, r_src, r_dst):
            op("act", "activation", dst, src_, AF.Ln, bias=eps_ap, scale=scale, r=list(r_src) + [r_cst], w=[r_dst])
            op("act", "activation", dst, dst, AF.Exp, scale=-0.5, r=[r_dst], w=[r_dst])

        def bc(ap, axis, shape):
            return ap.unsqueeze(axis).broadcast_to(list(shape))

        def w_in_cols(l, c0, n):
            return w_in[l].rearrange("(kc p) c -> p kc c", p=128)[:, :, c0:c0 + n]

        def wsq_cols(wt, l, c0, n):
            return wt[l].rearrange("(kc p) c -> p kc c", p=128)[:, :, c0:c0 + n]

        cst = A.f32(NCST)
        r_cst = Res()
        ld(cst, cst_in[:, :], [r_cst], slot=s_misc)
        cb = A.bf(1152)
        r_cb = Res()
        op("dve", "tensor_copy", cb, cst[:, 0:1152], r=[r_cst], w=[r_cb])
        ident_b, pswap_b, ones_b = cb[:, 0:128], cb[:, 128:256], cb[:, 256:384]
        MU_b, MUI_b, CAUS_b = cb[:, 384:512], cb[:, 512:640], cb[:, 768:896]
        ident_f, ones_f = cst[:, 0:128], cst[:, 256:384]
        MU_f, MUI_f, SL_f = cst[:, 384:512], cst[:, 512:640], cst[:, 640:768]
        CI0_f, CI1_f = cst[:, 896:1024], cst[:, 1024:1152]
        invf, sgn, eps6, eps5 = cst[:, 1152:1153], cst[:, 1153:1154], cst[:, 1154:1155], cst[:, 1155:1156]
        one_col = cst[:, 256:257]
        hT = A.bf(8 * L)
        hT3 = v3(hT, 8)
        r_hT = Res()
        mg_off = A.p
        mg = A.bf(8 * L)
        mg3 = v3(mg, 8)
        r_mg = Res()
        stat = A.f32(64)
        r_stat = Res()
        pmark = A.p

        def rope_tables():
            cosT = A.f32(L)
            sinT = A.f32(L)
            r_cos, r_sin = Res(), Res()
            tmark = A.p
            pi_ = A.i32(L)
            pf = A.f32(L)
            kf = A.f32(L)
            yv = A.f32(L)
            m1 = A.f32(L)
            ki = m1.bitcast(I32)
            r = Res()
            ld(pi_, pos_in.partition_broadcast(128), [r], slot=s_misc)
            op("dve", "tensor_copy", pf, pi_, r=[r], w=[r])
            op("dve", "tensor_scalar", pf, pf, invf, None, op0=ALU.mult, r=[r, r_cst], w=[r])
            op("dve", "tensor_scalar", kf, pf, 1.0 / (2 * math.pi), None, op0=ALU.mult, r=[r], w=[r])
            op("dve", "tensor_copy", ki, kf, r=[r], w=[r])
            op("dve", "tensor_copy", kf, ki, r=[r], w=[r])
            c1 = 6.28125
            c2 = 2 * math.pi - c1
            op("dve", "scalar_tensor_tensor", yv, kf, -c1, pf, op0=ALU.mult, op1=ALU.add, r=[r], w=[r])
            op("dve", "scalar_tensor_tensor", yv, kf, -c2, yv, op0=ALU.mult, op1=ALU.add, r=[r], w=[r])
            op("dve", "tensor_scalar", m1, yv, -math.pi, 2 * math.pi, op0=ALU.is_lt, op1=ALU.mult, r=[r], w=[r])
            op("dve", "tensor_tensor", yv, yv, m1, op=ALU.add, r=[r], w=[r])
            op("dve", "tensor_scalar", m1, yv, math.pi, 2 * math.pi, op0=ALU.is_gt, op1=ALU.mult, r=[r], w=[r])
            op("dve", "tensor_tensor", yv, yv, m1, op=ALU.subtract, r=[r], w=[r])
            op("act", "activation", sinT, yv, AF.Sin, r=[r], w=[r_sin])
            op("dve", "tensor_scalar", sinT, sinT, sgn, None, op0=ALU.mult, r=[r_sin, r_cst], w=[r_sin])
            op("dve", "tensor_scalar", yv, yv, math.pi / 2, None, op0=ALU.add, r=[r], w=[r])
            op("dve", "tensor_scalar", m1, yv, math.pi, 2 * math.pi, op0=ALU.is_gt, op1=ALU.mult, r=[r], w=[r])
            op("dve", "tensor_tensor", yv, yv, m1, op=ALU.subtract, r=[r], w=[r])
            op("act", "activation", cosT, yv, AF.Sin, r=[r], w=[r_cos])
            P.barrier()
            A.p = tmark
            return cosT, sinT, r_cos, r_sin

        def norm_to_hT(xt, r_xt, wbc, r_wbc, t, hn, r_hn, junk, r_junk, sc):
            ssq = stat[:, sc:sc + 1]
            rs = stat[:, sc + 1:sc + 2]
            op("dve", "memset", ssq, 0.0, w=[r_stat])
            op("act", "activation", junk, xt, AF.Square, accum_out=ssq, r=[r_xt, r_stat], w=[r_junk, r_stat])
            rsqrt_act(rs, ssq, 1.0 / D, eps6, [r_stat], r_stat)
            op("dve", "scalar_tensor_tensor", hn, xt, rs, wbc, op0=ALU.mult, op1=ALU.mult,
               r=[r_xt, r_stat, r_wbc], w=[r_hn])

        def hn_to_hT(hn, r_hn, t):
            pb, r_pb = bank()
            pbb = pb.bitcast(BF16)
            for kc in range(8):
                tr(pbb[:, kc * 128:(kc + 1) * 128], hn[:, kc * 128:(kc + 1) * 128], ident_b, [r_hn, r_cb], [r_pb])
            op("act", "copy", hT3[:, :, t * 128:(t + 1) * 128], v3(pbb, 8), r=[r_pb], w=[r_hT])

        def phase0(l):
            wbc = A.f32(D)
            r_wbc = Res()
            ld(wbc, norm_w[l].partition_broadcast(128), [r_wbc], slot=s_misc)
            xts = [(A.f32(D), Res()) for _ in range(2)]
            hns = [(A.bf(D), Res()) for _ in range(2)]
            junk = A.bf(D)
            r_junk = Res()
            for t in range(NT):
                xt, r_xt = xts[t % 2]
                hn, r_hn = hns[t % 2]
                ld(xt, x_in[t * 128:(t + 1) * 128, :], [r_xt])
                norm_to_hT(xt, r_xt, wbc, r_wbc, t, hn, r_hn, junk, r_junk, 2 * (t % 2))
                hn_to_hT(hn, r_hn, t)
            P.barrier()
            A.p = pmark

        def finish(act3, r_act, w_bo, l, goff, first):
            set_rot(list(range(8)))
            wbs = [(A.bf(8 * 128), Res()) for _ in range(2)]
            wgs = [(A.bf(8 * 128), Res()) for _ in range(2)]
            gss = [(A.f32(512), Res()) for _ in range(2)]
            tms = [(A.f32(512), Res()) for _ in range(2)]
            it = 0
            for oc in range(8):
                wb, r_wb = wbs[oc % 2]
                wg, r_wg = wgs[oc % 2]
                wb3, wg3 = v3(wb, 8), v3(wg, 8)
                ldw(wb3, wsq_cols(w_bo, l, oc * 128, 128), [r_wb])
                ldw(wg3, w_in_cols(l, goff + oc * 128, 128), [r_wg])
                for tc in range(4):
                    cs = slice(tc * 512, (tc + 1) * 512)
                    py, r_py = bank()
                    for c in range(8):
                        mm(py, wb3[:, c, :], act3[:, c, cs], c == 0, c == 7, [r_wb, r_act], [r_py])
                    pg, r_pg = bank()
                    for c in range(8):
                        mm(pg, wg3[:, c, :], hT3[:, c, cs], c == 0, c == 7, [r_wg, r_hT], [r_pg])
                    gs, r_gs = gss[it % 2]
                    tm, r_tm = tms[it % 2]
                    it += 1
                    sigmoid_act(gs, pg, [r_pg], r_gs)
                    if first:
                        op("dve", "tensor_tensor", mg3[:, oc, cs], py, gs, op=ALU.mult, r=[r_py, r_gs], w=[r_mg])
                    else:
                        op("dve", "tensor_tensor", tm, py, gs, op=ALU.mult, r=[r_py, r_gs], w=[r_tm])
                        op("pool", "tensor_tensor", mg3[:, oc, cs], mg3[:, oc, cs], tm, op=ALU.add,
                           r=[r_tm, r_mg], w=[r_mg])
            P.barrier()
            A.p = pmark

        def branch_A(l, first):
            li = 0.8 - 0.6 * math.exp(-0.3 * l)
            ao = A.bf(8 * L)
            ao3 = v3(ao, 8)
            r_ao = Res()
            amark = A.p
            cosT, sinT, r_cos, r_sin = rope_tables()
            lq = A.f32(256)
            r_lq = Res()
            ld(lq, lam_qk[l].partition_broadcast(128), [r_lq], slot=s_misc)
            sm = A.f32(8)
            r_sm = Res()
            pr = A.f32(64)
            op("dve", "tensor_tensor", pr, lq[:, 0:64], lq[:, 64:128], op=ALU.mult, r=[r_lq], w=[r_sm])
            op("dve", "reduce_sum", sm[:, 0:1], pr, axis=AX.X, r=[r_sm], w=[r_sm])
            op("dve", "tensor_tensor", pr, lq[:, 128:192], lq[:, 192:256], op=ALU.mult, r=[r_lq, r_sm], w=[r_sm])
            op("dve", "reduce_sum", sm[:, 1:2], pr, axis=AX.X, r=[r_sm], w=[r_sm])
            op("act", "activation", sm[:, 2:4], sm[:, 0:2], AF.Exp, r=[r_sm], w=[r_sm])
            op("dve", "tensor_tensor", sm[:, 4:5], sm[:, 3:4], sm[:, 2:3], op=ALU.subtract, r=[r_sm], w=[r_sm])
            op("dve", "tensor_scalar", sm[:, 4:5], sm[:, 4:5], -li, None, op0=ALU.add, r=[r_sm], w=[r_sm])
            nlam = sm[:, 4:5]
            wbc_ = A.f32(128)
            r_wbc_ = Res()
            ld(wbc_, attn_subln_w[l].partition_broadcast(128), [r_wbc_], slot=s_misc)
            op("dve", "tensor_tensor", wbc_, wbc_, ident_f, op=ALU.mult, r=[r_wbc_, r_cst], w=[r_wbc_])
            op("dve", "reduce_sum", sm[:, 5:6], wbc_, axis=AX.X, r=[r_wbc_, r_sm], w=[r_sm])
            op("dve", "tensor_scalar", sm[:, 6:7], sm[:, 5:6], 1.0 - li, None, op0=ALU.mult, r=[r_sm], w=[r_sm])
            wcol = sm[:, 6:7]
            STGA = int(os.environ.get("STGA", "9"))
            if STGA < 2:
                P.barrier(); A.p = pmark; return
            W4s = [(A.bf(8 * 512), Res()) for _ in range(2)]
            qTs = [(A.bf(L), Res()) for _ in range(2)]
            kTs = [(A.bf(L), Res()) for _ in range(2)]
            vhs = [(A.bf(L), Res()) for _ in range(2)]
            szs = [(A.bf(L), Res()) for _ in range(2)]
            raws = [(A.bf(512), Res()) for _ in range(2)]
            t1s = [(A.f32(512), Res()) for _ in range(2)]
            t2s = [(A.f32(512), Res()) for _ in range(2)]
            ebig = A.bf(2048)
            ebs = [(ebig[:, i * 512:(i + 1) * 512], Res()) for i in range(4)]
            ob = [(A.f32(512), Res()) for _ in range(4)]
            sqb = (A.bf(512), Res())
            O_b = [banks[0], banks[1]]
            S_b = [banks[2], banks[3]]
            sc_b = [[banks[4], banks[5]], [banks[6], banks[7]]]
            set_rot([4, 5, 6, 7])
            cnt = {"raw": 0, "e": 0}
            for h in range(8):
                W4, r_W4 = W4s[h % 2]
                W44 = W4.rearrange("p (k s c) -> p k s c", k=8, s=4)
                for s, nm in enumerate(("aq", "ak", "av", "az")):
                    ldw(W44[:, :, s, :], w_in_cols(l, OFF[nm] + h * 128, 128), [r_W4])
                qT, r_qT = qTs[h % 2]
                kT, r_kT = kTs[h % 2]
                vh, r_vh = vhs[h % 2]
                vh3 = v3(vh, 16)
                sz, r_sz = szs[h % 2]
                for s, (dst, r_dst) in enumerate(((qT, r_qT), (kT, r_kT))):
                    for tc in range(4):
                        cs = slice(tc * 512, (tc + 1) * 512)
                        pq, r_pq = bank()
                        for kc in range(8):
                            mm(pq, W44[:, kc, s, :], hT3[:, kc, cs], kc == 0, kc == 7, [r_W4, r_hT], [r_pq])
                        raw, r_raw = raws[cnt["raw"] % 2]
                        t1, r_t1 = t1s[cnt["raw"] % 2]
                        t2, r_t2 = t2s[cnt["raw"] % 2]
                        cnt["raw"] += 1
                        op("act", "copy", raw, pq, r=[r_pq], w=[r_raw])
                        op("dve", "tensor_tensor", t1, pq, cosT[:, cs], op=ALU.mult, r=[r_pq, r_cos], w=[r_t1])
                        p2, r_p2 = bank()
                        mm(p2, pswap_b, raw, True, True, [r_cb, r_raw], [r_p2])
                        op("dve", "tensor_tensor", t2, p2, sinT[:, cs], op=ALU.mult, r=[r_p2, r_sin], w=[r_t2])
                        op("pool", "tensor_tensor", dst[:, cs], t1, t2, op=ALU.add, r=[r_t1, r_t2], w=[r_dst])
                for g4 in range(4 if STGA >= 3 else 0):
                    pv, r_pv = bank()
                    pv3 = v3(pv, 4)
                    for tt in range(4):
                        t = g4 * 4 + tt
                        for kc in range(8):
                            mm(pv3[:, tt, :], hT3[:, kc, t * 128:(t + 1) * 128], W44[:, kc, 2, :], kc == 0, kc == 7,
                               [r_W4, r_hT], [r_pv])
                    op("act", "copy", vh3[:, g4 * 4:(g4 + 1) * 4, :], pv3, r=[r_pv], w=[r_vh])
                for tc in range(4 if STGA >= 3 else 0):
                    cs = slice(tc * 512, (tc + 1) * 512)
                    pz, r_pz = bank()
                    for kc in range(8):
                        mm(pz, W44[:, kc, 3, :], hT3[:, kc, cs], kc == 0, kc == 7, [r_W4, r_hT], [r_pz])
                    t1, r_t1 = t1s[tc % 2]
                    sigmoid_act(t1, pz, [r_pz], r_t1)
                    op("dve", "tensor_tensor", sz[:, cs], pz, t1, op=ALU.mult, r=[r_pz, r_t1], w=[r_sz])
                for qc in range(4 if STGA >= 4 else 0):
                    nk = 4 * qc + 4
                    qs0 = qc * 512
                    def c0_of(kt):
                        return max(kt - 4 * qc, 0) * 128

                    def emit_scores(kt):
                        c0 = c0_of(kt)
                        for m in range(2):
                            ps_, r_ps = sc_b[kt % 2][m]
                            rows = slice(m * 64, (m + 1) * 64)
                            mm(ps_[:, c0:512], kT[rows, kt * 128:(kt + 1) * 128], qT[rows, qs0 + c0:qs0 + 512],
                               True, True, [r_kT, r_qT], [r_ps])

                    def emit_exp(kt):
                        c0 = c0_of(kt)
                        par = kt % 2
                        pboth = v3(pst[2 + par][:, :], 2)[:, :, c0:512]
                        eboth = v3(ebig[:, par * 1024:(par + 1) * 1024], 2)
                        r_pss_ = [sc_b[par][0][1], sc_b[par][1][1]]
                        r_ebs_ = [ebs[par * 2][1], ebs[par * 2 + 1][1]]
                        op("act", "activation", eboth[:, :, c0:512], pboth, AF.Exp, scale=0.125, r=r_pss_, w=r_ebs_)
                        if kt - 4 * qc >= 0:
                            op("dve", "tensor_tensor", eboth[:, :, c0:c0 + 128], eboth[:, :, c0:c0 + 128],
                               bc(CAUS_b, 1, [128, 2, 128]), op=ALU.mult, r=r_ebs_ + [r_cb], w=r_ebs_)

                    def emit_pv(kt):
                        c0 = c0_of(kt)
                        for m in range(2):
                            eb, r_eb = ebs[(kt % 2) * 2 + m]
                            mm(O_b[m][0][:, c0:512], vh3[:, kt, :], eb[:, c0:512], kt == 0, kt == nk - 1,
                               [r_vh, r_eb], [O_b[m][1]])
                            mm(S_b[m][0][:, c0:512], ones_b, eb[:, c0:512], kt == 0, kt == nk - 1,
                               [r_cb, r_eb], [S_b[m][1]])
                    emit_scores(0)
                    for kt in range(nk):
                        if kt + 1 < nk:
                            emit_scores(kt + 1)
                        emit_exp(kt)
                        emit_pv(kt)
                    if STGA < 5:
                        continue
                    cs = slice(qs0, qs0 + 512)
                    (o0, r_o0), (o1, r_o1), (rc, r_rc), (rc1, r_rc1) = ob
                    op("dve", "tensor_copy", o0, O_b[0][0], r=[O_b[0][1]], w=[r_o0])
                    op("act", "activation", rc, S_b[0][0], AF.Ln, r=[S_b[0][1]], w=[r_rc])
                    op("dve", "tensor_copy", o1, O_b[1][0], r=[O_b[1][1]], w=[r_o1])
                    op("act", "activation", rc1, S_b[1][0], AF.Ln, r=[S_b[1][1]], w=[r_rc1])
                    op("act", "activation", rc, rc, AF.Exp, scale=-1.0, r=[r_rc], w=[r_rc])
                    op("act", "activation", rc1, rc1, AF.Exp, scale=-1.0, r=[r_rc1], w=[r_rc1])
                    op("dve", "tensor_tensor", o0, o0, rc, op=ALU.mult, r=[r_o0, r_rc], w=[r_o0])
                    op("dve", "tensor_tensor", o1, o1, rc1, op=ALU.mult, r=[r_o1, r_rc1], w=[r_o1])
                    op("dve", "scalar_tensor_tensor", o0, o1, nlam, o0, op0=ALU.mult, op1=ALU.add,
                       r=[r_o0, r_o1, r_sm], w=[r_o0])
                    op("act", "activation", sqb[0], o0, AF.Square, r=[r_o0], w=[sqb[1]])
                    pss, r_pss = bank()
                    mm(pss, ones_b, sqb[0], True, True, [r_cb, sqb[1]], [r_pss])
                    rsqrt_act(rc, pss, 1.0 / 128, eps6, [r_pss], r_rc)
                    op("dve", "tensor_tensor", o0, o0, rc, op=ALU.mult, r=[r_o0, r_rc], w=[r_o0])
                    op("dve", "scalar_tensor_tensor", ao3[:, h, cs], o0, wcol, sz[:, cs], op0=ALU.mult, op1=ALU.mult,
                       r=[r_o0, r_sm, r_sz], w=[r_ao])
            P.barrier()
            A.p = amark
            if STGA < 6:
                A.p = pmark; return
            finish(ao3, r_ao, w_attn_out, l, OFF["ga"], first)
            A.p = pmark

        def branch_C(l, first):
            co = A.bf(8 * L)
            co3 = v3(co, 8)
            r_co = Res()
            r_cos = [[Res() for _tc in range(4)] for _c in range(8)]
            mean = A.f32(L)
            rstd = A.f32(L)
            r_mr = Res()
            cmark = A.p
            set_rot(list(range(8)))
            cwr = A.f32(D)
            r_cwr = Res()
            ld(cwr[0:31, :], conv_dw_w[l], [r_cwr], slot=s_misc)
            cvr = A.f32(128)
            r_cvr = Res()
            ld(cvr[0:24, :], conv_vecs[l], [r_cvr], slot=s_misc)
            cwT = A.f32(8 * 32)
            cwT3 = v3(cwT, 8)
            r_cwT = Res()
            cv = A.f32(24)
            r_cv = Res()
            pb, r_pb = bank()
            for c in range(8):
                tr(pb[:, c * 32:c * 32 + 31], cwr[0:31, c * 128:(c + 1) * 128], ident_f[0:31, 0:31], [r_cwr, r_cst], [r_pb])
            op("dve", "tensor_copy", v3(cwT, 8)[:, :, 0:31], v3(pb[:, 0:256], 8)[:, :, 0:31], r=[r_pb], w=[r_cwT])
            pb2, r_pb2 = bank()
            tr(pb2[:, 0:24], cvr[0:24, :], ident_f[0:24, 0:24], [r_cvr, r_cst], [r_pb2])
            op("dve", "tensor_copy", cv, pb2[:, 0:24], r=[r_pb2], w=[r_cv])
            STG = int(os.environ.get("STG", "9"))
            if STG < 2:
                P.barrier(); A.p = pmark; return
            Wags = [(A.bf(8 * 256), Res()) for _ in range(2)]
            dgs = [(A.bf(31 * 128), Res()) for _ in range(2)]
            ups = [(A.bf(30 + L + 2), Res()) for _ in range(2)]
            sgs = [(A.f32(512), Res()) for _ in range(2)]
            for (u, r_u) in ups:
                op("pool", "memset", u[:, 0:30], 0.0, w=[r_u])
            itc = {"i": 0}
            cstate = {}

            def c1_setup(c):
                Wag, r_Wag = Wags[c % 2]
                Wag4 = Wag.rearrange("p (k s c) -> p k s c", k=8, s=2)
                ldw(Wag4[:, :, 0, :], w_in_cols(l, OFF["ca"] + c * 128, 128), [r_Wag])
                ldw(Wag4[:, :, 1, :], w_in_cols(l, OFF["cg"] + c * 128, 128), [r_Wag])
                dg, r_dg = dgs[c % 2]
                dg3 = v3(dg, 31)
                op("dve", "tensor_tensor", dg3, bc(ident_b, 1, [128, 31, 128]), bc(cwT3[:, c, 0:31], 2, [128, 31, 128]),
                   op=ALU.mult, r=[r_cb, r_cwT], w=[r_dg])
                cstate[c] = (Wag4, r_Wag, dg3, r_dg, ups[c % 2])

            def c1_proj(c, tc):
                Wag4, r_Wag, dg3, r_dg, (up, r_up) = cstate[c]
                cs = slice(tc * 512, (tc + 1) * 512)
                pa, r_pa = bank()
                for kc in range(8):
                    mm(pa, Wag4[:, kc, 0, :], hT3[:, kc, cs], kc == 0, kc == 7, [r_Wag, r_hT], [r_pa])
                pg, r_pg = bank()
                for kc in range(8):
                    mm(pg, Wag4[:, kc, 1, :], hT3[:, kc, cs], kc == 0, kc == 7, [r_Wag, r_hT], [r_pg])
                sg, r_sg = sgs[itc["i"] % 2]
                itc["i"] += 1
                sigmoid_act(sg, pg, [r_pg], r_sg)
                op("dve", "tensor_tensor", up[:, 30 + tc * 512:30 + (tc + 1) * 512], pa, sg, op=ALU.mult,
                   r=[r_pa, r_sg], w=[r_up])

            def c1_conv(c, tc):
                Wag4, r_Wag, dg3, r_dg, (up, r_up) = cstate[c]
                cs = slice(tc * 512, (tc + 1) * 512)
                pc, r_pc = bank()
                for j in range(31):
                    mm(pc, dg3[:, j, :], up[:, tc * 512 + j:tc * 512 + j + 512], j == 0, j == 30, [r_dg, r_up], [r_pc])
                op("dve", "tensor_scalar", co3[:, c, cs], pc, cv[:, c:c + 1], None, op0=ALU.add,
                   r=[r_pc, r_cv], w=[r_cos[c][tc]])

            for c in range(9):
                if c < 8:
                    c1_setup(c)
                for tc in range(4):
                    if c < 8:
                        c1_proj(c, tc)
                    if c >= 1:
                        c1_conv(c - 1, tc)
            if STG < 3:
                P.barrier(); A.p = pmark; return
            sqs = [(A.bf(512), Res()) for _ in range(2)]
            m2 = (A.f32(512), Res())
            it = 0
            for tc in range(4):
                cs = slice(tc * 512, (tc + 1) * 512)
                psm, r_psm = bank()
                psq, r_psq = bank()
                for c in range(8):
                    mm(psm, ones_b, co3[:, c, cs], c == 0, c == 7, [r_cb, r_cos[c][tc]], [r_psm])
                for c in range(8):
                    sq, r_sq = sqs[it % 2]
                    it += 1
                    op("pool", "tensor_tensor", sq, co3[:, c, cs], co3[:, c, cs], op=ALU.mult, r=[r_cos[c][tc]], w=[r_sq])
                    mm(psq, ones_b, sq, c == 0, c == 7, [r_cb, r_sq], [r_psq])
                op("dve", "tensor_scalar", mean[:, cs], psm, 1.0 / D, None, op0=ALU.mult, r=[r_psm], w=[r_mr])
                op("dve", "tensor_tensor", m2[0], mean[:, cs], mean[:, cs], op=ALU.mult, r=[r_mr], w=[m2[1]])
                op("dve", "scalar_tensor_tensor", m2[0], psq, 1.0 / D, m2[0], op0=ALU.mult, op1=ALU.subtract,
                   r=[r_psq, m2[1]], w=[m2[1]])
                rsqrt_act(rstd[:, cs], m2[0], 1.0, eps5, [m2[1]], r_mr)
            if STG < 4:
                P.barrier(); A.p = pmark; return
            Wzs = [(A.bf(8 * 128), Res()) for _ in range(2)]
            szs = [(A.f32(512), Res()) for _ in range(2)]
            n1s = [(A.f32(512), Res()) for _ in range(2)]
            s1s = [(A.f32(512), Res()) for _ in range(2)]
            it = 0
            for c in range(8):
                Wz, r_Wz = Wzs[c % 2]
                Wz3 = v3(Wz, 8)
                ldw(Wz3, w_in_cols(l, OFF["cz"] + c * 128, 128), [r_Wz])
                for tc in range(4):
                    cs = slice(tc * 512, (tc + 1) * 512)
                    pz, r_pz = bank()
                    for kc in range(8):
                        mm(pz, Wz3[:, kc, :], hT3[:, kc, cs], kc == 0, kc == 7, [r_Wz, r_hT], [r_pz])
                    sz, r_sz = szs[it % 2]
                    n1, r_n1 = n1s[it % 2]
                    it += 1
                    s1, r_s1 = s1s[(it - 1) % 2]
                    sigmoid_act(sz, pz, [r_pz], r_sz)
                    op("dve", "tensor_tensor", sz, pz, sz, op=ALU.mult, r=[r_pz, r_sz], w=[r_sz])
                    op("dve", "tensor_tensor", n1, co3[:, c, cs], mean[:, cs], op=ALU.subtract, r=[r_cos[c][tc], r_mr], w=[r_n1])
                    op("dve", "tensor_tensor", n1, n1, rstd[:, cs], op=ALU.mult, r=[r_n1, r_mr], w=[r_n1])
                    op("dve", "tensor_scalar", n1, n1, cv[:, 8 + c:9 + c], cv[:, 16 + c:17 + c], op0=ALU.mult, op1=ALU.add,
                       r=[r_n1, r_cv], w=[r_n1])
                    sigmoid_act(s1, n1, [r_n1], r_s1)
                    op("pool", "tensor_tensor", n1, n1, s1, op=ALU.mult, r=[r_n1, r_s1], w=[r_n1])
                    op("pool", "tensor_tensor", co3[:, c, cs], n1, sz, op=ALU.mult, r=[r_n1, r_sz, r_cos[c][tc]], w=[r_cos[c][tc]])
            P.barrier()
            A.p = cmark
            if STG < 5:
                A.p = pmark; return
            finish(co3, Res(), w_conv_out, l, OFF["gc"], first)
            A.p = pmark

        def run_interleaved(*gens):
            gens = [g_ for g_ in gens if g_ is not None]
            while gens:
                for g_ in list(gens):
                    try:
                        next(g_)
                    except StopIteration:
                        gens.remove(g_)

        def branch_D(l, first):
            do = A.bf(8 * L)
            do3 = v3(do, 8)
            r_do = Res()
            dmark = A.p
            cwd = A.f32(24 * 4)
            cwd3 = v3(cwd, 24)
            r_cwd = Res()
            sv = A.f32(32)
            r_sv = Res()
            Wba = A.bf(8 * 16)
            Wba3 = v3(Wba, 8)
            r_Wba = Res()
            gmark = A.p
            cwr = A.f32(3072)
            r_cwr = Res()
            ld(cwr[0:4, :], dn_conv_w[l], [r_cwr], slot=s_misc)
            set_rot(list(range(8)))
            pb, r_pb = bank()
            for c in range(24):
                tr(pb[:, c * 4:c * 4 + 4], cwr[0:4, c * 128:(c + 1) * 128], ident_f[0:4, 0:4], [r_cwr, r_cst], [r_pb])
            op("dve", "tensor_copy", cwd, pb[:, 0:96], r=[r_pb], w=[r_cwd])
            ld(sv[:, 0:8], dn_a_log[l].partition_broadcast(128), [r_sv], slot=s_misc)
            ld(sv[:, 8:16], dn_dt_bias[l].partition_broadcast(128), [r_sv], r=[r_sv], slot=s_misc)
            wbc_ = A.f32(128)
            r_wbc_ = Res()
            ld(wbc_, dn_norm_w[l].partition_broadcast(128), [r_wbc_], slot=s_misc)
            op("dve", "tensor_tensor", wbc_, wbc_, ident_f, op=ALU.mult, r=[r_wbc_, r_cst], w=[r_wbc_])
            op("dve", "reduce_sum", sv[:, 16:17], wbc_, axis=AX.X, r=[r_wbc_, r_sv], w=[r_sv])
            op("act", "activation", sv[:, 0:8], sv[:, 0:8], AF.Exp, r=[r_sv], w=[r_sv])
            op("dve", "tensor_scalar", sv[:, 0:8], sv[:, 0:8], -1.0, None, op0=ALU.mult, r=[r_sv], w=[r_sv])
            nA, dtb, dnw = sv[:, 0:8], sv[:, 8:16], sv[:, 16:17]
            ldw(Wba3, w_in_cols(l, OFF["db"], 16), [r_Wba])
            P.barrier()
            for g in range(2):
                A.p = gmark
                set_rot([0, 1, 2, 3, 5])
                psO, r_psO = banks[4]
                psR = pst[3][:, :]
                r_psR = banks[6][1]
                Wd = A.bf(8 * 3 * 512)
                Wd4 = Wd.rearrange("p (k s c) -> p k s c", k=8, s=3)
                r_Wd = Res()
                for s, nm in enumerate(("dq", "dk", "dv")):
                    for k2 in range(2):
                        ldw(Wd4[:, k2 * 4:(k2 + 1) * 4, s, :],
                            w_in_cols(l, OFF[nm] + g * 512, 512)[:, k2 * 4:(k2 + 1) * 4, :], [r_Wd])
                Wz = A.bf(8 * 512)
                Wz3 = v3(Wz, 8)
                r_Wz = Res()
                for k2 in range(2):
                    ldw(Wz3[:, k2 * 4:(k2 + 1) * 4, :], w_in_cols(l, OFF["dz"] + g * 512, 512)[:, k2 * 4:(k2 + 1) * 4, :], [r_Wz])
                dgd = A.bf(12 * 4 * 128)
                dgd4 = dgd.rearrange("p (c j d) -> p c j d", c=12, j=4)
                r_dgd = Res()
                dgd3 = v3(dgd, 48)
                for s in range(3):
                    c0 = (s * 8 + g * 4) * 4
                    op("dve", "tensor_tensor", dgd3[:, s * 16:(s + 1) * 16, :], bc(ident_b, 1, [128, 16, 128]),
                       bc(cwd[:, c0:c0 + 16], 2, [128, 16, 128]), op=ALU.mult, r=[r_cb, r_cwd], w=[r_dgd])
                rawb = [[(A.bf(4 * 132), Res()) for s in range(3)] for par in range(2)]
                S = A.f32(512)
                S3 = v3(S, 4)
                r_S = Res()
                Sb = A.bf(512)
                Sb3 = v3(Sb, 4)
                r_Sb = Res()
                op("dve", "memset", S, 0.0, w=[r_S])
                op("dve", "memset", Sb, 0.0, w=[r_Sb])

                A2 = Arena(None, None, 4 * L, t=A.t, lo=mg_off)

                def T(n, dt=BF16):
                    nf = n if dt == F32 else (n + 1) // 2
                    AA = A if A.p + nf <= A.n else A2
                    return ((AA.bf(n) if dt == BF16 else AA.f32(n)), Res())
                qs_, ks_ = T(512, F32), T(512, F32)
                sqq, sqk = T(512), T(512)
                rnq, rnk = T(512, F32), T(512, F32)
                vsT_ = T(512)
                Lg_, Gb_ = T(512, F32), T(512, F32)
                FB = [dict(qT=T(512), kT=T(512), kt=T(512), E=T(512, F32), EG=T(512, F32)) for _ in range(2)]
                Mp_, IT_ = T(512, F32), T(512, F32)
                Mpp = [T(512) for _ in range(2)]
                App = [T(512) for _ in range(2)]
                Ms = [Mpp[k % 2] for k in range(6)]
                As = [App[k % 2] for k in range(5)]
                PB = [dict(rb=T(1024), wT=T(512), qg=T(512), it=T(512), kd=T(512), col=T(64, F32)) for _ in range(3)]
                vn_ = T(512)
                sz_s, on_s = T(512, F32), T(512, F32)
                sqs_, rns_ = T(512), T(512, F32)
                psR3 = v3(psR, 4)
                psO3 = v3(psO, 4)

                def pre_cols(n, B, F):
                    ts_ = slice(n * 128, (n + 1) * 128)
                    c_, r_c = B["col"]
                    E_, EG_ = F["E"], F["EG"]
                    pba, r_pba = bank()
                    for kc in range(8):
                        mm(pba[:, 0:16], hT3[:, kc, ts_], Wba3[:, kc, :], kc == 0, kc == 7, [r_hT, r_Wba], [r_pba])
                    beta4 = c_[:, 0:4]
                    g4 = c_[:, 4:8]
                    sigmoid_act(beta4, pba[:, g * 4:g * 4 + 4], [r_pba], r_c)
                    op("dve", "tensor_tensor", c_[:, 8:12], pba[:, 8 + g * 4:12 + g * 4], dtb[:, g * 4:g * 4 + 4], op=ALU.add,
                       r=[r_pba, r_sv, r_c], w=[r_c])
                    yield
                    op("act", "activation", c_[:, 8:12], c_[:, 8:12], AF.Exp, r=[r_c], w=[r_c])
                    op("act", "activation", c_[:, 8:12], c_[:, 8:12], AF.Ln, bias=one_col, scale=1.0, r=[r_c, r_cst], w=[r_c])
                    op("dve", "tensor_tensor", g4, c_[:, 8:12], nA[:, g * 4:g * 4 + 4], op=ALU.mult, r=[r_c, r_sv], w=[r_c])
                    yield
                    pcl, r_pcl = bank()
                    mm(pcl[:, 0:4], MUI_f, g4, True, True, [r_cst, r_c], [r_pcl])
                    mm(pcl[:, 4:8], SL_f, g4, True, True, [r_cst, r_c], [r_pcl])
                    mm(pcl[:, 8:12], CI0_f, g4, True, True, [r_cst, r_c], [r_pcl])
                    mm(pcl[:, 12:16], CI1_f, g4, True, True, [r_cst, r_c], [r_pcl])
                    op("act", "activation", c_[:, 16:32], pcl[:, 0:16], AF.Exp, r=[r_pcl, r_c], w=[r_c])
                    Lg3, Gb3 = v3(Lg_[0], 4), v3(Gb_[0], 4)
                    op("dve", "tensor_tensor", Lg3, bc(SL_f, 1, [128, 4, 128]), bc(g4, 2, [128, 4, 128]), op=ALU.mult,
                       r=[r_cst, r_c], w=[Lg_[1]])
                    op("act", "activation", Gb3, bc(g4, 2, [128, 4, 128]), AF.Copy, r=[r_c], w=[Gb_[1]])
                    yield
                    pD, r_pD = bank()
                    pD3 = v3(pD, 4)
                    for hl in range(4):
                        mm(pD3[:, hl, :], Lg3[:, hl, :], MUI_f, True, True, [Lg_[1], r_cst], [r_pD])
                    op("act", "activation", E_[0], pD, AF.Exp, r=[r_pD], w=[E_[1]])
                    pGC, r_pGC = bank()
                    pGC3 = v3(pGC, 4)
                    for hl in range(4):
                        mm(pGC3[:, hl, :], Gb3[:, hl, :], MUI_f, True, True, [Gb_[1], r_cst], [r_pGC])
                    op("act", "activation", EG_[0], pGC, AF.Exp, r=[r_pGC], w=[EG_[1]])

                def preA(n):
                    par = n % 2
                    ts_ = slice(n * 128, (n + 1) * 128)
                    B = PB[n % 3]
                    F = FB[n % 2]
                    rb_, kd_ = B["rb"], B["kd"]
                    qT_, kT_, kt_ = F["qT"], F["kT"], F["kt"]
                    cols = pre_cols(n, B, F)
                    prs = []
                    for s in range(3):
                        pr_, r_pr = bank()
                        pr3 = v3(pr_, 4)
                        for hl in range(4):
                            for kc in range(8):
                                mm(pr3[:, hl, :], Wd4[:, kc, s, hl * 128:(hl + 1) * 128], hT3[:, kc, ts_], kc == 0, kc == 7,
                                   [r_Wd, r_hT], [r_pr])
                        prs.append((pr3, r_pr))
                    rws = []
                    for s in range(3):
                        rw, r_rw = rawb[par][s]
                        rw3 = v3(rw, 4)
                        if n == 0:
                            op("pool", "memset", rw3[:, :, 0:3], 0.0, w=[r_rw])
                        else:
                            pw, r_pw = rawb[1 - par][s]
                            op("pool", "tensor_copy", rw3[:, :, 0:3], v3(pw, 4)[:, :, 128:131], r=[r_pw], w=[r_rw])
                        if s == 1:
                            op("dve", "tensor_copy", rw3[:, :, 3:131], prs[s][0], r=[prs[s][1]], w=[r_rw])
                        else:
                            op("act", "copy", rw3[:, :, 3:131], prs[s][0], r=[prs[s][1]], w=[r_rw])
                        rws.append((rw3, r_rw))
                    next(cols, None)
                    yield
                    pcs = []
                    for s in range(3):
                        pc_, r_pc = bank()
                        pc3 = v3(pc_, 4)
                        rw3, r_rw = rws[s]
                        for hl in range(4):
                            for j in range(4):
                                mm(pc3[:, hl, :], dgd4[:, s * 4 + hl, j, :], rw3[:, hl, j:j + 128], j == 0, j == 3,
                                   [r_dgd, r_rw], [r_pc])
                        pcs.append((pc_, r_pc))
                    sg_tmp = (qs_, ks_, rnq)
                    for s in range(3):
                        sigmoid_act(sg_tmp[s][0], pcs[s][0], [pcs[s][1]], sg_tmp[s][1])
                    for s in range(3):
                        dst = (qs_, ks_, vsT_)[s]
                        op("dve", "tensor_tensor", dst[0], pcs[s][0], sg_tmp[s][0], op=ALU.mult,
                           r=[pcs[s][1], sg_tmp[s][1]], w=[dst[1]])
                    next(cols, None)
                    yield
                    pns = []
                    for (src_, sq_) in ((qs_, sqq), (ks_, sqk)):
                        op("pool", "tensor_tensor", sq_[0], src_[0], src_[0], op=ALU.mult, r=[src_[1]], w=[sq_[1]])
                    for sq_ in (sqq, sqk):
                        pn, r_pn = bank()
                        mm(pn, ones_b, sq_[0], True, True, [r_cb, sq_[1]], [r_pn])
                        pns.append((pn, r_pn))
                    for (pn, r_pn), rn_ in zip(pns, (rnq, rnk)):
                        rsqrt_act(rn_[0], pn, 1.0, eps6, [r_pn], rn_[1])
                    for (src_, dstT, scl, rn_) in ((qs_, qT_, 128.0 ** -0.5, rnq), (ks_, kT_, 1.0, rnk)):
                        op("dve", "scalar_tensor_tensor", dstT[0], src_[0], scl, rn_[0], op0=ALU.mult, op1=ALU.mult,
                           r=[src_[1], rn_[1]], w=[dstT[1]])
                    next(cols, None)
                    yield
                    qT3, kT3, vsT3 = v3(qT_[0], 4), v3(kT_[0], 4), v3(vsT_[0], 4)
                    rb3 = v3(rb_[0], 4)
                    kt3, kd3 = v3(kt_[0], 4), v3(kd_[0], 4)
                    pt, r_pt = bank()
                    ptb = v3(pt.bitcast(BF16)[:, 0:512], 4)
                    for hl in range(4):
                        tr(ptb[:, hl, :], kT3[:, hl, :], ident_b, [kT_[1], r_cb], [r_pt])
                    pt2, r_pt2 = bank()
                    ptb2 = v3(pt2.bitcast(BF16)[:, 0:512], 4)
                    for hl in range(4):
                        tr(ptb2[:, hl, :], vsT3[:, hl, :], ident_b, [vsT_[1], r_cb], [r_pt2])
                    op("act", "copy", kt3, ptb, r=[r_pt], w=[kt_[1]])
                    op("dve", "tensor_copy", rb3[:, :, 0:128], ptb2, r=[r_pt2], w=[rb_[1]])
                    for _ in cols:
                        yield

                def preB(n):
                    B = PB[n % 3]
                    F = FB[n % 2]
                    rb_, wT_, qg_, it_, kd_ = B["rb"], B["wT"], B["qg"], B["it"], B["kd"]
                    c_, r_c = B["col"]
                    beta4, egc4, edec4 = c_[:, 0:4], c_[:, 16:20], c_[:, 20:24]
                    qT_, kT_, kt_, E_, EG_ = F["qT"], F["kT"], F["kt"], F["E"], F["EG"]
                    qT3, kT3 = v3(qT_[0], 4), v3(kT_[0], 4)
                    rb3 = v3(rb_[0], 4)
                    kt3, kd3 = v3(kt_[0], 4), v3(kd_[0], 4)
                    op("dve", "tensor_tensor", qg_[0], qT_[0], EG_[0], op=ALU.mult, r=[qT_[1], EG_[1]], w=[qg_[1]])
                    pG, r_pG = bank()
                    pG3 = v3(pG, 4)
                    for hl in range(4):
                        mm(pG3[:, hl, :], kT3[:, hl, :], kT3[:, hl, :], True, True, [kT_[1]], [r_pG])
                    pQ, r_pQ = bank()
                    pQ3 = v3(pQ, 4)
                    for hl in range(4):
                        mm(pQ3[:, hl, :], kT3[:, hl, :], qT3[:, hl, :], True, True, [kT_[1], qT_[1]], [r_pQ])
                    op("dve", "tensor_tensor", Mp_[0], pG, E_[0], op=ALU.mult, r=[r_pG, E_[1]], w=[Mp_[1]])
                    op("dve", "tensor_tensor", IT_[0], pQ, E_[0], op=ALU.mult, r=[r_pQ, E_[1]], w=[IT_[1]])
                    Mp3, IT3, it3 = v3(Mp_[0], 4), v3(IT_[0], 4), v3(it_[0], 4)
                    M3 = [v3(m_[0], 4) for m_ in Ms]
                    A3 = [v3(a_[0], 4) for a_ in As]
                    op("dve", "tensor_tensor", Mp3, Mp3, bc(beta4, 2, [128, 4, 128]), op=ALU.mult, r=[Mp_[1], r_c], w=[Mp_[1]])
                    op("dve", "tensor_tensor", M3[0], Mp3, bc(MU_f, 1, [128, 4, 128]), op=ALU.mult,
                       r=[Mp_[1], r_cst], w=[Ms[0][1]])
                    op("pool", "tensor_tensor", it3, IT3, bc(MUI_f, 1, [128, 4, 128]), op=ALU.mult,
                       r=[IT_[1], r_cst], w=[it_[1]])
                    op("dve", "tensor_tensor", rb3[:, :, 128:256], kt3, bc(egc4, 2, [128, 4, 128]), op=ALU.mult,
                       r=[kt_[1], r_c], w=[rb_[1]])
                    yield

                    def chain(k):
                        for hl in range(4):
                            mm(psR3[:, hl, :], M3[k][:, hl, :], rb3[:, hl, :], True, True, [Ms[k][1], rb_[1]], [r_psR])
                        op("dve", "tensor_tensor", rb_[0], rb_[0], psR, op=(ALU.subtract if k == 0 else ALU.add),
                           r=[r_psR, rb_[1]], w=[rb_[1]])
                    pt, r_pt = bank()
                    ptb = v3(pt.bitcast(BF16)[:, 0:512], 4)
                    for hl in range(4):
                        tr(ptb[:, hl, :], M3[0][:, hl, :], ident_b, [Ms[0][1], r_cb], [r_pt])
                    op("act", "copy", A3[0], ptb, r=[r_pt], w=[As[0][1]])
                    yield
                    for k in range(1, 6):
                        pM, r_pM = bank()
                        pM3 = v3(pM, 4)
                        for hl in range(4):
                            mm(pM3[:, hl, :], A3[k - 1][:, hl, :], M3[k - 1][:, hl, :], True, True,
                               [As[k - 1][1], Ms[k - 1][1]], [r_pM])
                        if k < 5:
                            pA, r_pA = bank()
                            pA3 = v3(pA, 4)
                            for hl in range(4):
                                mm(pA3[:, hl, :], M3[k - 1][:, hl, :], A3[k - 1][:, hl, :], True, True,
                                   [As[k - 1][1], Ms[k - 1][1]], [r_pA])
                        op("act", "copy", Ms[k][0], pM, r=[r_pM], w=[Ms[k][1]])
                        if k < 5:
                            op("dve", "tensor_copy", As[k][0], pA, r=[r_pA], w=[As[k][1]])
                        chain(k - 1)
                        yield
                    chain(5)
                    op("pool", "tensor_tensor", kd3, kt3, bc(edec4, 2, [128, 4, 128]), op=ALU.mult,
                       r=[kt_[1], r_c], w=[kd_[1]])
                    op("dve", "tensor_tensor", rb3, rb3, bc(beta4, 2, [128, 4, 256]), op=ALU.mult,
                       r=[r_c, rb_[1]], w=[rb_[1]])
                    yield
                    pt, r_pt = bank()
                    ptb = v3(pt.bitcast(BF16)[:, 0:512], 4)
                    for hl in range(4):
                        tr(ptb[:, hl, :], rb3[:, hl, 128:256], ident_b, [rb_[1], r_cb], [r_pt])
                    op("act", "copy", v3(wT_[0], 4), ptb, r=[r_pt], w=[wT_[1]])

                def scan(n):
                    ts_ = slice(n * 128, (n + 1) * 128)
                    B = PB[n % 3]
                    rb_, wT_, qg_, it_, kd_ = B["rb"], B["wT"], B["qg"], B["it"], B["kd"]
                    c_, r_c = B["col"]
                    egl = [c_[:, 24:28], c_[:, 28:32]]
                    rb3, wT3, qg3, it3, kd3 = v3(rb_[0], 4), v3(wT_[0], 4), v3(qg_[0], 4), v3(it_[0], 4), v3(kd_[0], 4)
                    vn3 = v3(vn_[0], 4)
                    pz, r_pz = bank()
                    pz3 = v3(pz, 4)
                    for hl in range(4):
                        for kc in range(8):
                            mm(pz3[:, hl, :], Wz3[:, kc, hl * 128:(hl + 1) * 128], hT3[:, kc, ts_], kc == 0, kc == 7,
                               [r_Wz, r_hT], [r_pz])
                    sigmoid_act(sz_s[0], pz, [r_pz], sz_s[1])
                    op("dve", "tensor_tensor", sz_s[0], pz, sz_s[0], op=ALU.mult, r=[r_pz, sz_s[1]], w=[sz_s[1]])
                    yield
                    for c in range(2):
                        rows = slice(c * 64, (c + 1) * 64)
                        pW, r_pW = bank()
                        pW3 = v3(pW, 4)
                        for hl in range(4):
                            mm(pW3[rows, hl, :], wT3[:, hl, rows], Sb3[:, hl, :], True, True, [wT_[1], r_Sb], [r_pW])
                        op("dve", "tensor_tensor", vn3[rows, :, :], rb3[rows, :, 0:128], pW3[rows, :, :], op=ALU.subtract,
                           r=[rb_[1], r_pW], w=[vn_[1]])
                        yield
                        pS, r_pS = bank()
                        pS3 = v3(pS, 4)
                        for hl in range(4):
                            mm(pS3[:, hl, :], kd3[rows, hl, :], vn3[rows, hl, :], True, True, [kd_[1], vn_[1]], [r_pS])
                        for hl in range(4):
                            mm(psO3[:, hl, rows], Sb3[:, hl, :], qg3[:, hl, rows], True, False, [r_Sb, qg_[1]], [r_psO])
                            mm(psO3[:, hl, rows], vn3[rows, hl, :], it3[rows, hl, rows], False, True, [vn_[1], it_[1]], [r_psO])
                        op("dve", "tensor_tensor", S3, S3, bc(egl[c], 2, [128, 4, 128]), op=ALU.mult, r=[r_S, r_c], w=[r_S])
                        op("dve", "tensor_tensor", S3, S3, pS3, op=ALU.add, r=[r_S, r_pS], w=[r_S])
                        op("act", "copy", Sb, S, r=[r_S], w=[r_Sb])
                        yield
                    op("act", "activation", sqs_[0], psO, AF.Square, r=[r_psO], w=[sqs_[1]])
                    pn, r_pn = bank()
                    mm(pn, ones_b, sqs_[0], True, True, [r_cb, sqs_[1]], [r_pn])
                    rsqrt_act(rns_[0], pn, 1.0 / 128, eps6, [r_pn], rns_[1])
                    yield
                    op("dve", "tensor_tensor", on_s[0], psO, rns_[0], op=ALU.mult, r=[r_psO, rns_[1]], w=[on_s[1]])
                    op("dve", "scalar_tensor_tensor", do3[:, g * 4:(g + 1) * 4, ts_], v3(on_s[0], 4), dnw, v3(sz_s[0], 4),
                       op0=ALU.mult, op1=ALU.mult, r=[on_s[1], r_sv, sz_s[1]], w=[r_do])

                run_interleaved(preA(0))
                run_interleaved(preB(0), preA(1))
                for n in range(NT):
                    run_interleaved(scan(n), preB(n + 1) if n + 1 < NT else None, preA(n + 2) if n + 2 < NT else None)
                P.barrier()
            A.p = dmark
            finish(do3, r_do, w_dn_out, l, OFF["gd"], first)
            A.p = pmark

        def phase_end(l, last):
            set_rot(list(range(8)))
            Wo = A.bf(8 * D)
            Wo3 = v3(Wo, 8)
            r_Wo = Res()
            for kc in range(8):
                ldw(Wo3[:, kc:kc + 1, :], wsq_cols(w_out, l, 0, D)[:, kc:kc + 1, :], [r_Wo])
            wbc = A.f32(D)
            r_wbc = Res()
            src = final_norm_w if last else norm_w[l + 1]
            ld(wbc, src.partition_broadcast(128), [r_wbc], slot=s_misc)
            xts = [(A.f32(D), Res()) for _ in range(2)]
            hns = [(A.bf(D), Res()) for _ in range(2)]
            outs = [(A.f32(D), Res()) for _ in range(2)]
            junk = A.bf(D)
            r_junk = Res()
            xsrc = x_in if l == 0 else xs
            r_xs = Res()
            r_y = Res()
            for t in range(NT):
                ts_ = slice(t * 128, (t + 1) * 128)
                xt, r_xt = xts[t % 2]
                ld(xt, xsrc[ts_, :], [r_xt], r=[r_xs])
                for hf in range(2):
                    pp, r_pp = bank()
                    for kc in range(8):
                        mm(pp, mg3[:, kc, ts_], Wo3[:, kc, hf * 512:(hf + 1) * 512], kc == 0, kc == 7, [r_mg, r_Wo], [r_pp])
                    op("dve", "tensor_tensor", xt[:, hf * 512:(hf + 1) * 512], xt[:, hf * 512:(hf + 1) * 512], pp, op=ALU.add,
                       r=[r_pp, r_xt], w=[r_xt])
                if not last:
                    P.dma("sp", s_st, (lambda o_, i_: (lambda e: e.dma_start(out=o_, in_=i_)))(xs[ts_, :], xt), [r_xt], [r_xs])
                    hn, r_hn = hns[t % 2]
                    norm_to_hT(xt, r_xt, wbc, r_wbc, t, hn, r_hn, junk, r_junk, 2 * (t % 2))
                    hn_to_hT(hn, r_hn, t)
                else:
                    ot, r_ot = outs[t % 2]
                    norm_to_hT(xt, r_xt, wbc, r_wbc, t, ot, r_ot, junk, r_junk, 2 * (t % 2))
                    P.dma("sp", s_st, (lambda o_, i_: (lambda e: e.dma_start(out=o_, in_=i_)))(y_out[ts_, :], ot), [r_ot], [r_y])
            P.barrier()
            A.p = pmark
            return r_y

        def dump_mg():
            tmp = A.f32(L)
            r_tmp = Res()
            r_d = Res()
            for c in range(8):
                op("dve", "tensor_copy", tmp, mg3[:, c, :], r=[r_mg], w=[r_tmp])
                P.dma("sp", s_st, (lambda o_, i_: (lambda e: e.dma_start(out=o_, in_=i_)))(dbg_out[:, c, :], tmp), [r_tmp], [r_d])
            P.wait_for("sp", [r_d])
            P.barrier()

        phase0(0)
        r_y = None
        stop = False
        for l in range(depth):
            first = True
            for b in branches:
                {"D": branch_D, "A": branch_A, "C": branch_C}[b](l, first)
                first = False
            if dbg == "mg" and l == depth - 1:
                dump_mg()
                stop = True
                break
            r_y = phase_end(l, l == DEPTH - 1)
        if dbg == "hT":
            tmp = A.f32(L)
            r_tmp, r_d = Res(), Res()
            for c in range(8):
                op("dve", "tensor_copy", tmp, hT3[:, c, :], r=[r_hT], w=[r_tmp])
                P.dma("sp", s_st, (lambda o_, i_: (lambda e: e.dma_start(out=o_, in_=i_)))(dbg_out[:, c, :], tmp), [r_tmp], [r_d])
            P.wait_for("sp", [r_d])
        if r_y is not None:
            P.wait_for("sp", [r_y])
        P.barrier()
        P.emit(st)
    return nc


def make_consts():
    c = np.zeros((128, NCST), np.float32)
    i = np.arange(128)
    c[:, 0:128] = np.eye(128)
    sw = np.where((i % 64) < 32, i + 32, i - 32)
    c[sw, 128 + i] = 1.0
    c[:, 256:384] = 1.0
    same = (i[:, None] // 64) == (i[None, :] // 64)
    c[:, 384:512] = ((i[None, :] > i[:, None]) & same)
    c[:, 512:640] = ((i[None, :] >= i[:, None]) & same)
    c[:, 640:768] = ((i[:, None] > i[None, :]) & same)
    c[:, 768:896] = (i[None, :] >= i[:, None])
    c[:, 896:1024] = (i[:, None] < 64)
    c[:, 1024:1152] = (i[:, None] >= 64)
    invf = (10000.0 ** (-np.arange(0, 64, 2, dtype=np.float32) / 64)).astype(np.float32)
    c[:, 1152] = np.tile(invf, 4)
    c[:, 1153] = np.where((i % 64) < 32, -1.0, 1.0)
    c[:, 1154] = 1e-6
    c[:, 1155] = 1e-5
    return c


_CACHE = {}


def _prep_common(inputs):
    f = lambda k: np.ascontiguousarray(np.asarray(inputs[k], dtype=np.float32))
    com = {k: f(k) for k in ("norm_w", "w_in", "attn_subln_w", "w_attn_out", "conv_dw_w", "w_conv_out", "dn_conv_w",
                             "dn_a_log", "dn_dt_bias", "dn_norm_w", "w_dn_out", "w_out", "final_norm_w")}
    com["lam_qk"] = f("lam_qk").reshape(DEPTH, 256)
    cv = np.stack([f("conv_dw_b").reshape(DEPTH, 8, 128), f("conv_ln_w").reshape(DEPTH, 8, 128),
                   f("conv_ln_b").reshape(DEPTH, 8, 128)], axis=1).reshape(DEPTH, 24, 128)
    com["conv_vecs"] = np.ascontiguousarray(cv)
    com["cst"] = make_consts()
    return com


def kernel(**inputs):
    x = np.asarray(inputs["x"], dtype=np.float32)
    pos = np.asarray(inputs["positions"], dtype=np.int32)
    B = x.shape[0]
    com = _prep_common(inputs)
    if "nc" not in _CACHE:
        _CACHE["nc"] = build()
    nc = _CACHE["nc"]
    in_maps = []
    for b in range(B):
        m = dict(com)
        m["x"] = np.ascontiguousarray(x[b])
        m["positions"] = np.ascontiguousarray(pos[b])
        in_maps.append(m)
    res = run_bass_kernel_spmd(nc, in_maps, core_ids=list(range(B)))
    return np.stack([np.asarray(r["y"], dtype=np.float32) for r in res.results], axis=0)
```

```python
import contextlib
import math
import os
import numpy as np
import concourse.bass as bass
import concourse.mybir as mybir
from concourse.bass_utils import run_bass_kernel_spmd

F32 = mybir.dt.float32
BF16 = mybir.dt.bfloat16
I32 = mybir.dt.int32
AF = mybir.ActivationFunctionType
ALU = mybir.AluOpType
AX = mybir.AxisListType

L = 2048
D = 1024
NT = 16
DEPTH = 2
INW = 14352
OFF = dict(aq=0, ak=1024, av=2048, az=3072, ca=4096, cg=5120, cz=6144, dq=7168, dk=8192, dv=9216,
           dz=10240, db=11264, da=11272, ga=11280, gc=12304, gd=13328)
NCST = 1160
ARENA_F32 = 53200


class Res:
    __slots__ = ("w", "r", "excl")

    def __init__(self, excl=False):
        self.w = None
        self.r = {}
        self.excl = excl


class Prog:
    ENGS = ("pe", "act", "dve", "pool", "sp")
    ENGOBJ = {"pe": "tensor", "act": "scalar", "dve": "vector", "pool": "gpsimd", "sp": "sync"}

    def __init__(self, nc):
        self.nc = nc
        self.q = {e: [] for e in self.ENGS}
        self.cnt = {e: 0 for e in self.ENGS}
        self.seen = {e: {} for e in self.ENGS}
        self.slots = []

    def slot(self, name):
        self.cnt[name] = 0
        self.slots.append(name)
        return name

    def _deps(self, eng, reads, writes):
        waits = {}

        def need(dep):
            if dep is None:
                return
            c, s = dep
            if c == eng and eng == "pe":
                return
            if c in self.slots:
                s = self.cnt[c]
            if self.seen[eng].get(c, 0) < s:
                waits[c] = max(waits.get(c, 0), s)
        for r in reads:
            need(r.w)
        for w in writes:
            need(w.w)
            for c, s in w.r.items():
                need((c, s))
        for c, s in waits.items():
            self.seen[eng][c] = s
        return waits

    def _mark(self, counter, seq, reads, writes):
        for r in reads:
            r.r[counter] = max(r.r.get(counter, 0), seq)
        for w in writes:
            w.w = (counter, seq)
            w.r = {}

    def op(self, eng, fn, reads=(), writes=(), sig=True):
        ex = [r for r in reads if r.excl]
        if ex:
            reads = [r for r in reads if not r.excl]
            writes = list(writes) + ex
        waits = self._deps(eng, reads, writes)
        seq = self.cnt[eng] + 1
        if sig:
            self.cnt[eng] = seq
        self.q[eng].append((waits, fn, eng if sig else None, 1))
        self._mark(eng, seq, reads, writes)

    def dma(self, queue, slot, fn, reads=(), writes=()):
        waits = self._deps(queue, reads, writes)
        self.cnt[slot] += 1
        self.q[queue].append((waits, fn, slot, 16))
        self._mark(slot, self.cnt[slot], reads, writes)

    def wait_for(self, eng, resources):
        waits = self._deps(eng, resources, ())
        self.q[eng].append((waits, None, None, 0))

    def barrier(self):
        for e in self.ENGS:
            waits = {}
            for c, v in self.cnt.items():
                if c == e:
                    continue
                if self.seen[e].get(c, 0) < v:
                    waits[c] = v
                    self.seen[e][c] = v
            self.q[e].append((waits, None, None, 0))

    def emit(self, st):
        nc = self.nc
        sems = {c: st.enter_context(nc.semaphore("s_" + c)) for c in self.cnt}
        mult = {c: (16 if c in self.slots else 1) for c in self.cnt}
        with nc.Block() as block:
            def mk(ename):
                def body(eng):
                    for (waits, fn, inc, amt) in self.q[ename]:
                        for c, s in waits.items():
                            eng.wait_ge(sems[c], s * mult[c])
                        if fn is not None:
                            ins = fn(eng)
                            if inc is not None:
                                ins.then_inc(sems[inc], amt)
                return body
            for ename in self.ENGS:
                getattr(block, self.ENGOBJ[ename])(mk(ename))


class Arena:
    def __init__(self, nc, st, n, t=None, lo=0):
        self.t = t if t is not None else st.enter_context(nc.sbuf_tensor("arena", [128, n], F32))
        self.p = lo
        self.n = lo + n
        self.hi = 0

    def f32(self, n):
        ap = self.t[:, self.p:self.p + n]
        self.p += n
        self.hi = max(self.hi, self.p)
        assert self.p <= self.n, ("arena overflow", self.p, self.n)
        return ap

    def bf(self, n):
        nf = (n + 1) // 2
        return self.f32(nf).bitcast(BF16)

    def i32(self, n):
        return self.f32(n).bitcast(I32)


def v3(ap, a):
    return ap.rearrange("p (a b) -> p a b", a=a)


def build(dbg=None, depth=DEPTH, branches="DAC"):
    nc = bass.Bass("TRN2", target_bir_lowering=False)
    dram = {}

    def din(name, shape, dt=F32):
        dram[name] = nc.dram_tensor(name, list(shape), dt, kind="ExternalInput").ap()
        return dram[name]
    x_in = din("x", [L, D])
    pos_in = din("positions", [L], I32)
    norm_w = din("norm_w", [DEPTH, D])
    w_in = din("w_in", [DEPTH, D, INW])
    lam_qk = din("lam_qk", [DEPTH, 256])
    attn_subln_w = din("attn_subln_w", [DEPTH, 128])
    w_attn_out = din("w_attn_out", [DEPTH, D, D])
    conv_dw_w = din("conv_dw_w", [DEPTH, 31, D])
    conv_vecs = din("conv_vecs", [DEPTH, 24, 128])
    w_conv_out = din("w_conv_out", [DEPTH, D, D])
    dn_conv_w = din("dn_conv_w", [DEPTH, 4, 3072])
    dn_a_log = din("dn_a_log", [DEPTH, 8])
    dn_dt_bias = din("dn_dt_bias", [DEPTH, 8])
    dn_norm_w = din("dn_norm_w", [DEPTH, 128])
    w_dn_out = din("w_dn_out", [DEPTH, D, D])
    w_out = din("w_out", [DEPTH, D, D])
    final_norm_w = din("final_norm_w", [D])
    cst_in = din("cst", [128, NCST])
    y_out = nc.dram_tensor("y", [L, D], F32, kind="ExternalOutput").ap()
    xs = nc.dram_tensor("xs", [L, D], F32, kind="Internal").ap()
    dbg_out = None
    if dbg:
        dbg_out = nc.dram_tensor("dbg", [128, 8, L], F32, kind="ExternalOutput").ap()

    P = Prog(nc)
    st = contextlib.ExitStack()
    with st:
        A = Arena(nc, st, ARENA_F32)
        pst = [st.enter_context(nc.psum_tensor(f"ps{i}", [128, 1024], F32)) for i in range(4)]
        banks = []
        for i in range(4):
            for hh in range(2):
                banks.append((pst[i][:, hh * 512:(hh + 1) * 512], Res(excl=True)))
        rot = {"list": list(range(8)), "i": 0}

        def set_rot(lst):
            rot["list"] = lst
            rot["i"] = 0

        def bank():
            b = banks[rot["list"][rot["i"] % len(rot["list"])]]
            rot["i"] += 1
            return b

        s_ld = P.slot("d_ld")
        s_st = P.slot("d_st")
        s_w = [P.slot("d_w0"), P.slot("d_w1")]
        s_misc = P.slot("d_misc")
        wslot = {"i": 0}

        def op(eng, name, *args, r=(), w=(), **kw):
            P.op(eng, lambda e: getattr(e, name)(*args, **kw), r, w)

        def mm(out, lhsT, rhs, start, stop, r, w):
            P.op("pe", lambda e: e.matmul(out, lhsT=lhsT, rhs=rhs, start=start, stop=stop), r, w, sig=True)

        def tr(out, in_, ident, r, w):
            P.op("pe", lambda e: e.transpose(out, in_, ident), r, w)

        def ld(out, in_, w, r=(), slot=None):
            P.dma("sp", slot or s_ld, lambda e: e.dma_start(out=out, in_=in_), r, w)

        def ldw(out, in_, w, r=()):
            s = s_w[0]
            P.dma("pool", s, lambda e: e.dma_start(out=out, in_=in_), r, w)

        def sigmoid_act(dst, src_, r_src, r_dst):
            op("act", "activation", dst, src_, AF.Exp, scale=-1.0, r=list(r_src), w=[r_dst])
            op("act", "activation", dst, dst, AF.Ln, bias=cst[:, 256:257], scale=1.0, r=[r_dst, r_cst], w=[r_dst])
            op("act", "activation", dst, dst, AF.Exp, scale=-1.0, r=[r_dst], w=[r_dst])

        def rsqrt_act(dst, src_, scale, eps_ap, r_src, r_dst):
            op("act", "activation", dst, src_, AF.Ln, bias=eps_ap, scale=scale, r=list(r_src) + [r_cst], w=[r_dst])
            op("act", "activation", dst, dst, AF.Exp, scale=-0.5, r=[r_dst], w=[r_dst])

        def bc(ap, axis, shape):
            return ap.unsqueeze(axis).broadcast_to(list(shape))

        def w_in_cols(l, c0, n):
            return w_in[l].rearrange("(kc p) c -> p kc c", p=128)[:, :, c0:c0 + n]

        def wsq_cols(wt, l, c0, n):
            return wt[l].rearrange("(kc p) c -> p kc c", p=128)[:, :, c0:c0 + n]

        cst = A.f32(NCST)
        r_cst = Res()
        ld(cst, cst_in[:, :], [r_cst], slot=s_misc)
        cb = A.bf(1152)
        r_cb = Res()
        op("dve", "tensor_copy", cb, cst[:, 0:1152], r=[r_cst], w=[r_cb])
        ident_b, pswap_b, ones_b = cb[:, 0:128], cb[:, 128:256], cb[:, 256:384]
        MU_b, MUI_b, CAUS_b = cb[:, 384:512], cb[:, 512:640], cb[:, 768:896]
        ident_f, ones_f = cst[:, 0:128], cst[:, 256:384]
        MU_f, MUI_f, SL_f = cst[:, 384:512], cst[:, 512:640], cst[:, 640:768]
        CI0_f, CI1_f = cst[:, 896:1024], cst[:, 1024:1152]
        invf, sgn, eps6, eps5 = cst[:, 1152:1153], cst[:, 1153:1154], cst[:, 1154:1155], cst[:, 1155:1156]
        one_col = cst[:, 256:257]
        hT = A.bf(8 * L)
        hT3 = v3(hT, 8)
        r_hT = Res()
        mg_off = A.p
        mg = A.bf(8 * L)
        mg3 = v3(mg, 8)
        r_mg = Res()
        stat = A.f32(64)
        r_stat = Res()
        pmark = A.p

        def rope_tables():
            cosT = A.f32(L)
            sinT = A.f32(L)
            r_cos, r_sin = Res(), Res()
            tmark = A.p
            pi_ = A.i32(L)
            pf = A.f32(L)
            kf = A.f32(L)
            yv = A.f32(L)
            m1 = A.f32(L)
            ki = m1.bitcast(I32)
            r = Res()
            ld(pi_, pos_in.partition_broadcast(128), [r], slot=s_misc)
            op("dve", "tensor_copy", pf, pi_, r=[r], w=[r])
            op("dve", "tensor_scalar", pf, pf, invf, None, op0=ALU.mult, r=[r, r_cst], w=[r])
            op("dve", "tensor_scalar", kf, pf, 1.0 / (2 * math.pi), None, op0=ALU.mult, r=[r], w=[r])
            op("dve", "tensor_copy", ki, kf, r=[r], w=[r])
            op("dve", "tensor_copy", kf, ki, r=[r], w=[r])
            c1 = 6.28125
            c2 = 2 * math.pi - c1
            op("dve", "scalar_tensor_tensor", yv, kf, -c1, pf, op0=ALU.mult, op1=ALU.add, r=[r], w=[r])
            op("dve", "scalar_tensor_tensor", yv, kf, -c2, yv, op0=ALU.mult, op1=ALU.add, r=[r], w=[r])
            op("dve", "tensor_scalar", m1, yv, -math.pi, 2 * math.pi, op0=ALU.is_lt, op1=ALU.mult, r=[r], w=[r])
            op("dve", "tensor_tensor", yv, yv, m1, op=ALU.add, r=[r], w=[r])
            op("dve", "tensor_scalar", m1, yv, math.pi, 2 * math.pi, op0=ALU.is_gt, op1=ALU.mult, r=[r], w=[r])
            op("dve", "tensor_tensor", yv, yv, m1, op=ALU.subtract, r=[r], w=[r])
            op("act", "activation", sinT, yv, AF.Sin, r=[r], w=[r_sin])
            op("dve", "tensor_scalar", sinT, sinT, sgn, None, op0=ALU.mult, r=[r_sin, r_cst], w=[r_sin])
            op("dve", "tensor_scalar", yv, yv, math.pi / 2, None, op0=ALU.add, r=[r], w=[r])
            op("dve", "tensor_scalar", m1, yv, math.pi, 2 * math.pi, op0=ALU.is_gt, op1=ALU.mult, r=[r], w=[r])
            op("dve", "tensor_tensor", yv, yv, m1, op=ALU.subtract, r=[r], w=[r])
            op("act", "activation", cosT, yv, AF.Sin, r=[r], w=[r_cos])
            P.barrier()
            A.p = tmark
            return cosT, sinT, r_cos, r_sin

        def norm_to_hT(xt, r_xt, wbc, r_wbc, t, hn, r_hn, junk, r_junk, sc):
            ssq = stat[:, sc:sc + 1]
            rs = stat[:, sc + 1:sc + 2]
            op("dve", "memset", ssq, 0.0, w=[r_stat])
            op("act", "activation", junk, xt, AF.Square, accum_out=ssq, r=[r_xt, r_stat], w=[r_junk, r_stat])
            rsqrt_act(rs, ssq, 1.0 / D, eps6, [r_stat], r_stat)
            op("dve", "scalar_tensor_tensor", hn, xt, rs, wbc, op0=ALU.mult, op1=ALU.mult,
               r=[r_xt, r_stat, r_wbc], w=[r_hn])

        def hn_to_hT(hn, r_hn, t):
            pb, r_pb = bank()
            pbb = pb.bitcast(BF16)
            for kc in range(8):
                tr(pbb[:, kc * 128:(kc + 1) * 128], hn[:, kc * 128:(kc + 1) * 128], ident_b, [r_hn, r_cb], [r_pb])
            op("act", "copy", hT3[:, :, t * 128:(t + 1) * 128], v3(pbb, 8), r=[r_pb], w=[r_hT])

        def phase0(l):
            wbc = A.f32(D)
            r_wbc = Res()
            ld(wbc, norm_w[l].partition_broadcast(128), [r_wbc], slot=s_misc)
            xts = [(A.f32(D), Res()) for _ in range(2)]
            hns = [(A.bf(D), Res()) for _ in range(2)]
            junk = A.bf(D)
            r_junk = Res()
            for t in range(NT):
                xt, r_xt = xts[t % 2]
                hn, r_hn = hns[t % 2]
                ld(xt, x_in[t * 128:(t + 1) * 128, :], [r_xt])
                norm_to_hT(xt, r_xt, wbc, r_wbc, t, hn, r_hn, junk, r_junk, 2 * (t % 2))
                hn_to_hT(hn, r_hn, t)
            P.barrier()
            A.p = pmark

        def finish(act3, r_act, w_bo, l, goff, first):
            set_rot(list(range(8)))
            wbs = [(A.bf(8 * 128), Res()) for _ in range(2)]
            wgs = [(A.bf(8 * 128), Res()) for _ in range(2)]
            gss = [(A.f32(512), Res()) for _ in range(2)]
            tms = [(A.f32(512), Res()) for _ in range(2)]
            it = 0
            for oc in range(8):
                wb, r_wb = wbs[oc % 2]
                wg, r_wg = wgs[oc % 2]
                wb3, wg3 = v3(wb, 8), v3(wg, 8)
                ldw(wb3, wsq_cols(w_bo, l, oc * 128, 128), [r_wb])
                ldw(wg3, w_in_cols(l, goff + oc * 128, 128), [r_wg])
                for tc in range(4):
                    cs = slice(tc * 512, (tc + 1) * 512)
                    py, r_py = bank()
                    for c in range(8):
                        mm(py, wb3[:, c, :], act3[:, c, cs], c == 0, c == 7, [r_wb, r_act], [r_py])
                    pg, r_pg = bank()
                    for c in range(8):
                        mm(pg, wg3[:, c, :], hT3[:, c, cs], c == 0, c == 7, [r_wg, r_hT], [r_pg])
                    gs, r_gs = gss[it % 2]
                    tm, r_tm = tms[it % 2]
                    it += 1
                    sigmoid_act(gs, pg, [r_pg], r_gs)
                    if first:
                        op("dve", "tensor_tensor", mg3[:, oc, cs], py, gs, op=ALU.mult, r=[r_py, r_gs], w=[r_mg])
                    else:
                        op("dve", "tensor_tensor", tm, py, gs, op=ALU.mult, r=[r_py, r_gs], w=[r_tm])
                        op("dve", "tensor_tensor", mg3[:, oc, cs], mg3[:, oc, cs], tm, op=ALU.add,
                           r=[r_tm, r_mg], w=[r_mg])
            P.barrier()
            A.p = pmark

        def branch_A(l, first):
            li = 0.8 - 0.6 * math.exp(-0.3 * l)
            ao = A.bf(8 * L)
            ao3 = v3(ao, 8)
            r_ao = Res()
            amark = A.p
            cosT, sinT, r_cos, r_sin = rope_tables()
            lq = A.f32(256)
            r_lq = Res()
            ld(lq, lam_qk[l].partition_broadcast(128), [r_lq], slot=s_misc)
            sm = A.f32(8)
            r_sm = Res()
            pr = A.f32(64)
            op("dve", "tensor_tensor", pr, lq[:, 0:64], lq[:, 64:128], op=ALU.mult, r=[r_lq], w=[r_sm])
            op("dve", "reduce_sum", sm[:, 0:1], pr, axis=AX.X, r=[r_sm], w=[r_sm])
            op("dve", "tensor_tensor", pr, lq[:, 128:192], lq[:, 192:256], op=ALU.mult, r=[r_lq, r_sm], w=[r_sm])
            op("dve", "reduce_sum", sm[:, 1:2], pr, axis=AX.X, r=[r_sm], w=[r_sm])
            op("act", "activation", sm[:, 2:4], sm[:, 0:2], AF.Exp, r=[r_sm], w=[r_sm])
            op("dve", "tensor_tensor", sm[:, 4:5], sm[:, 3:4], sm[:, 2:3], op=ALU.subtract, r=[r_sm], w=[r_sm])
            op("dve", "tensor_scalar", sm[:, 4:5], sm[:, 4:5], -li, None, op0=ALU.add, r=[r_sm], w=[r_sm])
            nlam = sm[:, 4:5]
            wbc_ = A.f32(128)
            r_wbc_ = Res()
            ld(wbc_, attn_subln_w[l].partition_broadcast(128), [r_wbc_], slot=s_misc)
            op("dve", "tensor_tensor", wbc_, wbc_, ident_f, op=ALU.mult, r=[r_wbc_, r_cst], w=[r_wbc_])
            op("dve", "reduce_sum", sm[:, 5:6], wbc_, axis=AX.X, r=[r_wbc_, r_sm], w=[r_sm])
            op("dve", "tensor_scalar", sm[:, 6:7], sm[:, 5:6], 1.0 - li, None, op0=ALU.mult, r=[r_sm], w=[r_sm])
            wcol = sm[:, 6:7]
            STGA = int(os.environ.get("STGA", "9"))
            if STGA < 2:
                P.barrier(); A.p = pmark; return
            W4s = [(A.bf(8 * 512), Res()) for _ in range(2)]
            qTs = [(A.bf(L), Res()) for _ in range(2)]
            kTs = [(A.bf(L), Res()) for _ in range(2)]
            vhs = [(A.bf(L), Res()) for _ in range(2)]
            szs = [(A.bf(L), Res()) for _ in range(2)]
            raws = [(A.bf(512), Res()) for _ in range(2)]
            t1s = [(A.f32(512), Res()) for _ in range(2)]
            t2s = [(A.f32(512), Res()) for _ in range(2)]
            ebig = A.bf(2048)
            ebs = [(ebig[:, i * 512:(i + 1) * 512], Res()) for i in range(4)]
            ob = [(A.f32(512), Res()) for _ in range(4)]
            sqb = (A.bf(512), Res())
            O_b = [banks[0], banks[1]]
            S_b = [banks[2], banks[3]]
            sc_b = [[banks[4], banks[5]], [banks[6], banks[7]]]
            set_rot([4, 5, 6, 7])
            cnt = {"raw": 0, "e": 0}
            for h in range(8):
                W4, r_W4 = W4s[h % 2]
                W44 = W4.rearrange("p (k s c) -> p k s c", k=8, s=4)
                for s, nm in enumerate(("aq", "ak", "av", "az")):
                    ldw(W44[:, :, s, :], w_in_cols(l, OFF[nm] + h * 128, 128), [r_W4])
                qT, r_qT = qTs[h % 2]
                kT, r_kT = kTs[h % 2]
                vh, r_vh = vhs[h % 2]
                vh3 = v3(vh, 16)
                sz, r_sz = szs[h % 2]
                for s, (dst, r_dst) in enumerate(((qT, r_qT), (kT, r_kT))):
                    for tc in range(4):
                        cs = slice(tc * 512, (tc + 1) * 512)
                        pq, r_pq = bank()
                        for kc in range(8):
                            mm(pq, W44[:, kc, s, :], hT3[:, kc, cs], kc == 0, kc == 7, [r_W4, r_hT], [r_pq])
                        raw, r_raw = raws[cnt["raw"] % 2]
                        t1, r_t1 = t1s[cnt["raw"] % 2]
                        t2, r_t2 = t2s[cnt["raw"] % 2]
                        cnt["raw"] += 1
                        op("act", "copy", raw, pq, r=[r_pq], w=[r_raw])
                        op("dve", "tensor_tensor", t1, pq, cosT[:, cs], op=ALU.mult, r=[r_pq, r_cos], w=[r_t1])
                        p2, r_p2 = bank()
                        mm(p2, pswap_b, raw, True, True, [r_cb, r_raw], [r_p2])
                        op("dve", "tensor_tensor", t2, p2, sinT[:, cs], op=ALU.mult, r=[r_p2, r_sin], w=[r_t2])
                        op("dve", "tensor_tensor", dst[:, cs], t1, t2, op=ALU.add, r=[r_t1, r_t2], w=[r_dst])
                for g4 in range(4 if STGA >= 3 else 0):
                    pv, r_pv = bank()
                    pv3 = v3(pv, 4)
                    for tt in range(4):
                        t = g4 * 4 + tt
                        for kc in range(8):
                            mm(pv3[:, tt, :], hT3[:, kc, t * 128:(t + 1) * 128], W44[:, kc, 2, :], kc == 0, kc == 7,
                               [r_W4, r_hT], [r_pv])
                    op("act", "copy", vh3[:, g4 * 4:(g4 + 1) * 4, :], pv3, r=[r_pv], w=[r_vh])
                for tc in range(4 if STGA >= 3 else 0):
                    cs = slice(tc * 512, (tc + 1) * 512)
                    pz, r_pz = bank()
                    for kc in range(8):
                        mm(pz, W44[:, kc, 3, :], hT3[:, kc, cs], kc == 0, kc == 7, [r_W4, r_hT], [r_pz])
                    t1, r_t1 = t1s[tc % 2]
                    sigmoid_act(t1, pz, [r_pz], r_t1)
                    op("dve", "tensor_tensor", sz[:, cs], pz, t1, op=ALU.mult, r=[r_pz, r_t1], w=[r_sz])
                for qc in range(4 if STGA >= 4 else 0):
                    nk = 4 * qc + 4
                    qs0 = qc * 512
                    def c0_of(kt):
                        return max(kt - 4 * qc, 0) * 128

                    def emit_scores(kt):
                        c0 = c0_of(kt)
                        for m in range(2):
                            ps_, r_ps = sc_b[kt % 2][m]
                            rows = slice(m * 64, (m + 1) * 64)
                            mm(ps_[:, c0:512], kT[rows, kt * 128:(kt + 1) * 128], qT[rows, qs0 + c0:qs0 + 512],
                               True, True, [r_kT, r_qT], [r_ps])

                    def emit_exp(kt):
                        c0 = c0_of(kt)
                        par = kt % 2
                        pboth = v3(pst[2 + par][:, :], 2)[:, :, c0:512]
                        eboth = v3(ebig[:, par * 1024:(par + 1) * 1024], 2)
                        r_pss_ = [sc_b[par][0][1], sc_b[par][1][1]]
                        r_ebs_ = [ebs[par * 2][1], ebs[par * 2 + 1][1]]
                        op("act", "activation", eboth[:, :, c0:512], pboth, AF.Exp, scale=0.125, r=r_pss_, w=r_ebs_)
                        if kt - 4 * qc >= 0:
                            op("dve", "tensor_tensor", eboth[:, :, c0:c0 + 128], eboth[:, :, c0:c0 + 128],
                               bc(CAUS_b, 1, [128, 2, 128]), op=ALU.mult, r=r_ebs_ + [r_cb], w=r_ebs_)

                    def emit_pv(kt):
                        c0 = c0_of(kt)
                        for m in range(2):
                            eb, r_eb = ebs[(kt % 2) * 2 + m]
                            mm(O_b[m][0][:, c0:512], vh3[:, kt, :], eb[:, c0:512], kt == 0, kt == nk - 1,
                               [r_vh, r_eb], [O_b[m][1]])
                            mm(S_b[m][0][:, c0:512], ones_b, eb[:, c0:512], kt == 0, kt == nk - 1,
                               [r_cb, r_eb], [S_b[m][1]])
                    emit_scores(0)
                    for kt in range(nk):
                        if kt + 1 < nk:
                            emit_scores(kt + 1)
                        emit_exp(kt)
                        emit_pv(kt)
                    if STGA < 5:
                        continue
                    cs = slice(qs0, qs0 + 512)
                    (o0, r_o0), (o1, r_o1), (rc, r_rc), (rc1, r_rc1) = ob
                    op("dve", "tensor_copy", o0, O_b[0][0], r=[O_b[0][1]], w=[r_o0])
                    op("act", "activation", rc, S_b[0][0], AF.Ln, r=[S_b[0][1]], w=[r_rc])
                    op("dve", "tensor_copy", o1, O_b[1][0], r=[O_b[1][1]], w=[r_o1])
                    op("act", "activation", rc1, S_b[1][0], AF.Ln, r=[S_b[1][1]], w=[r_rc1])
                    op("act", "activation", rc, rc, AF.Exp, scale=-1.0, r=[r_rc], w=[r_rc])
                    op("act", "activation", rc1, rc1, AF.Exp, scale=-1.0, r=[r_rc1], w=[r_rc1])
                    op("dve", "tensor_tensor", o0, o0, rc, op=ALU.mult, r=[r_o0, r_rc], w=[r_o0])
                    op("dve", "tensor_tensor", o1, o1, rc1, op=ALU.mult, r=[r_o1, r_rc1], w=[r_o1])
                    op("dve", "scalar_tensor_tensor", o0, o1, nlam, o0, op0=ALU.mult, op1=ALU.add,
                       r=[r_o0, r_o1, r_sm], w=[r_o0])
                    op("act", "activation", sqb[0], o0, AF.Square, r=[r_o0], w=[sqb[1]])
                    pss, r_pss = bank()
                    mm(pss, ones_b, sqb[0], True, True, [r_cb, sqb[1]], [r_pss])
                    rsqrt_act(rc, pss, 1.0 / 128, eps6, [r_pss], r_rc)
                    op("dve", "tensor_tensor", o0, o0, rc, op=ALU.mult, r=[r_o0, r_rc], w=[r_o0])
                    op("dve", "scalar_tensor_tensor", ao3[:, h, cs], o0, wcol, sz[:, cs], op0=ALU.mult, op1=ALU.mult,
                       r=[r_o0, r_sm, r_sz], w=[r_ao])
            P.barrier()
            A.p = amark
            if STGA < 6:
                A.p = pmark; return
            finish(ao3, r_ao, w_attn_out, l, OFF["ga"], first)
            A.p = pmark

        def branch_C(l, first):
            co = A.bf(8 * L)
            co3 = v3(co, 8)
            r_co = Res()
            r_cos = [[Res() for _tc in range(4)] for _c in range(8)]
            mean = A.f32(L)
            rstd = A.f32(L)
            r_mr = Res()
            cmark = A.p
            set_rot(list(range(8)))
            cwr = A.f32(D)
            r_cwr = Res()
            ld(cwr[0:31, :], conv_dw_w[l], [r_cwr], slot=s_misc)
            cvr = A.f32(128)
            r_cvr = Res()
            ld(cvr[0:24, :], conv_vecs[l], [r_cvr], slot=s_misc)
            cwT = A.f32(8 * 32)
            cwT3 = v3(cwT, 8)
            r_cwT = Res()
            cv = A.f32(24)
            r_cv = Res()
            pb, r_pb = bank()
            for c in range(8):
                tr(pb[:, c * 32:c * 32 + 31], cwr[0:31, c * 128:(c + 1) * 128], ident_f[0:31, 0:31], [r_cwr, r_cst], [r_pb])
            op("dve", "tensor_copy", v3(cwT, 8)[:, :, 0:31], v3(pb[:, 0:256], 8)[:, :, 0:31], r=[r_pb], w=[r_cwT])
            pb2, r_pb2 = bank()
            tr(pb2[:, 0:24], cvr[0:24, :], ident_f[0:24, 0:24], [r_cvr, r_cst], [r_pb2])
            op("dve", "tensor_copy", cv, pb2[:, 0:24], r=[r_pb2], w=[r_cv])
            STG = int(os.environ.get("STG", "9"))
            if STG < 2:
                P.barrier(); A.p = pmark; return
            Wags = [(A.bf(8 * 256), Res()) for _ in range(2)]
            dgs = [(A.bf(31 * 128), Res()) for _ in range(2)]
            ups = [(A.bf(30 + L + 2), Res()) for _ in range(2)]
            sgs = [(A.f32(512), Res()) for _ in range(2)]
            for (u, r_u) in ups:
                op("pool", "memset", u[:, 0:30], 0.0, w=[r_u])
            itc = {"i": 0}
            cstate = {}

            def c1_setup(c):
                Wag, r_Wag = Wags[c % 2]
                Wag4 = Wag.rearrange("p (k s c) -> p k s c", k=8, s=2)
                ldw(Wag4[:, :, 0, :], w_in_cols(l, OFF["ca"] + c * 128, 128), [r_Wag])
                ldw(Wag4[:, :, 1, :], w_in_cols(l, OFF["cg"] + c * 128, 128), [r_Wag])
                dg, r_dg = dgs[c % 2]
                dg3 = v3(dg, 31)
                op("dve", "tensor_tensor", dg3, bc(ident_b, 1, [128, 31, 128]), bc(cwT3[:, c, 0:31], 2, [128, 31, 128]),
                   op=ALU.mult, r=[r_cb, r_cwT], w=[r_dg])
                cstate[c] = (Wag4, r_Wag, dg3, r_dg, ups[c % 2])

            def c1_proj(c, tc):
                Wag4, r_Wag, dg3, r_dg, (up, r_up) = cstate[c]
                cs = slice(tc * 512, (tc + 1) * 512)
                pa, r_pa = bank()
                for kc in range(8):
                    mm(pa, Wag4[:, kc, 0, :], hT3[:, kc, cs], kc == 0, kc == 7, [r_Wag, r_hT], [r_pa])
                pg, r_pg = bank()
                for kc in range(8):
                    mm(pg, Wag4[:, kc, 1, :], hT3[:, kc, cs], kc == 0, kc == 7, [r_Wag, r_hT], [r_pg])
                sg, r_sg = sgs[itc["i"] % 2]
                itc["i"] += 1
                sigmoid_act(sg, pg, [r_pg], r_sg)
                op("dve", "tensor_tensor", up[:, 30 + tc * 512:30 + (tc + 1) * 512], pa, sg, op=ALU.mult,
                   r=[r_pa, r_sg], w=[r_up])

            def c1_conv(c, tc):
                Wag4, r_Wag, dg3, r_dg, (up, r_up) = cstate[c]
                cs = slice(tc * 512, (tc + 1) * 512)
                pc, r_pc = bank()
                for j in range(31):
                    mm(pc, dg3[:, j, :], up[:, tc * 512 + j:tc * 512 + j + 512], j == 0, j == 30, [r_dg, r_up], [r_pc])
                op("dve", "tensor_scalar", co3[:, c, cs], pc, cv[:, c:c + 1], None, op0=ALU.add,
                   r=[r_pc, r_cv], w=[r_cos[c][tc]])

            for c in range(9):
                if c < 8:
                    c1_setup(c)
                for tc in range(4):
                    if c < 8:
                        c1_proj(c, tc)
                    if c >= 1:
                        c1_conv(c - 1, tc)
            if STG < 3:
                P.barrier(); A.p = pmark; return
            sqs = [(A.bf(512), Res()) for _ in range(2)]
            m2 = (A.f32(512), Res())
            it = 0
            for tc in range(4):
                cs = slice(tc * 512, (tc + 1) * 512)
                psm, r_psm = bank()
                psq, r_psq = bank()
                for c in range(8):
                    mm(psm, ones_b, co3[:, c, cs], c == 0, c == 7, [r_cb, r_cos[c][tc]], [r_psm])
                for c in range(8):
                    sq, r_sq = sqs[it % 2]
                    it += 1
                    op("act", "activation", sq, co3[:, c, cs], AF.Square, r=[r_cos[c][tc]], w=[r_sq])
                    mm(psq, ones_b, sq, c == 0, c == 7, [r_cb, r_sq], [r_psq])
                op("dve", "tensor_scalar", mean[:, cs], psm, 1.0 / D, None, op0=ALU.mult, r=[r_psm], w=[r_mr])
                op("dve", "tensor_tensor", m2[0], mean[:, cs], mean[:, cs], op=ALU.mult, r=[r_mr], w=[m2[1]])
                op("dve", "scalar_tensor_tensor", m2[0], psq, 1.0 / D, m2[0], op0=ALU.mult, op1=ALU.subtract,
                   r=[r_psq, m2[1]], w=[m2[1]])
                rsqrt_act(rstd[:, cs], m2[0], 1.0, eps5, [m2[1]], r_mr)
            if STG < 4:
                P.barrier(); A.p = pmark; return
            Wzs = [(A.bf(8 * 128), Res()) for _ in range(2)]
            szs = [(A.f32(512), Res()) for _ in range(2)]
            n1s = [(A.f32(512), Res()) for _ in range(2)]
            s1s = [(A.f32(512), Res()) for _ in range(2)]
            it = 0
            for c in range(8):
                Wz, r_Wz = Wzs[c % 2]
                Wz3 = v3(Wz, 8)
                ldw(Wz3, w_in_cols(l, OFF["cz"] + c * 128, 128), [r_Wz])
                for tc in range(4):
                    cs = slice(tc * 512, (tc + 1) * 512)
                    pz, r_pz = bank()
                    for kc in range(8):
                        mm(pz, Wz3[:, kc, :], hT3[:, kc, cs], kc == 0, kc == 7, [r_Wz, r_hT], [r_pz])
                    sz, r_sz = szs[it % 2]
                    n1, r_n1 = n1s[it % 2]
                    it += 1
                    s1, r_s1 = s1s[(it - 1) % 2]
                    sigmoid_act(sz, pz, [r_pz], r_sz)
                    op("dve", "tensor_tensor", sz, pz, sz, op=ALU.mult, r=[r_pz, r_sz], w=[r_sz])
                    op("dve", "tensor_tensor", n1, co3[:, c, cs], mean[:, cs], op=ALU.subtract, r=[r_cos[c][tc], r_mr], w=[r_n1])
                    op("dve", "tensor_tensor", n1, n1, rstd[:, cs], op=ALU.mult, r=[r_n1, r_mr], w=[r_n1])
                    op("dve", "tensor_scalar", n1, n1, cv[:, 8 + c:9 + c], cv[:, 16 + c:17 + c], op0=ALU.mult, op1=ALU.add,
                       r=[r_n1, r_cv], w=[r_n1])
                    sigmoid_act(s1, n1, [r_n1], r_s1)
                    op("dve", "tensor_tensor", n1, n1, s1, op=ALU.mult, r=[r_n1, r_s1], w=[r_n1])
                    op("pool", "tensor_tensor", co3[:, c, cs], n1, sz, op=ALU.mult, r=[r_n1, r_sz, r_cos[c][tc]], w=[r_cos[c][tc]])
            P.barrier()
            A.p = cmark
            if STG < 5:
                A.p = pmark; return
            finish(co3, Res(), w_conv_out, l, OFF["gc"], first)
            A.p = pmark

        def run_interleaved(*gens):
            gens = [g_ for g_ in gens if g_ is not None]
            while gens:
                for g_ in list(gens):
                    try:
                        next(g_)
                    except StopIteration:
                        gens.remove(g_)

        def branch_D(l, first):
            do = A.bf(8 * L)
            do3 = v3(do, 8)
            r_do = Res()
            dmark = A.p
            cwd = A.f32(24 * 4)
            cwd3 = v3(cwd, 24)
            r_cwd = Res()
            sv = A.f32(32)
            r_sv = Res()
            Wba = A.bf(8 * 16)
            Wba3 = v3(Wba, 8)
            r_Wba = Res()
            gmark = A.p
            cwr = A.f32(3072)
            r_cwr = Res()
            ld(cwr[0:4, :], dn_conv_w[l], [r_cwr], slot=s_misc)
            set_rot(list(range(8)))
            pb, r_pb = bank()
            for c in range(24):
                tr(pb[:, c * 4:c * 4 + 4], cwr[0:4, c * 128:(c + 1) * 128], ident_f[0:4, 0:4], [r_cwr, r_cst], [r_pb])
            op("dve", "tensor_copy", cwd, pb[:, 0:96], r=[r_pb], w=[r_cwd])
            ld(sv[:, 0:8], dn_a_log[l].partition_broadcast(128), [r_sv], slot=s_misc)
            ld(sv[:, 8:16], dn_dt_bias[l].partition_broadcast(128), [r_sv], r=[r_sv], slot=s_misc)
            wbc_ = A.f32(128)
            r_wbc_ = Res()
            ld(wbc_, dn_norm_w[l].partition_broadcast(128), [r_wbc_], slot=s_misc)
            op("dve", "tensor_tensor", wbc_, wbc_, ident_f, op=ALU.mult, r=[r_wbc_, r_cst], w=[r_wbc_])
            op("dve", "reduce_sum", sv[:, 16:17], wbc_, axis=AX.X, r=[r_wbc_, r_sv], w=[r_sv])
            op("act", "activation", sv[:, 0:8], sv[:, 0:8], AF.Exp, r=[r_sv], w=[r_sv])
            op("dve", "tensor_scalar", sv[:, 0:8], sv[:, 0:8], -1.0, None, op0=ALU.mult, r=[r_sv], w=[r_sv])
            nA, dtb, dnw = sv[:, 0:8], sv[:, 8:16], sv[:, 16:17]
            ldw(Wba3, w_in_cols(l, OFF["db"], 16), [r_Wba])
            P.barrier()
            for g in range(2):
                A.p = gmark
                set_rot([0, 1, 2, 3, 5])
                psO, r_psO = banks[4]
                psR = pst[3][:, :]
                r_psR = banks[6][1]
                Wd = A.bf(8 * 3 * 512)
                Wd4 = Wd.rearrange("p (k s c) -> p k s c", k=8, s=3)
                r_Wd = Res()
                for s, nm in enumerate(("dq", "dk", "dv")):
                    for k2 in range(2):
                        ldw(Wd4[:, k2 * 4:(k2 + 1) * 4, s, :],
                            w_in_cols(l, OFF[nm] + g * 512, 512)[:, k2 * 4:(k2 + 1) * 4, :], [r_Wd])
                Wz = A.bf(8 * 512)
                Wz3 = v3(Wz, 8)
                r_Wz = Res()
                for k2 in range(2):
                    ldw(Wz3[:, k2 * 4:(k2 + 1) * 4, :], w_in_cols(l, OFF["dz"] + g * 512, 512)[:, k2 * 4:(k2 + 1) * 4, :], [r_Wz])
                dgd = A.bf(12 * 4 * 128)
                dgd4 = dgd.rearrange("p (c j d) -> p c j d", c=12, j=4)
                r_dgd = Res()
                dgd3 = v3(dgd, 48)
                for s in range(3):
                    c0 = (s * 8 + g * 4) * 4
                    op("dve", "tensor_tensor", dgd3[:, s * 16:(s + 1) * 16, :], bc(ident_b, 1, [128, 16, 128]),
                       bc(cwd[:, c0:c0 + 16], 2, [128, 16, 128]), op=ALU.mult, r=[r_cb, r_cwd], w=[r_dgd])
                rawb = [[(A.bf(4 * 132), Res()) for s in range(3)] for par in range(2)]
                S = A.f32(512)
                S3 = v3(S, 4)
                r_S = Res()
                Sb = A.bf(512)
                Sb3 = v3(Sb, 4)
                r_Sb = Res()
                op("dve", "memset", S, 0.0, w=[r_S])
                op("dve", "memset", Sb, 0.0, w=[r_Sb])

                A2 = Arena(None, None, 4 * L, t=A.t, lo=mg_off)

                def T(n, dt=BF16):
                    nf = n if dt == F32 else (n + 1) // 2
                    AA = A if A.p + nf <= A.n else A2
                    return ((AA.bf(n) if dt == BF16 else AA.f32(n)), Res())
                qs_, ks_ = T(512, F32), T(512, F32)
                sqq, sqk = T(512), T(512)
                rnq, rnk = T(512, F32), T(512, F32)
                vsT_ = T(512)
                Lg_, Gb_ = T(512, F32), T(512, F32)
                FB = [dict(qT=T(512), kT=T(512), kt=T(512), E=T(512, F32), EG=T(512, F32)) for _ in range(2)]
                Mp_, IT_ = T(512, F32), T(512, F32)
                Mpp = [T(512) for _ in range(2)]
                App = [T(512) for _ in range(2)]
                Ms = [Mpp[k % 2] for k in range(6)]
                As = [App[k % 2] for k in range(5)]
                PB = [dict(rb=T(1024), wT=T(512), qg=T(512), it=T(512), kd=T(512), col=T(64, F32)) for _ in range(3)]
                vn_ = T(512)
                sz_s, on_s = T(512, F32), T(512, F32)
                sqs_, rns_ = T(512), T(512, F32)
                psR3 = v3(psR, 4)
                psO3 = v3(psO, 4)

                def pre_cols(n, B, F):
                    ts_ = slice(n * 128, (n + 1) * 128)
                    c_, r_c = B["col"]
                    E_, EG_ = F["E"], F["EG"]
                    pba, r_pba = bank()
                    for kc in range(8):
                        mm(pba[:, 0:16], hT3[:, kc, ts_], Wba3[:, kc, :], kc == 0, kc == 7, [r_hT, r_Wba], [r_pba])
                    beta4 = c_[:, 0:4]
                    g4 = c_[:, 4:8]
                    sigmoid_act(beta4, pba[:, g * 4:g * 4 + 4], [r_pba], r_c)
                    op("dve", "tensor_tensor", c_[:, 8:12], pba[:, 8 + g * 4:12 + g * 4], dtb[:, g * 4:g * 4 + 4], op=ALU.add,
                       r=[r_pba, r_sv, r_c], w=[r_c])
                    yield
                    op("act", "activation", c_[:, 8:12], c_[:, 8:12], AF.Exp, r=[r_c], w=[r_c])
                    op("act", "activation", c_[:, 8:12], c_[:, 8:12], AF.Ln, bias=one_col, scale=1.0, r=[r_c, r_cst], w=[r_c])
                    op("dve", "tensor_tensor", g4, c_[:, 8:12], nA[:, g * 4:g * 4 + 4], op=ALU.mult, r=[r_c, r_sv], w=[r_c])
                    yield
                    pcl, r_pcl = bank()
                    mm(pcl[:, 0:4], MUI_f, g4, True, True, [r_cst, r_c], [r_pcl])
                    mm(pcl[:, 4:8], SL_f, g4, True, True, [r_cst, r_c], [r_pcl])
                    mm(pcl[:, 8:12], CI0_f, g4, True, True, [r_cst, r_c], [r_pcl])
                    mm(pcl[:, 12:16], CI1_f, g4, True, True, [r_cst, r_c], [r_pcl])
                    op("act", "activation", c_[:, 16:32], pcl[:, 0:16], AF.Exp, r=[r_pcl, r_c], w=[r_c])
                    Lg3, Gb3 = v3(Lg_[0], 4), v3(Gb_[0], 4)
                    op("dve", "tensor_tensor", Lg3, bc(SL_f, 1, [128, 4, 128]), bc(g4, 2, [128, 4, 128]), op=ALU.mult,
                       r=[r_cst, r_c], w=[Lg_[1]])
                    op("act", "activation", Gb3, bc(g4, 2, [128, 4, 128]), AF.Copy, r=[r_c], w=[Gb_[1]])
                    yield
                    pD, r_pD = bank()
                    pD3 = v3(pD, 4)
                    for hl in range(4):
                        mm(pD3[:, hl, :], Lg3[:, hl, :], MUI_f, True, True, [Lg_[1], r_cst], [r_pD])
                    op("act", "activation", E_[0], pD, AF.Exp, r=[r_pD], w=[E_[1]])
                    pGC, r_pGC = bank()
                    pGC3 = v3(pGC, 4)
                    for hl in range(4):
                        mm(pGC3[:, hl, :], Gb3[:, hl, :], MUI_f, True, True, [Gb_[1], r_cst], [r_pGC])
                    op("act", "activation", EG_[0], pGC, AF.Exp, r=[r_pGC], w=[EG_[1]])

                def preA(n):
                    par = n % 2
                    ts_ = slice(n * 128, (n + 1) * 128)
                    B = PB[n % 3]
                    F = FB[n % 2]
                    rb_, kd_ = B["rb"], B["kd"]
                    qT_, kT_, kt_ = F["qT"], F["kT"], F["kt"]
                    cols = pre_cols(n, B, F)
                    prs = []
                    for s in range(3):
                        pr_, r_pr = bank()
                        pr3 = v3(pr_, 4)
                        for hl in range(4):
                            for kc in range(8):
                                mm(pr3[:, hl, :], Wd4[:, kc, s, hl * 128:(hl + 1) * 128], hT3[:, kc, ts_], kc == 0, kc == 7,
                                   [r_Wd, r_hT], [r_pr])
                        prs.append((pr3, r_pr))
                    rws = []
                    for s in range(3):
                        rw, r_rw = rawb[par][s]
                        rw3 = v3(rw, 4)
                        if n == 0:
                            op("pool", "memset", rw3[:, :, 0:3], 0.0, w=[r_rw])
                        else:
                            pw, r_pw = rawb[1 - par][s]
                            op("pool", "tensor_copy", rw3[:, :, 0:3], v3(pw, 4)[:, :, 128:131], r=[r_pw], w=[r_rw])
                        if s == 1:
                            op("dve", "tensor_copy", rw3[:, :, 3:131], prs[s][0], r=[prs[s][1]], w=[r_rw])
                        else:
                            op("act", "copy", rw3[:, :, 3:131], prs[s][0], r=[prs[s][1]], w=[r_rw])
                        rws.append((rw3, r_rw))
                    next(cols, None)
                    yield
                    pcs = []
                    for s in range(3):
                        pc_, r_pc = bank()
                        pc3 = v3(pc_, 4)
                        rw3, r_rw = rws[s]
                        for hl in range(4):
                            for j in range(4):
                                mm(pc3[:, hl, :], dgd4[:, s * 4 + hl, j, :], rw3[:, hl, j:j + 128], j == 0, j == 3,
                                   [r_dgd, r_rw], [r_pc])
                        pcs.append((pc_, r_pc))
                    sg_tmp = (qs_, ks_, rnq)
                    for s in range(3):
                        sigmoid_act(sg_tmp[s][0], pcs[s][0], [pcs[s][1]], sg_tmp[s][1])
                    for s in range(3):
                        dst = (qs_, ks_, vsT_)[s]
                        op("dve", "tensor_tensor", dst[0], pcs[s][0], sg_tmp[s][0], op=ALU.mult,
                           r=[pcs[s][1], sg_tmp[s][1]], w=[dst[1]])
                    next(cols, None)
                    yield
                    pns = []
                    for (src_, sq_) in ((qs_, sqq), (ks_, sqk)):
                        op("pool", "tensor_tensor", sq_[0], src_[0], src_[0], op=ALU.mult, r=[src_[1]], w=[sq_[1]])
                    for sq_ in (sqq, sqk):
                        pn, r_pn = bank()
                        mm(pn, ones_b, sq_[0], True, True, [r_cb, sq_[1]], [r_pn])
                        pns.append((pn, r_pn))
                    for (pn, r_pn), rn_ in zip(pns, (rnq, rnk)):
                        rsqrt_act(rn_[0], pn, 1.0, eps6, [r_pn], rn_[1])
                    for (src_, dstT, scl, rn_) in ((qs_, qT_, 128.0 ** -0.5, rnq), (ks_, kT_, 1.0, rnk)):
                        op("dve", "scalar_tensor_tensor", dstT[0], src_[0], scl, rn_[0], op0=ALU.mult, op1=ALU.mult,
                           r=[src_[1], rn_[1]], w=[dstT[1]])
                    next(cols, None)
                    yield
                    qT3, kT3, vsT3 = v3(qT_[0], 4), v3(kT_[0], 4), v3(vsT_[0], 4)
                    rb3 = v3(rb_[0], 4)
                    kt3, kd3 = v3(kt_[0], 4), v3(kd_[0], 4)
                    pt, r_pt = bank()
                    ptb = v3(pt.bitcast(BF16)[:, 0:512], 4)
                    for hl in range(4):
                        tr(ptb[:, hl, :], kT3[:, hl, :], ident_b, [kT_[1], r_cb], [r_pt])
                    pt2, r_pt2 = bank()
                    ptb2 = v3(pt2.bitcast(BF16)[:, 0:512], 4)
                    for hl in range(4):
                        tr(ptb2[:, hl, :], vsT3[:, hl, :], ident_b, [vsT_[1], r_cb], [r_pt2])
                    op("act", "copy", kt3, ptb, r=[r_pt], w=[kt_[1]])
                    op("dve", "tensor_copy", rb3[:, :, 0:128], ptb2, r=[r_pt2], w=[rb_[1]])
                    for _ in cols:
                        yield

                def preB(n):
                    B = PB[n % 3]
                    F = FB[n % 2]
                    rb_, wT_, qg_, it_, kd_ = B["rb"], B["wT"], B["qg"], B["it"], B["kd"]
                    c_, r_c = B["col"]
                    beta4, egc4, edec4 = c_[:, 0:4], c_[:, 16:20], c_[:, 20:24]
                    qT_, kT_, kt_, E_, EG_ = F["qT"], F["kT"], F["kt"], F["E"], F["EG"]
                    qT3, kT3 = v3(qT_[0], 4), v3(kT_[0], 4)
                    rb3 = v3(rb_[0], 4)
                    kt3, kd3 = v3(kt_[0], 4), v3(kd_[0], 4)
                    op("dve", "tensor_tensor", qg_[0], qT_[0], EG_[0], op=ALU.mult, r=[qT_[1], EG_[1]], w=[qg_[1]])
                    pG, r_pG = bank()
                    pG3 = v3(pG, 4)
                    for hl in range(4):
                        mm(pG3[:, hl, :], kT3[:, hl, :], kT3[:, hl, :], True, True, [kT_[1]], [r_pG])
                    pQ, r_pQ = bank()
                    pQ3 = v3(pQ, 4)
                    for hl in range(4):
                        mm(pQ3[:, hl, :], kT3[:, hl, :], qT3[:, hl, :], True, True, [kT_[1], qT_[1]], [r_pQ])
                    op("dve", "tensor_tensor", Mp_[0], pG, E_[0], op=ALU.mult, r=[r_pG, E_[1]], w=[Mp_[1]])
                    op("dve", "tensor_tensor", IT_[0], pQ, E_[0], op=ALU.mult, r=[r_pQ, E_[1]], w=[IT_[1]])
                    Mp3, IT3, it3 = v3(Mp_[0], 4), v3(IT_[0], 4), v3(it_[0], 4)
                    M3 = [v3(m_[0], 4) for m_ in Ms]
                    A3 = [v3(a_[0], 4) for a_ in As]
                    op("dve", "tensor_tensor", Mp3, Mp3, bc(beta4, 2, [128, 4, 128]), op=ALU.mult, r=[Mp_[1], r_c], w=[Mp_[1]])
                    op("dve", "tensor_tensor", M3[0], Mp3, bc(MU_f, 1, [128, 4, 128]), op=ALU.mult,
                       r=[Mp_[1], r_cst], w=[Ms[0][1]])
                    op("pool", "tensor_tensor", it3, IT3, bc(MUI_f, 1, [128, 4, 128]), op=ALU.mult,
                       r=[IT_[1], r_cst], w=[it_[1]])
                    op("dve", "tensor_tensor", rb3[:, :, 128:256], kt3, bc(egc4, 2, [128, 4, 128]), op=ALU.mult,
                       r=[kt_[1], r_c], w=[rb_[1]])
                    yield

                    def chain(k):
                        for hl in range(4):
                            mm(psR3[:, hl, :], M3[k][:, hl, :], rb3[:, hl, :], True, True, [Ms[k][1], rb_[1]], [r_psR])
                        op("dve", "tensor_tensor", rb_[0], rb_[0], psR, op=(ALU.subtract if k == 0 else ALU.add),
                           r=[r_psR, rb_[1]], w=[rb_[1]])
                    pt, r_pt = bank()
                    ptb = v3(pt.bitcast(BF16)[:, 0:512], 4)
                    for hl in range(4):
                        tr(ptb[:, hl, :], M3[0][:, hl, :], ident_b, [Ms[0][1], r_cb], [r_pt])
                    op("act", "copy", A3[0], ptb, r=[r_pt], w=[As[0][1]])
                    yield
                    for k in range(1, 6):
                        pM, r_pM = bank()
                        pM3 = v3(pM, 4)
                        for hl in range(4):
                            mm(pM3[:, hl, :], A3[k - 1][:, hl, :], M3[k - 1][:, hl, :], True, True,
                               [As[k - 1][1], Ms[k - 1][1]], [r_pM])
                        if k < 5:
                            pA, r_pA = bank()
                            pA3 = v3(pA, 4)
                            for hl in range(4):
                                mm(pA3[:, hl, :], M3[k - 1][:, hl, :], A3[k - 1][:, hl, :], True, True,
                                   [As[k - 1][1], Ms[k - 1][1]], [r_pA])
                        op("act", "copy", Ms[k][0], pM, r=[r_pM], w=[Ms[k][1]])
                        if k < 5:
                            op("dve", "tensor_copy", As[k][0], pA, r=[r_pA], w=[As[k][1]])
                        chain(k - 1)
                        yield
                    chain(5)
                    op("pool", "tensor_tensor", kd3, kt3, bc(edec4, 2, [128, 4, 128]), op=ALU.mult,
                       r=[kt_[1], r_c], w=[kd_[1]])
                    op("dve", "tensor_tensor", rb3, rb3, bc(beta4, 2, [128, 4, 256]), op=ALU.mult,
                       r=[r_c, rb_[1]], w=[rb_[1]])
                    yield
                    pt, r_pt = bank()
                    ptb = v3(pt.bitcast(BF16)[:, 0:512], 4)
                    for hl in range(4):
                        tr(ptb[:, hl, :], rb3[:, hl, 128:256], ident_b, [rb_[1], r_cb], [r_pt])
                    op("act", "copy", v3(wT_[0], 4), ptb, r=[r_pt], w=[wT_[1]])

                def scan(n):
                    ts_ = slice(n * 128, (n + 1) * 128)
                    B = PB[n % 3]
                    rb_, wT_, qg_, it_, kd_ = B["rb"], B["wT"], B["qg"], B["it"], B["kd"]
                    c_, r_c = B["col"]
                    egl = [c_[:, 24:28], c_[:, 28:32]]
                    rb3, wT3, qg3, it3, kd3 = v3(rb_[0], 4), v3(wT_[0], 4), v3(qg_[0], 4), v3(it_[0], 4), v3(kd_[0], 4)
                    vn3 = v3(vn_[0], 4)
                    pz, r_pz = bank()
                    pz3 = v3(pz, 4)
                    for hl in range(4):
                        for kc in range(8):
                            mm(pz3[:, hl, :], Wz3[:, kc, hl * 128:(hl + 1) * 128], hT3[:, kc, ts_], kc == 0, kc == 7,
                               [r_Wz, r_hT], [r_pz])
                    sigmoid_act(sz_s[0], pz, [r_pz], sz_s[1])
                    op("dve", "tensor_tensor", sz_s[0], pz, sz_s[0], op=ALU.mult, r=[r_pz, sz_s[1]], w=[sz_s[1]])
                    yield
                    for c in range(2):
                        rows = slice(c * 64, (c + 1) * 64)
                        pW, r_pW = bank()
                        pW3 = v3(pW, 4)
                        for hl in range(4):
                            mm(pW3[rows, hl, :], wT3[:, hl, rows], Sb3[:, hl, :], True, True, [wT_[1], r_Sb], [r_pW])
                        op("dve", "tensor_tensor", vn3[rows, :, :], rb3[rows, :, 0:128], pW3[rows, :, :], op=ALU.subtract,
                           r=[rb_[1], r_pW], w=[vn_[1]])
                        yield
                        pS, r_pS = bank()
                        pS3 = v3(pS, 4)
                        for hl in range(4):
                            mm(pS3[:, hl, :], kd3[rows, hl, :], vn3[rows, hl, :], True, True, [kd_[1], vn_[1]], [r_pS])
                        for hl in range(4):
                            mm(psO3[:, hl, rows], Sb3[:, hl, :], qg3[:, hl, rows], True, False, [r_Sb, qg_[1]], [r_psO])
                            mm(psO3[:, hl, rows], vn3[rows, hl, :], it3[rows, hl, rows], False, True, [vn_[1], it_[1]], [r_psO])
                        op("dve", "tensor_tensor", S3, S3, bc(egl[c], 2, [128, 4, 128]), op=ALU.mult, r=[r_S, r_c], w=[r_S])
                        op("dve", "tensor_tensor", S3, S3, pS3, op=ALU.add, r=[r_S, r_pS], w=[r_S])
                        op("act", "copy", Sb, S, r=[r_S], w=[r_Sb])
                        yield
                    op("act", "activation", sqs_[0], psO, AF.Square, r=[r_psO], w=[sqs_[1]])
                    pn, r_pn = bank()
                    mm(pn, ones_b, sqs_[0], True, True, [r_cb, sqs_[1]], [r_pn])
                    rsqrt_act(rns_[0], pn, 1.0 / 128, eps6, [r_pn], rns_[1])
                    yield
                    op("dve", "tensor_tensor", on_s[0], psO, rns_[0], op=ALU.mult, r=[r_psO, rns_[1]], w=[on_s[1]])
                    op("dve", "scalar_tensor_tensor", do3[:, g * 4:(g + 1) * 4, ts_], v3(on_s[0], 4), dnw, v3(sz_s[0], 4),
                       op0=ALU.mult, op1=ALU.mult, r=[on_s[1], r_sv, sz_s[1]], w=[r_do])

                run_interleaved(preA(0))
                run_interleaved(preB(0), preA(1))
                for n in range(NT):
                    run_interleaved(preB(n + 1) if n + 1 < NT else None, preA(n + 2) if n + 2 < NT else None, scan(n))
                P.barrier()
            A.p = dmark
            finish(do3, r_do, w_dn_out, l, OFF["gd"], first)
            A.p = pmark

        def phase_end(l, last):
            set_rot(list(range(8)))
            Wo = A.bf(8 * D)
            Wo3 = v3(Wo, 8)
            r_Wo = Res()
            for kc in range(8):
                ldw(Wo3[:, kc:kc + 1, :], wsq_cols(w_out, l, 0, D)[:, kc:kc + 1, :], [r_Wo])
            wbc = A.f32(D)
            r_wbc = Res()
            src = final_norm_w if last else norm_w[l + 1]
            ld(wbc, src.partition_broadcast(128), [r_wbc], slot=s_misc)
            xts = [(A.f32(D), Res()) for _ in range(2)]
            hns = [(A.bf(D), Res()) for _ in range(2)]
            outs = [(A.f32(D), Res()) for _ in range(2)]
            junk = A.bf(D)
            r_junk = Res()
            xsrc = x_in if l == 0 else xs
            r_xs = Res()
            r_y = Res()
            for t in range(NT):
                ts_ = slice(t * 128, (t + 1) * 128)
                xt, r_xt = xts[t % 2]
                ld(xt, xsrc[ts_, :], [r_xt], r=[r_xs])
                for hf in range(2):
                    pp, r_pp = bank()
                    for kc in range(8):
                        mm(pp, mg3[:, kc, ts_], Wo3[:, kc, hf * 512:(hf + 1) * 512], kc == 0, kc == 7, [r_mg, r_Wo], [r_pp])
                    op("dve", "tensor_tensor", xt[:, hf * 512:(hf + 1) * 512], xt[:, hf * 512:(hf + 1) * 512], pp, op=ALU.add,
                       r=[r_pp, r_xt], w=[r_xt])
                if not last:
                    P.dma("sp", s_st, (lambda o_, i_: (lambda e: e.dma_start(out=o_, in_=i_)))(xs[ts_, :], xt), [r_xt], [r_xs])
                    hn, r_hn = hns[t % 2]
                    norm_to_hT(xt, r_xt, wbc, r_wbc, t, hn, r_hn, junk, r_junk, 2 * (t % 2))
                    hn_to_hT(hn, r_hn, t)
                else:
                    ot, r_ot = outs[t % 2]
                    norm_to_hT(xt, r_xt, wbc, r_wbc, t, ot, r_ot, junk, r_junk, 2 * (t % 2))
                    P.dma("sp", s_st, (lambda o_, i_: (lambda e: e.dma_start(out=o_, in_=i_)))(y_out[ts_, :], ot), [r_ot], [r_y])
            P.barrier()
            A.p = pmark
            return r_y

        def dump_mg():
            tmp = A.f32(L)
            r_tmp = Res()
            r_d = Res()
            for c in range(8):
                op("dve", "tensor_copy", tmp, mg3[:, c, :], r=[r_mg], w=[r_tmp])
                P.dma("sp", s_st, (lambda o_, i_: (lambda e: e.dma_start(out=o_, in_=i_)))(dbg_out[:, c, :], tmp), [r_tmp], [r_d])
            P.wait_for("sp", [r_d])
            P.barrier()

        phase0(0)
        r_y = None
        stop = False
        for l in range(depth):
            first = True
            for b in branches:
                {"D": branch_D, "A": branch_A, "C": branch_C}[b](l, first)
                first = False
            if dbg == "mg" and l == depth - 1:
                dump_mg()
                stop = True
                break
            r_y = phase_end(l, l == DEPTH - 1)
        if dbg == "hT":
            tmp = A.f32(L)
            r_tmp, r_d = Res(), Res()
            for c in range(8):
                op("dve", "tensor_copy", tmp, hT3[:, c, :], r=[r_hT], w=[r_tmp])
                P.dma("sp", s_st, (lambda o_, i_: (lambda e: e.dma_start(out=o_, in_=i_)))(dbg_out[:, c, :], tmp), [r_tmp], [r_d])
            P.wait_for("sp", [r_d])
        if r_y is not None:
            P.wait_for("sp", [r_y])
        P.barrier()
        P.emit(st)
    return nc


def make_consts():
    c = np.zeros((128, NCST), np.float32)
    i = np.arange(128)
    c[:, 0:128] = np.eye(128)
    sw = np.where((i % 64) < 32, i + 32, i - 32)
    c[sw, 128 + i] = 1.0
    c[:, 256:384] = 1.0
    same = (i[:, None] // 64) == (i[None, :] // 64)
    c[:, 384:512] = ((i[None, :] > i[:, None]) & same)
    c[:, 512:640] = ((i[None, :] >= i[:, None]) & same)
    c[:, 640:768] = ((i[:, None] > i[None, :]) & same)
    c[:, 768:896] = (i[None, :] >= i[:, None])
    c[:, 896:1024] = (i[:, None] < 64)
    c[:, 1024:1152] = (i[:, None] >= 64)
    invf = (10000.0 ** (-np.arange(0, 64, 2, dtype=np.float32) / 64)).astype(np.float32)
    c[:, 1152] = np.tile(invf, 4)
    c[:, 1153] = np.where((i % 64) < 32, -1.0, 1.0)
    c[:, 1154] = 1e-6
    c[:, 1155] = 1e-5
    return c


_CACHE = {}


def _prep_common(inputs):
    f = lambda k: np.ascontiguousarray(np.asarray(inputs[k], dtype=np.float32))
    com = {k: f(k) for k in ("norm_w", "w_in", "attn_subln_w", "w_attn_out", "conv_dw_w", "w_conv_out", "dn_conv_w",
                             "dn_a_log", "dn_dt_bias", "dn_norm_w", "w_dn_out", "w_out", "final_norm_w")}
    com["lam_qk"] = f("lam_qk").reshape(DEPTH, 256)
    cv = np.stack([f("conv_dw_b").reshape(DEPTH, 8, 128), f("conv_ln_w").reshape(DEPTH, 8, 128),
                   f("conv_ln_b").reshape(DEPTH, 8, 128)], axis=1).reshape(DEPTH, 24, 128)
    com["conv_vecs"] = np.ascontiguousarray(cv)
    com["cst"] = make_consts()
    return com


def kernel(**inputs):
    x = np.asarray(inputs["x"], dtype=np.float32)
    pos = np.asarray(inputs["positions"], dtype=np.int32)
    B = x.shape[0]
    com = _prep_common(inputs)
    if "nc" not in _CACHE:
        _CACHE["nc"] = build()
    nc = _CACHE["nc"]
    in_maps = []
    for b in range(B):
        m = dict(com)
        m["x"] = np.ascontiguousarray(x[b])
        m["positions"] = np.ascontiguousarray(pos[b])
        in_maps.append(m)
    res = run_bass_kernel_spmd(nc, in_maps, core_ids=list(range(B)))
    return np.stack([np.asarray(r["y"], dtype=np.float32) for r in res.results], axis=0)
```
